# Optimizing a Trainium2 kernel written in Bass

```python
import math
import jax, jax.numpy as jnp
from jax import lax
import numpy as np

D_MODEL = 2048
BATCH = 8
SEQ = 2048
DEPTH = 1

HEAD_DIM = 128
N_HEADS = D_MODEL // HEAD_DIM
N_HEADS_SB = N_HEADS // 2
N_HEADS_DIL = N_HEADS - N_HEADS_SB
D_SB = N_HEADS_SB * HEAD_DIM
D_DIL = N_HEADS_DIL * HEAD_DIM
QKV_WIDTH = 3 * (D_SB + D_DIL)
DILATED_BRANCHES = ((128, 1), (512, 4), (2048, 16))
QUERY_BLOCK = 128
D_FF = 5504
CONV_WIDTH = 3
ROPE_THETA = 10000.0
RMS_EPS = 1e-6

kernel_name = 'hybrid_stickbreaking_dilated_convffn_layer'


def rmsnorm(x, gain):
    xf = x.astype(jnp.float32)
    y = xf * lax.rsqrt(jnp.mean(xf * xf, axis=-1, keepdims=True) + RMS_EPS)
    return (y * gain.astype(jnp.float32)).astype(x.dtype)


def head_rmsnorm(o, gain):
    H, Dh = o.shape[1], o.shape[3]
    of = o.astype(jnp.float32)
    y = of * lax.rsqrt(jnp.mean(of * of, axis=-1, keepdims=True) + RMS_EPS)
    return (y * gain.astype(jnp.float32).reshape(1, H, 1, Dh)).astype(o.dtype)


def apply_rope(x):
    S, Dh = x.shape[2], x.shape[3]
    inv_freq = ROPE_THETA ** (-jnp.arange(0, Dh, 2, dtype=jnp.float32) / Dh)
    ang = jnp.arange(S, dtype=jnp.float32)[:, None] * inv_freq[None, :]
    cos, sin = jnp.cos(ang), jnp.sin(ang)
    x1, x2 = jnp.split(x.astype(jnp.float32), 2, axis=-1)
    out = jnp.concatenate([x1 * cos - x2 * sin, x2 * cos + x1 * sin], axis=-1)
    return out.astype(x.dtype)


def to_heads(t, n_heads):
    B, S, _ = t.shape
    return t.reshape(B, S, n_heads, HEAD_DIM).transpose(0, 2, 1, 3)


def stick_breaking_attention(q, k, v):
    S, Dh = q.shape[2], q.shape[3]
    scale = Dh ** -0.5
    outs = []
    for blk in range(S // QUERY_BLOCK):
        q0 = blk * QUERY_BLOCK
        n_keys = q0 + QUERY_BLOCK
        q_blk = q[:, :, q0:n_keys]
        k_pre, v_pre = k[:, :, :n_keys], v[:, :, :n_keys]
        z = jnp.einsum('bhqd,bhkd->bhqk', q_blk, k_pre).astype(jnp.float32) * scale
        q_pos = q0 + jnp.arange(QUERY_BLOCK)
        k_pos = jnp.arange(n_keys)
        causal = k_pos[None, :] < q_pos[:, None]
        log_beta = jax.nn.log_sigmoid(z)
        log_keep = jnp.where(causal, jax.nn.log_sigmoid(-z), 0.0)
        log_remain = lax.cumsum(log_keep, axis=3, reverse=True) - log_keep
        a = jnp.where(causal, jnp.exp(log_beta + log_remain), 0.0)
        outs.append(jnp.einsum('bhqk,bhkd->bhqd', a.astype(v.dtype), v_pre))
    return jnp.concatenate(outs, axis=2)


def dilated_branch(q, k, v, window, dilation):
    B, H, S, Dh = q.shape
    n_back = window // dilation
    QB = QUERY_BLOCK
    L = S // dilation
    n_blocks = -(-L // QB)
    Lp = n_blocks * QB
    scale = Dh ** -0.5

    def to_sub(t):
        return t.reshape(B, H, L, dilation, Dh).transpose(0, 1, 3, 2, 4)

    qs = jnp.pad(to_sub(q), ((0, 0), (0, 0), (0, 0), (0, Lp - L), (0, 0)))
    pad_kv = ((0, 0), (0, 0), (0, 0), (QB, Lp - L), (0, 0))
    ks = jnp.pad(to_sub(k), pad_kv)
    vs = jnp.pad(to_sub(v), pad_kv)
    qb = qs.reshape(B, H, dilation, n_blocks, QB, Dh)

    def band(t):
        prev = t[:, :, :, :Lp].reshape(B, H, dilation, n_blocks, QB, Dh)
        cur = t[:, :, :, QB:QB + Lp].reshape(B, H, dilation, n_blocks, QB, Dh)
        return jnp.concatenate([prev, cur], axis=4)

    kb, vb = band(ks), band(vs)
    s = jnp.einsum('bhrnqd,bhrnkd->bhrnqk', qb, kb).astype(jnp.float32) * scale
    q_idx = jnp.arange(n_blocks)[:, None] * QB + jnp.arange(QB)[None, :]
    k_idx = jnp.arange(n_blocks)[:, None] * QB - QB + jnp.arange(2 * QB)[None, :]
    dist = q_idx[:, :, None] - k_idx[:, None, :]
    valid = (dist >= 0) & (dist <= n_back) & (k_idx[:, None, :] >= 0)
    s = jnp.where(valid, s, -jnp.inf)
    m = jnp.max(s, axis=-1, keepdims=True)
    p = jnp.exp(s - m)
    den = jnp.sum(p, axis=-1, keepdims=True)
    out = jnp.einsum('bhrnqk,bhrnkd->bhrnqd', p.astype(v.dtype), vb).astype(jnp.float32) / den
    lse = (m + jnp.log(den))[..., 0]
    out = out.reshape(B, H, dilation, Lp, Dh)[:, :, :, :L]
    out = out.transpose(0, 1, 3, 2, 4).reshape(B, H, S, Dh)
    lse = lse.reshape(B, H, dilation, Lp)[:, :, :, :L].transpose(0, 1, 3, 2).reshape(B, H, S)
    return out, lse


def dilated_attention(q, k, v):
    outs, lses = [], []
    for window, dilation in DILATED_BRANCHES:
        o, l = dilated_branch(q, k, v, window, dilation)
        outs.append(o)
        lses.append(l)
    w = jax.nn.softmax(jnp.stack(lses, axis=0), axis=0)
    out = jnp.sum(w[..., None] * jnp.stack(outs, axis=0), axis=0)
    return out.astype(q.dtype)


def conv_geglu_ffn(h, w_up, conv_w, conv_b, w_down):
    S = h.shape[1]
    u = jnp.einsum('bsd,df->bsf', h, w_up)
    up = jnp.pad(u, ((0, 0), (CONV_WIDTH - 1, 0), (0, 0)))
    u = sum(up[:, j:j + S] * conv_w[j] for j in range(CONV_WIDTH)) + conv_b
    gate, val = jnp.split(u, 2, axis=-1)
    y = jax.nn.gelu(gate, approximate=True) * val
    return jnp.einsum('bsf,fd->bsd', y, w_down)


def setup_inputs(seed: int = 0) -> dict:
    key = jax.random.key(seed)
    ks = jax.random.split(key, 16)
    f32 = jnp.float32

    def gain(k, n):
        return 1.0 + 0.05 * jax.random.normal(k, (DEPTH, n), f32)

    return {
        'x': jax.random.normal(ks[0], (BATCH, SEQ, D_MODEL), f32),
        'pre_mix_gain': gain(ks[1], D_MODEL),
        'post_mix_gain': gain(ks[2], D_MODEL),
        'pre_ffn_gain': gain(ks[3], D_MODEL),
        'post_ffn_gain': gain(ks[4], D_MODEL),
        'w_in': jax.random.normal(ks[5], (DEPTH, D_MODEL, QKV_WIDTH), f32) * D_MODEL ** -0.5,
        'sb_out_gain': gain(ks[6], D_SB),
        'dil_out_gain': gain(ks[7], D_DIL),
        'w_out': jax.random.normal(ks[8], (DEPTH, D_SB + D_DIL, D_MODEL), f32) * (D_SB + D_DIL) ** -0.5,
        'w_up': jax.random.normal(ks[9], (DEPTH, D_MODEL, 2 * D_FF), f32) * D_MODEL ** -0.5,
        'conv_w': jax.random.normal(ks[10], (DEPTH, CONV_WIDTH, 2 * D_FF), f32) * CONV_WIDTH ** -0.5,
        'conv_b': 0.02 * jax.random.normal(ks[11], (DEPTH, 2 * D_FF), f32),
        'w_down': jax.random.normal(ks[12], (DEPTH, D_FF, D_MODEL), f32) * D_FF ** -0.5,
    }


def reference(x, pre_mix_gain, post_mix_gain, pre_ffn_gain, post_ffn_gain, w_in,
              sb_out_gain, dil_out_gain, w_out, w_up, conv_w, conv_b, w_down):
    splits = [D_SB, 2 * D_SB, 3 * D_SB, 3 * D_SB + D_DIL, 3 * D_SB + 2 * D_DIL]
    for layer in range(DEPTH):
        h = rmsnorm(x, pre_mix_gain[layer])
        proj = jnp.einsum('bsd,de->bse', h, w_in[layer])
        q_sb, k_sb, v_sb, q_dl, k_dl, v_dl = jnp.split(proj, splits, axis=-1)
        o_sb = stick_breaking_attention(to_heads(q_sb, N_HEADS_SB), to_heads(k_sb, N_HEADS_SB),
                                        to_heads(v_sb, N_HEADS_SB))
        o_dl = dilated_attention(apply_rope(to_heads(q_dl, N_HEADS_DIL)),
                                 apply_rope(to_heads(k_dl, N_HEADS_DIL)),
                                 to_heads(v_dl, N_HEADS_DIL))
        o_sb = head_rmsnorm(o_sb, sb_out_gain[layer])
        o_dl = head_rmsnorm(o_dl, dil_out_gain[layer])
        B, _, S, _ = o_sb.shape
        mixed = jnp.concatenate([o_sb.transpose(0, 2, 1, 3).reshape(B, S, D_SB),
                                 o_dl.transpose(0, 2, 1, 3).reshape(B, S, D_DIL)], axis=-1)
        mix_out = jnp.einsum('bse,ed->bsd', mixed, w_out[layer])
        x = x + rmsnorm(mix_out, post_mix_gain[layer])
        h = rmsnorm(x, pre_ffn_gain[layer])
        f = conv_geglu_ffn(h, w_up[layer], conv_w[layer], conv_b[layer], w_down[layer])
        x = x + rmsnorm(f, post_ffn_gain[layer])
    return x
```

```python
import numpy as np
import os
LVL = int(os.environ.get('ATT_LVL', '9'))
SUB = int(os.environ.get('ATT_SUB', '9'))
from contextlib import ExitStack
import concourse.bass as bass
import concourse.mybir as mybir
from concourse.bass_utils import run_bass_kernel_spmd

F32 = mybir.dt.float32
BF = mybir.dt.bfloat16
AF = mybir.ActivationFunctionType
ALU = mybir.AluOpType

S = 2048
D = 2048
NH = 16
DH = 128
DFF = 5504
NFC = DFF // 128
QKV = 6144
EPS = 1e-6
SCALE = DH ** -0.5
NCORES = 8
LAST_COUNTS = {}


class Buf:
    __slots__ = ("name", "w", "r")

    def __init__(self, name):
        self.name = name
        self.w = None
        self.r = []


class Slot:
    def __init__(self, sem, key):
        self.sem = sem
        self.key = key
        self.count = 0


class Sched:
    ENGS = ("pe", "act", "dve", "pool", "sp")

    def __init__(self, nc, es):
        self.nc = nc
        self.q = {e: [] for e in self.ENGS}
        self.cnt = {e: 0 for e in self.ENGS}
        self.seen = {e: {} for e in self.ENGS}
        self.sems = {}
        for e in self.ENGS:
            self.sems[e] = es.enter_context(nc.semaphore("sem_" + e))
        self.es = es
        self.nslots = 0

    def slot(self):
        key = "dma%d" % self.nslots
        self.nslots += 1
        sem = self.es.enter_context(self.nc.semaphore("sem_" + key))
        self.sems[key] = sem
        return Slot(sem, key)

    def _deps(self, eng, reads, writes):
        deps = []
        for b in reads:
            if b.w is not None:
                deps.append(b.w)
            if b.name.startswith("zb") or b.name.startswith("ob") or b.name.startswith("wb") or b.name.startswith("sbk") \
                    or b.name.startswith("pp") or b.name.startswith("ptr") or b.name.startswith("PS"):
                deps.extend(t for t in b.r if t[0] != eng)
        for b in writes:
            if b.w is not None:
                deps.append(b.w)
            deps.extend(b.r)
        out = {}
        for (k, v) in deps:
            if k == "pe" and eng == "pe":
                continue
            if v > out.get(k, 0):
                out[k] = v
        res = []
        for k, v in out.items():
            if v > self.seen[eng].get(k, 0):
                self.seen[eng][k] = v
                res.append((k, v))
        return res

    def _emit_waits(self, eng, waits):
        for (k, v) in waits:
            sem = self.sems[k]
            self.q[eng].append(lambda e, sem=sem, v=v: e.wait_ge(sem, v))

    def _mark(self, ticket, reads, writes):
        for b in writes:
            b.w = ticket
            b.r = []
        for b in reads:
            b.r.append(ticket)

    def op(self, eng, fn, reads=(), writes=()):
        waits = self._deps(eng, reads, writes)
        self._emit_waits(eng, waits)
        self.cnt[eng] += 1
        sem = self.sems[eng]
        self.q[eng].append(lambda e, fn=fn, sem=sem: fn(e).then_inc(sem, 1))
        t = (eng, self.cnt[eng])
        self._mark(t, reads, writes)
        return t

    def group(self, fns, reads=(), writes=()):
        eng = "pe"
        waits = self._deps(eng, reads, writes)
        self._emit_waits(eng, waits)
        self.cnt[eng] += 1
        sem = self.sems[eng]
        n = len(fns)
        for i, fn in enumerate(fns):
            if i == n - 1:
                self.q[eng].append(lambda e, fn=fn, sem=sem: fn(e).then_inc(sem, 1))
            else:
                self.q[eng].append(lambda e, fn=fn: fn(e))
        t = (eng, self.cnt[eng])
        self._mark(t, reads, writes)
        return t

    def dma(self, eng, fn, slot, reads=(), writes=(), n=1):
        waits = self._deps(eng, reads, writes)
        self._emit_waits(eng, waits)
        slot.count += 16 * n
        sem = slot.sem
        self.q[eng].append(lambda e, fn=fn, sem=sem: fn(e, lambda ins: ins.then_inc(sem, 16)))
        t = (slot.key, slot.count)
        self._mark(t, reads, writes)
        return t

    def wait_all(self, eng, tickets):
        waits = []
        for (k, v) in tickets:
            if v > self.seen[eng].get(k, 0):
                self.seen[eng][k] = v
                waits.append((k, v))
        self._emit_waits(eng, waits)

    def barrier(self, extra=()):
        tickets = [(e, self.cnt[e]) for e in self.ENGS if self.cnt[e] > 0] + list(extra)
        for eng in self.ENGS:
            self.wait_all(eng, [t for t in tickets if not (t[0] == eng and eng == "pe")])

    def flush(self):
        nc = self.nc
        q = self.q
        with nc.Block() as block:
            @block.tensor
            def _(e):
                for f in q["pe"]:
                    f(e)

            @block.scalar
            def _(e):
                for f in q["act"]:
                    f(e)

            @block.vector
            def _(e):
                for f in q["dve"]:
                    f(e)

            @block.gpsimd
            def _(e):
                for f in q["pool"]:
                    f(e)

            @block.sync
            def _(e):
                for f in q["sp"]:
                    f(e)
        self.q = {e: [] for e in self.ENGS}


def _consts():
    f32 = np.float32
    kl = np.arange(128)[:, None]
    x = np.arange(19 * 128)[None, :]
    dl = x - 384 - kl
    c = ((dl >= 0) & (dl <= 128)).astype(f32)
    c += ((dl >= 0) & (dl % 4 == 0) & (dl <= 512)).astype(f32)
    c += ((dl >= 0) & (dl % 16 == 0) & (dl <= 2048)).astype(f32)
    x7 = np.arange(7 * 128)[None, :]
    sbm = ((x7 - 384 - kl) > 0).astype(f32)
    j = np.arange(128)[:, None]
    s = np.arange(128)[None, :]
    tge = (j >= s).astype(f32)
    ident = np.eye(128, dtype=f32)
    ones = np.ones((128, 128), f32)
    inv_freq = (np.float32(10000.0) ** (-np.arange(0, 128, 2, dtype=f32) / np.float32(128))).astype(f32)
    ang = (np.arange(S, dtype=f32)[:, None] * inv_freq[None, :]).astype(f32)
    cos = np.cos(ang).astype(f32).T
    sin = np.sin(ang).astype(f32).T
    cosT = np.concatenate([cos, cos], axis=0)
    sinS = np.concatenate([-sin, sin], axis=0)
    return dict(c_dlm=c, c_sbm=sbm, c_tge=tge, c_ident=ident, c_ones=ones,
                c_cos=np.ascontiguousarray(cosT), c_sin=np.ascontiguousarray(sinS))


def build(dbg=None, stop_after=None):
    nc = bass.Bass("TRN2", target_bir_lowering=False)
    es0 = ExitStack()

    def din(name, shape, dt=F32):
        return nc.dram_tensor(name, list(shape), dt, kind="ExternalInput").ap()

    x_d = din("x", [S, D])
    w_in_d = din("w_in", [D, QKV])
    w_out_d = din("w_out", [D, D])
    w_up_d = din("w_up", [D, 2 * DFF])
    w_down_d = din("w_down", [DFF, D])
    gpre_d = din("g_pre_mix", [128, D])
    gpost_d = din("g_post_mix", [128, 16])
    gpre2_d = din("g_pre_ffn", [128, 16])
    gpost2_d = din("g_post_ffn", [128, 16])
    og_d = din("g_heads", [128, 16])
    cw_d = din("conv_w", [128, 3, 86])
    cb_d = din("conv_b", [128, 86])
    c_dlm_d = din("c_dlm", [128, 19 * 128])
    c_sbm_d = din("c_sbm", [128, 7 * 128])
    c_tge_d = din("c_tge", [128, 128])
    c_ident_d = din("c_ident", [128, 128])
    c_ones_d = din("c_ones", [128, 128])
    c_cos_d = din("c_cos", [128, S])
    c_sin_d = din("c_sin", [128, S])
    out_d = nc.dram_tensor("out", [S, D], F32, kind="ExternalOutput").ap()
    dbg_d = {}
    if dbg:
        for name, (shape, dt) in dbg.items():
            dbg_d[name] = nc.dram_tensor("dbg_" + name, list(shape), dt, kind="ExternalOutput").ap()

    def SB(es, name, shape, dt):
        return es.enter_context(nc.sbuf_tensor(name, list(shape), dt))

    def PS(es, name, shape, dt):
        return es.enter_context(nc.psum_tensor(name, list(shape), dt))

    sch = Sched(nc, es0)

    mixT = SB(es0, "mixT", [128, 16, S], BF)
    ident_bf = SB(es0, "ident_bf", [128, 128], BF)
    ident_f = SB(es0, "ident_f", [128, 128], F32)
    ones_bf = SB(es0, "ones_bf", [128, 128], BF)
    ones_f = SB(es0, "ones_f", [128, 128], F32)
    b_hT = [Buf("hT%d" % i) for i in range(16)]
    b_mix = [[Buf("mix%d_%d" % (h, q)) for q in range(4)] for h in range(16)]
    b_const = Buf("const")
    cslot = sch.slot()
    es_h = ExitStack()
    hT = SB(es_h, "hT", [128, 16, S], BF)

    cslot_p = sch.slot()
    b_constp = Buf("constp")

    def ld_const(dst, src, eng="pool"):
        if eng == "pool":
            sch.dma(eng, lambda e, inc, dst=dst, src=src: inc(e.dma_start(out=dst, in_=src)), cslot_p,
                    writes=[b_constp])
        else:
            sch.dma(eng, lambda e, inc, dst=dst, src=src: inc(e.dma_start(out=dst, in_=src)), cslot,
                    writes=[b_const])

    def sync_pool_consts():
        for eng in ("pe", "act", "dve", "pool"):
            sch.wait_all(eng, [(cslot_p.key, cslot_p.count)])

    ld_const(ident_bf[:], c_ident_d[:])
    ld_const(ones_bf[:], c_ones_d[:])
    ld_const(ident_f[:], c_ident_d[:], eng="sp")
    ld_const(ones_f[:], c_ones_d[:], eng="sp")
    sync_pool_consts()

    es = ExitStack()
    gB = SB(es, "gB", [128, D], F32)
    ld_const(gB[:], gpre_d[:], eng="sp")
    xt = [SB(es, "xt%d" % i, [128, D], F32) for i in range(2)]
    b_xt = [Buf("xt%d" % i) for i in range(2)]
    xslot = [sch.slot() for _ in range(2)]
    junk = SB(es, "junk", [128, D], BF)
    b_junk = Buf("junk")
    stat = SB(es, "stat", [128, 16, 4], F32)
    b_stat = [Buf("stat%d" % i) for i in range(16)]
    hn = [SB(es, "hn%d" % i, [128, D], BF) for i in range(2)]
    b_hn = [Buf("hn%d" % i) for i in range(2)]
    ptr = [PS(es, "ptr%d" % i, [128, 1024], BF) for i in range(4)]
    b_ptr = [Buf("ptr%d" % i) for i in range(4)]

    for tb in range(16):
        i = tb % 2
        sch.dma("sp", lambda e, inc, i=i, tb=tb: inc(e.dma_start(out=xt[i][:], in_=x_d[tb * 128:(tb + 1) * 128, :])),
                xslot[i], writes=[b_xt[i]])
        sch.op("act", lambda e, i=i, tb=tb: e.activation(out=junk[:], in_=xt[i][:], func=AF.Square,
                                                          accum_out=stat[:, tb, 0:1]),
               reads=[b_xt[i]], writes=[b_junk, b_stat[tb]])
        sch.op("act", lambda e, tb=tb: e.activation(out=stat[:, tb, 1:2], in_=stat[:, tb, 0:1], func=AF.Ln,
                                                    scale=1.0 / D, bias=EPS),
               reads=[b_stat[tb]], writes=[b_stat[tb]])
        sch.op("act", lambda e, tb=tb: e.activation(out=stat[:, tb, 2:3], in_=stat[:, tb, 1:2], func=AF.Exp,
                                                    scale=-0.5),
               reads=[b_stat[tb]], writes=[b_stat[tb]])
        sch.op("dve", lambda e, i=i, tb=tb: e.scalar_tensor_tensor(out=hn[i][:], in0=xt[i][:], scalar=stat[:, tb, 2:3],
                                                                  in1=gB[:], op0=ALU.mult, op1=ALU.mult),
               reads=[b_xt[i], b_stat[tb], b_const], writes=[b_hn[i]])
        for half in range(2):
            pb = (tb * 2 + half) % 4
            fns = []
            for j in range(8):
                c = half * 8 + j
                fns.append(lambda e, pb=pb, j=j, c=c, i=i: e.transpose(ptr[pb][:, j * 128:(j + 1) * 128],
                                                                    hn[i][:, c * 128:(c + 1) * 128], ident_bf[:]))
            sch.group(fns, reads=[b_hn[i], b_const], writes=[b_ptr[pb]])
            dst = hT[:, half * 8:(half + 1) * 8, tb * 128:(tb + 1) * 128]
            src = ptr[pb][:].rearrange("p (c t) -> p c t", c=8)
            if half == 0:
                sch.op("act", lambda e, dst=dst, src=src: e.copy(out=dst, in_=src), reads=[b_ptr[pb]], writes=[b_hT[tb]])
            else:
                sch.op("dve", lambda e, dst=dst, src=src: e.tensor_copy(out=dst, in_=src), reads=[b_ptr[pb]],
                       writes=[b_hT[tb]])
    oslot = sch.slot()

    def finish(*inner):
        global LAST_COUNTS
        LAST_COUNTS = dict(sch.cnt)
        LAST_COUNTS["max_dma_slot"] = max([0] + [v for k, v in sch.seen["sp"].items() if k.startswith("dma")])
        sch.wait_all("sp", [(oslot.key, oslot.count)])
        sch.flush()
        for s_ in inner:
            s_.close()
        es0.close()
        return nc

    def dump(name, src_ap, bufs):
        sch.dma("sp", lambda e, inc: inc(e.dma_start(out=dbg_d[name][:], in_=src_ap)), oslot, reads=bufs)

    if stop_after == "p0":
        dump("hT", hT[:], b_hT)
        return finish(es, es_h)
    sch.flush()
    es.close()

    es = ExitStack()
    NW = 4
    wbuf = [SB(es, "wbuf%d" % i, [128, 16, 128], BF) for i in range(NW)]
    b_w = [Buf("w%d" % i) for i in range(NW)]
    wslot = [sch.slot() for _ in range(NW)]
    QT = SB(es, "QT", [128, S], BF)
    KT = SB(es, "KT", [128, S], BF)
    Vt = SB(es, "Vt", [128, 16, 128], BF)
    b_QT = [Buf("QT%d" % i) for i in range(4)]
    b_KT = [Buf("KT%d" % i) for i in range(4)]
    b_V = [Buf("V%d" % i) for i in range(4)]
    cosT = SB(es, "cosT", [128, S], F32)
    sinS = SB(es, "sinS", [128, S], F32)
    dlm = SB(es, "dlm", [128, 19 * 128], BF)
    sbm = SB(es, "sbm", [128, 7 * 128], BF)
    tge = SB(es, "tge", [128, 128], BF)
    og = SB(es, "og", [128, 16], F32)
    ld_const(cosT[:], c_cos_d[:], eng="sp")
    ld_const(sinS[:], c_sin_d[:], eng="sp")
    ld_const(og[:], og_d[:], eng="sp")
    ld_const(dlm[:], c_dlm_d[:])
    ld_const(sbm[:], c_sbm_d[:])
    ld_const(tge[:], c_tge_d[:])
    sync_pool_consts()

    def tmp(name, dt, n=2):
        ts = [SB(es, "%s%d" % (name, i), [128, 512], dt) for i in range(n)]
        return ts, [Buf("%s%d" % (name, i)) for i in range(n)]

    e_t, b_e = tmp("e_t", F32)
    sp_t, b_sp = tmp("sp_t", BF)
    t_t, b_t = tmp("t_t", F32)
    a_t, b_a = tmp("a_t", BF)
    csb_l, b_csb_l = tmp("csb", F32, 1)
    csb, b_csb = csb_l[0], b_csb_l[0]
    rr1, b_rr1 = e_t, b_e
    rr2, b_rr2 = t_t, b_t
    sq_l, b_sq_l = tmp("sq", F32, 1)
    sq, b_sq = sq_l[0], b_sq_l[0]
    rstd_l, b_rstd_l = tmp("rstd", F32, 1)
    rstd, b_rstd = rstd_l[0], b_rstd_l[0]
    lnv, b_lnv = rstd, b_rstd
    ot_l, b_ot_l = tmp("ot", F32, 1)
    ot, b_ot = ot_l[0], b_ot_l[0]
    rden, b_rden = sq, b_sq
    sqh_l, b_sqh_l = tmp("sqh", BF, 1)
    sqh, b_sqh = sqh_l[0], b_sqh_l[0]
    sql_l, b_sql_l = tmp("sql", BF, 1)
    sql, b_sql = sql_l[0], b_sql_l[0]

    pp = [PS(es, "pp%d" % i, [128, 512], F32) for i in range(2)]
    b_pp = [Buf("pp%d" % i) for i in range(2)]
    zb = [PS(es, "zb%d" % i, [128, 512], F32) for i in range(2)]
    b_zb = [Buf("zb%d" % i) for i in range(2)]
    wb = PS(es, "wb", [128, 512], F32)
    b_wb = Buf("wb")
    sbk = PS(es, "sbk", [128, 512], F32)
    b_sbk = Buf("sbk")
    ob = [PS(es, "ob%d" % i, [128, 512], F32) for i in range(2)]
    b_ob = [Buf("ob%d" % i) for i in range(2)]

    ctr = dict(w=0, p=0, z=0, t=0)

    def head_cols(h):
        if h < 8:
            return h * 128, 1024 + h * 128, 2048 + h * 128
        hh = h - 8
        return 3072 + hh * 128, 4096 + hh * 128, 5120 + hh * 128

    wq_list = []
    for h in range(NH):
        wq_list.extend(head_cols(h))
    loaded = {}

    def load_w(i):
        if i >= len(wq_list) or i in loaded:
            return
        k = i % NW
        col0 = wq_list[i]
        src = w_in_d[:, col0:col0 + 128].rearrange("(c p) e -> p c e", p=128)
        sch.dma("pool", lambda e, inc, k=k, src=src: inc(e.dma_start(out=wbuf[k][:], in_=src)), wslot[k],
                writes=[b_w[k]])
        loaded[i] = k

    for i in range(NW):
        load_w(i)

    def proj_fm(wi, evac):
        k = loaded[wi]
        for tt in range(4):
            pb = ctr["p"] % 2
            ctr["p"] += 1
            fns = []
            for c in range(16):
                fns.append(lambda e, pb=pb, k=k, c=c, tt=tt: e.matmul(
                    pp[pb][:], wbuf[k][:, c, :], hT[:, c, tt * 512:(tt + 1) * 512], start=(c == 0), stop=(c == 15)))
            sch.group(fns, reads=[b_w[k], b_const] + b_hT[4 * tt:4 * tt + 4], writes=[b_pp[pb]])
            evac(tt, pb)

    def evac_plain(dstT, b_dst, eng):
        def f(tt, pb):
            dst = dstT[:, tt * 512:(tt + 1) * 512]
            if eng == "act":
                sch.op("act", lambda e: e.copy(out=dst, in_=pp[pb][:]), reads=[b_pp[pb]], writes=[b_dst[tt]])
            else:
                sch.op("dve", lambda e: e.tensor_copy(out=dst, in_=pp[pb][:]), reads=[b_pp[pb]], writes=[b_dst[tt]])
        return f

    def evac_rope(dstT, b_dst):
        def f(tt, pb):
            i = ctr["t"] % 2
            ctr["t"] += 1
            ts = slice(tt * 512, (tt + 1) * 512)
            dst = dstT[:, ts]
            sch.op("dve", lambda e: e.tensor_tensor(out=rr1[i][:], in0=pp[pb][:], in1=cosT[:, ts], op=ALU.mult),
                   reads=[b_pp[pb], b_const], writes=[b_rr1[i]])
            sch.op("dve", lambda e: e.tensor_tensor(out=rr2[i][0:64, :], in0=pp[pb][64:128, :], in1=sinS[0:64, ts],
                                                    op=ALU.mult),
                   reads=[b_pp[pb], b_const], writes=[b_rr2[i]])
            sch.op("dve", lambda e: e.tensor_tensor(out=rr2[i][64:128, :], in0=pp[pb][0:64, :], in1=sinS[64:128, ts],
                                                    op=ALU.mult),
                   reads=[b_pp[pb], b_const], writes=[b_rr2[i]])
            sch.op("pool", lambda e: e.tensor_tensor(out=dst, in0=rr1[i][:], in1=rr2[i][:], op=ALU.add),
                   reads=[b_rr1[i], b_rr2[i]], writes=[b_dst[tt]])
        return f

    def proj_v(wi):
        k = loaded[wi]
        for g in range(4):
            pb = ctr["p"] % 2
            ctr["p"] += 1
            fns = []
            for tb in range(4 * g, 4 * g + 4):
                for c in range(16):
                    fns.append(lambda e, pb=pb, k=k, c=c, tb=tb: e.matmul(
                        pp[pb][:, (tb % 4) * 128:(tb % 4 + 1) * 128], hT[:, c, tb * 128:(tb + 1) * 128],
                        wbuf[k][:, c, :], start=(c == 0), stop=(c == 15)))
            sch.group(fns, reads=[b_w[k], b_const] + b_hT[4 * g:4 * g + 4], writes=[b_pp[pb]])
            dst = Vt[:, 4 * g:4 * g + 4, :]
            src = pp[pb][:].rearrange("p (a d) -> p a d", a=4)
            sch.op("act", lambda e, dst=dst, src=src: e.copy(out=dst, in_=src), reads=[b_pp[pb]], writes=[b_V[g]])

    def head_norm(h, Qt, src_ap, b_src):
        qs = slice(Qt * 512, (Qt + 1) * 512)
        sch.op("act", lambda e: e.activation(out=sq[:], in_=src_ap, func=AF.Square), reads=[b_src], writes=[b_sq])
        sch.op("dve", lambda e: e.tensor_copy(out=sqh[:], in_=sq[:]), reads=[b_sq], writes=[b_sqh])
        sch.op("dve", lambda e: e.tensor_tensor(out=sql[:], in0=sq[:], in1=sqh[:], op=ALU.subtract),
               reads=[b_sq, b_sqh], writes=[b_sql])
        sch.group([lambda e: e.matmul(wb[:], ones_bf[:], sqh[:], start=True, stop=False),
                   lambda e: e.matmul(wb[:], ones_bf[:], sql[:], start=False, stop=True)],
                  reads=[b_sqh, b_sql, b_const], writes=[b_wb])
        sch.op("act", lambda e: e.activation(out=lnv[:], in_=wb[:], func=AF.Ln, scale=1.0 / DH, bias=EPS),
               reads=[b_wb], writes=[b_lnv])
        sch.op("act", lambda e: e.activation(out=rstd[:], in_=lnv[:], func=AF.Exp, scale=-0.5),
               reads=[b_lnv], writes=[b_rstd])
        sch.op("dve", lambda e: e.scalar_tensor_tensor(out=mixT[:, h, qs], in0=src_ap, scalar=og[:, h:h + 1],
                                                       in1=rstd[:], op0=ALU.mult, op1=ALU.mult),
               reads=[b_src, b_rstd, b_const], writes=[b_mix[h][Qt]])

    wbs, b_wbs = [wb, pp[0]], [b_wb, b_pp[0]]
    sbks, b_sbks = [sbk, pp[1]], [b_sbk, b_pp[1]]

    def attn_sb(h):
        for Qt in range(4):
            qs = slice(Qt * 512, (Qt + 1) * 512)
            oi = Qt % 2
            kbs = list(range(4 * Qt + 3, -1, -1))
            for idx, kb in enumerate(kbs):
                zi = ctr["z"] % 2
                ctr["z"] += 1
                d0 = 4 * Qt - kb
                ks = slice(kb * 128, (kb + 1) * 128)
                sch.group([lambda e, zi=zi, ks=ks, qs=qs: e.matmul(zb[zi][:], KT[:, ks], QT[:, qs], start=True, stop=True)],
                          reads=[b_KT[kb // 4], b_QT[Qt]], writes=[b_zb[zi]])
                sch.op("act", lambda e, zi=zi: e.activation(out=e_t[zi][:], in_=zb[zi][:], func=AF.Exp, scale=SCALE),
                       reads=[b_zb[zi]], writes=[b_e[zi]])
                sch.op("act", lambda e, zi=zi: e.activation(out=sp_t[zi][:], in_=e_t[zi][:], func=AF.Ln, bias=1.0),
                       reads=[b_e[zi]], writes=[b_sp[zi]])
                if LVL < 2:
                    continue
                if d0 <= 0:
                    ms = slice(128 * (d0 + 3), 128 * (d0 + 3) + 512)
                    sch.op("pool", lambda e, zi=zi, ms=ms: e.tensor_tensor(out=sp_t[zi][:], in0=sp_t[zi][:],
                                                                          in1=sbm[:, ms], op=ALU.mult),
                           reads=[b_sp[zi], b_const], writes=[b_sp[zi]])
                wbz, b_wbz, sbz, b_sbz = wbs[zi], b_wbs[zi], sbks[zi], b_sbks[zi]
                sch.group([lambda e, zi=zi, wbz=wbz: e.matmul(wbz[:], tge[:], sp_t[zi][:], start=True, stop=True)],
                          reads=[b_sp[zi], b_const], writes=[b_wbz])
                if kb > 0:
                    sch.group([lambda e, zi=zi, sbz=sbz: e.matmul(sbz[:], ones_bf[:], sp_t[zi][:], start=True, stop=True)],
                              reads=[b_sp[zi], b_const], writes=[b_sbz])
                if LVL < 3:
                    continue
                if idx == 0:
                    sch.op("dve", lambda e, zi=zi: e.tensor_scalar(out=t_t[zi][:], in0=zb[zi][:], scalar1=SCALE,
                                                                   scalar2=None, op0=ALU.mult),
                           reads=[b_zb[zi]], writes=[b_t[zi]])
                else:
                    sch.op("dve", lambda e, zi=zi: e.scalar_tensor_tensor(out=t_t[zi][:], in0=zb[zi][:], scalar=SCALE,
                                                                          in1=csb[:], op0=ALU.mult, op1=ALU.subtract),
                           reads=[b_zb[zi], b_csb], writes=[b_t[zi]])
                if SUB < 2:
                    continue
                sch.op("dve", lambda e, zi=zi, wbz=wbz: e.scalar_tensor_tensor(out=t_t[zi][:], in0=wbz[:], scalar=-1.0,
                                                                               in1=t_t[zi][:], op0=ALU.mult, op1=ALU.add),
                       reads=[b_t[zi], b_wbz], writes=[b_t[zi]])
                if kb > 0 and SUB >= 3:
                    if idx == 0:
                        sch.op("dve", lambda e, sbz=sbz: e.tensor_copy(out=csb[:], in_=sbz[:]), reads=[b_sbz],
                               writes=[b_csb])
                    else:
                        sch.op("dve", lambda e, sbz=sbz: e.tensor_tensor(out=csb[:], in0=sbz[:], in1=csb[:], op=ALU.add),
                               reads=[b_csb, b_sbz], writes=[b_csb])
                if LVL < 4:
                    continue
                sch.op("act", lambda e, zi=zi: e.activation(out=a_t[zi][:], in_=t_t[zi][:], func=AF.Exp),
                       reads=[b_t[zi]], writes=[b_a[zi]])
                if d0 <= 0:
                    sch.op("pool", lambda e, zi=zi, ms=ms: e.tensor_tensor(out=a_t[zi][:], in0=a_t[zi][:],
                                                                          in1=sbm[:, ms], op=ALU.mult),
                           reads=[b_a[zi], b_const], writes=[b_a[zi]])
                sch.group([lambda e, zi=zi, kb=kb, oi=oi, idx=idx, n=len(kbs): e.matmul(
                    ob[oi][:], Vt[:, kb, :], a_t[zi][:], start=(idx == 0), stop=(idx == n - 1))],
                    reads=[b_a[zi], b_V[kb // 4]], writes=[b_ob[oi]])
            if LVL >= 5:
                head_norm(h, Qt, ob[oi][:], b_ob[oi])

    def attn_dl(h):
        for Qt in range(4):
            qs = slice(Qt * 512, (Qt + 1) * 512)
            oi = Qt % 2
            kbs = list(range(4 * Qt + 3, -1, -1))
            n = len(kbs)
            for idx, kb in enumerate(kbs):
                zi = ctr["z"] % 2
                ctr["z"] += 1
                d0 = 4 * Qt - kb
                ks = slice(kb * 128, (kb + 1) * 128)
                ms = slice(128 * (d0 + 3), 128 * (d0 + 3) + 512)
                sch.group([lambda e, zi=zi, ks=ks, qs=qs: e.matmul(zb[zi][:], KT[:, ks], QT[:, qs], start=True, stop=True)],
                          reads=[b_KT[kb // 4], b_QT[Qt]], writes=[b_zb[zi]])
                sch.op("act", lambda e, zi=zi: e.activation(out=e_t[zi][:], in_=zb[zi][:], func=AF.Exp, scale=SCALE),
                       reads=[b_zb[zi]], writes=[b_e[zi]])
                meng = "dve" if (idx % 2 == 0) else "pool"
                sch.op(meng, lambda e, zi=zi, ms=ms: e.tensor_tensor(out=a_t[zi][:], in0=e_t[zi][:], in1=dlm[:, ms],
                                                                    op=ALU.mult),
                       reads=[b_e[zi], b_const], writes=[b_a[zi]])
                sch.group([lambda e, zi=zi, kb=kb, oi=oi, idx=idx, n=n: e.matmul(
                    ob[oi][:], Vt[:, kb, :], a_t[zi][:], start=(idx == 0), stop=(idx == n - 1))],
                    reads=[b_a[zi], b_V[kb // 4]], writes=[b_ob[oi]])
                sch.group([lambda e, zi=zi, idx=idx, n=n: e.matmul(
                    sbk[:], ones_bf[:], a_t[zi][:], start=(idx == 0), stop=(idx == n - 1))],
                    reads=[b_a[zi], b_const], writes=[b_sbk])
            sch.op("dve", lambda e: e.reciprocal(out=rden[:], in_=sbk[:]), reads=[b_sbk], writes=[b_rden])
            sch.op("dve", lambda e, oi=oi: e.tensor_tensor(out=ot[:], in0=ob[oi][:], in1=rden[:], op=ALU.mult),
                   reads=[b_ob[oi], b_rden], writes=[b_ot])
            head_norm(h, Qt, ot[:], b_ot)

    heads = {"pA1": [0, 8], "pAs": [0], "pAd": [8], "pAp": [0]}.get(stop_after, list(range(NH)))
    for h in heads:
        wi = 3 * h
        for j in range(3):
            load_w(wi + j)
        if h < 8:
            proj_fm(wi, evac_plain(QT, b_QT, "act"))
            load_w(wi + 4)
            proj_fm(wi + 1, evac_plain(KT, b_KT, "dve"))
            load_w(wi + 5)
            proj_v(wi + 2)
            load_w(wi + 6)
            if stop_after != "pAp":
                attn_sb(h)
        else:
            proj_fm(wi, evac_rope(QT, b_QT))
            load_w(wi + 4)
            proj_fm(wi + 1, evac_rope(KT, b_KT))
            load_w(wi + 5)
            proj_v(wi + 2)
            load_w(wi + 6)
            attn_dl(h)

    if stop_after in ("pA", "pA1", "pAs", "pAd", "pAp"):
        dump("mixT", mixT[:], [b for row in b_mix for b in row])
        return finish(es, es_h)
    sch.flush()
    es.close()
    es_h.close()

    es = ExitStack()
    x1T = SB(es, "x1T", [128, 16, 512], F32)
    accT = SB(es, "accT", [128, 16, 512], F32)
    h2T = SB(es, "h2T", [128, 16, 512], BF)
    b_x1 = [Buf("x1_%d" % i) for i in range(16)]
    b_acc = [Buf("acc_%d" % i) for i in range(16)]
    b_h2 = [Buf("h2_%d" % i) for i in range(16)]
    NWT = 4
    wbt = [SB(es, "wbt%d" % i, [128, 16, 128], BF) for i in range(NWT)]
    b_wt = [Buf("wt%d" % i) for i in range(NWT)]
    wtslot = [sch.slot() for _ in range(NWT)]
    NWD = 4
    wdb = [SB(es, "wdb%d" % i, [128, D], BF) for i in range(NWD)]
    b_wd = [Buf("wd%d" % i) for i in range(NWD)]
    wdslot = [sch.slot() for _ in range(NWD)]
    yT = [SB(es, "yT%d" % i, [128, 512], BF) for i in range(4)]
    b_y = [Buf("y%d" % i) for i in range(4)]
    g1 = SB(es, "g1", [128, 16], F32)
    g2 = SB(es, "g2", [128, 16], F32)
    g3 = SB(es, "g3", [128, 16], F32)
    cw = SB(es, "cw", [128, 3, 86], F32)
    cb = SB(es, "cb", [128, 86], F32)
    carry = [SB(es, "carry%d" % i, [128, 86, 2], F32) for i in range(2)]
    b_carry = [[Buf("carry%d_%d" % (i, c)) for c in range(86)] for i in range(2)]
    ld_const(g1[:], gpost_d[:], eng="sp")
    ld_const(g2[:], gpre2_d[:], eng="sp")
    ld_const(g3[:], gpost2_d[:], eng="sp")
    ld_const(cw[:], cw_d[:], eng="sp")
    ld_const(cb[:], cb_d[:], eng="sp")
    xs = [SB(es, "xs%d" % i, [128, D], F32) for i in range(1)]
    b_xs = [Buf("xs%d" % i) for i in range(1)]
    xsslot = [sch.slot() for _ in range(1)]
    ta, b_ta = [], []
    for i in range(3):
        ta.append(SB(es, "ta%d" % i, [128, 512], F32))
        b_ta.append(Buf("ta%d" % i))
    gl = SB(es, "gl", [128, 512], F32)
    b_gl = Buf("gl")
    sqb = [SB(es, "sqb%d" % i, [128, 512], BF) for i in range(2)]
    b_sqb = [Buf("sqb%d" % i) for i in range(2)]
    rs = SB(es, "rs", [128, 512], F32)
    b_rs = Buf("rs")
    PSug = [PS(es, "PSug%d" % i, [128, 512], F32) for i in range(2)]
    b_ug = [Buf("PSug%d" % i) for i in range(2)]
    PSuv = [PS(es, "PSuv%d" % i, [128, 512], F32) for i in range(2)]
    b_uv = [Buf("PSuv%d" % i) for i in range(2)]
    PSdp = [PS(es, "PSdp%d" % i, [128, 512], F32) for i in range(2)]
    b_dp = [Buf("PSdp%d" % i) for i in range(2)]
    PSss = PS(es, "PSss", [128, 512], F32)
    b_ss = Buf("PSss")
    PStr = PS(es, "PStr", [128, 512], F32)
    b_tr = Buf("PStr")
    tc = dict(w=0, wd=0, dp=0, ta=0, sq=0, u=0, xs=0)

    def load_wt(src):
        k = tc["w"] % NWT
        tc["w"] += 1
        sch.dma("pool", lambda e, inc, k=k, src=src: inc(e.dma_start(out=wbt[k][:], in_=src)), wtslot[k],
                writes=[b_wt[k]])
        return k

    def load_wd(fc):
        k = tc["wd"] % NWD
        tc["wd"] += 1
        src = w_down_d[fc * 128:(fc + 1) * 128, :]
        sch.dma("pool", lambda e, inc, k=k, src=src: inc(e.dma_start(out=wdb[k][:], in_=src)), wdslot[k],
                writes=[b_wd[k]])
        return k

    def wo_src(dc):
        return w_out_d[:, dc * 128:(dc + 1) * 128].rearrange("(c p) e -> p c e", p=128)

    def wu_src(col0):
        return w_up_d[:, col0:col0 + 128].rearrange("(c p) e -> p c e", p=128)

    def norm_rstd(n):
        sch.op("act", lambda e: e.activation(out=rs[:], in_=PSss[:], func=AF.Ln, scale=1.0 / n, bias=EPS),
               reads=[b_ss], writes=[b_rs])
        sch.op("act", lambda e: e.activation(out=rs[:], in_=rs[:], func=AF.Exp, scale=-0.5), reads=[b_rs],
               writes=[b_rs])

    def sumsq(src_ap, b_src, dc):
        i = tc["sq"] % 2
        tc["sq"] += 1
        sch.op("act", lambda e, i=i: e.activation(out=sqb[i][:], in_=src_ap, func=AF.Square), reads=[b_src],
               writes=[b_sqb[i]])
        sch.group([lambda e, i=i, dc=dc: e.matmul(PSss[:], ones_bf[:], sqb[i][:], start=(dc == 0), stop=(dc == 15))],
                  reads=[b_sqb[i], b_const], writes=[b_ss])

    ntiles = 4 if stop_after not in ("pT1", "pTx") else 1
    for tt in range(ntiles):
        ts = slice(tt * 512, (tt + 1) * 512)
        for j in range(4):
            i = 0
            tc["xs"] += 1
            r0 = tt * 512 + j * 128
            sch.dma("sp", lambda e, inc, i=i, r0=r0: inc(e.dma_start(out=xs[i][:], in_=x_d[r0:r0 + 128, :])), xsslot[i],
                    writes=[b_xs[i]])
            for q4 in range(4):
                fns = []
                for dd in range(4):
                    dc = q4 * 4 + dd
                    fns.append(lambda e, i=i, dc=dc, dd=dd: e.matmul(PStr[:, dd * 128:(dd + 1) * 128],
                                                                    xs[i][:, dc * 128:(dc + 1) * 128], ident_f[:],
                                                                    start=True, stop=True))
                sch.group(fns, reads=[b_xs[i], b_const], writes=[b_tr])
                dst = x1T[:, q4 * 4:q4 * 4 + 4, j * 128:(j + 1) * 128]
                src = PStr[:].rearrange("p (a t) -> p a t", a=4)
                eng = "act" if q4 % 2 == 0 else "dve"
                if eng == "act":
                    sch.op("act", lambda e, dst=dst, src=src: e.copy(out=dst, in_=src), reads=[b_tr],
                           writes=b_x1[q4 * 4:q4 * 4 + 4])
                else:
                    sch.op("dve", lambda e, dst=dst, src=src: e.tensor_copy(out=dst, in_=src), reads=[b_tr],
                           writes=b_x1[q4 * 4:q4 * 4 + 4])
        wks = {}
        for dc in range(2):
            wks[dc] = load_wt(wo_src(dc))
        for dc in range(16):
            if dc + 2 < 16:
                wks[dc + 2] = load_wt(wo_src(dc + 2))
            k = wks[dc]
            pb = tc["dp"] % 2
            tc["dp"] += 1
            fns = []
            for c in range(16):
                fns.append(lambda e, pb=pb, k=k, c=c, ts=ts: e.matmul(PSdp[pb][:], wbt[k][:, c, :], mixT[:, c, ts],
                                                                     start=(c == 0), stop=(c == 15)))
            sch.group(fns, reads=[b_wt[k]] + [b_mix[h][tt] for h in range(16)], writes=[b_dp[pb]])
            sch.op("act", lambda e, pb=pb, dc=dc: e.copy(out=accT[:, dc, :], in_=PSdp[pb][:]), reads=[b_dp[pb]],
                   writes=[b_acc[dc]])
            sumsq(accT[:, dc, :], b_acc[dc], dc)
        norm_rstd(D)
        for dc in range(16):
            i = tc["ta"] % 3
            tc["ta"] += 1
            sch.op("dve", lambda e, i=i, dc=dc: e.scalar_tensor_tensor(out=ta[i][:], in0=accT[:, dc, :],
                                                                      scalar=g1[:, dc:dc + 1], in1=rs[:],
                                                                      op0=ALU.mult, op1=ALU.mult),
                   reads=[b_acc[dc], b_rs, b_const], writes=[b_ta[i]])
            sch.op("pool", lambda e, i=i, dc=dc: e.tensor_tensor(out=x1T[:, dc, :], in0=ta[i][:], in1=x1T[:, dc, :],
                                                                op=ALU.add),
                   reads=[b_ta[i], b_x1[dc]], writes=[b_x1[dc]])
        for dc in range(16):
            sumsq(x1T[:, dc, :], b_x1[dc], dc)
        norm_rstd(D)
        for dc in range(16):
            sch.op("dve", lambda e, dc=dc: e.scalar_tensor_tensor(out=h2T[:, dc, :], in0=x1T[:, dc, :],
                                                                 scalar=g2[:, dc:dc + 1], in1=rs[:],
                                                                 op0=ALU.mult, op1=ALU.mult),
                   reads=[b_x1[dc], b_rs, b_const], writes=[b_h2[dc]])
        if stop_after == "pTx":
            dump("x1T", x1T[:], b_x1)
            dump("h2T", h2T[:], b_h2)
            return finish(es)
        cpar = tt % 2
        groups = [list(range(g0, min(g0 + 4, NFC))) for g0 in range(0, NFC, 4)]
        pre = {}

        def prefetch(i):
            if i < NFC and i not in pre:
                pre[i] = (load_wt(wu_src(i * 128)), load_wt(wu_src(DFF + i * 128)))

        prefetch(0)
        prefetch(1)
        for gi, grp in enumerate(groups):
            wdk = {}
            ys = {}
            for i in grp:
                kg, kv = pre[i]
                ui = tc["u"] % 2
                tc["u"] += 1
                A = {}
                for (which, kk, PSu, b_u, ch) in (("g", kg, PSug, b_ug, i), ("v", kv, PSuv, b_uv, NFC + i)):
                    fns = []
                    for c in range(16):
                        fns.append(lambda e, ui=ui, kk=kk, c=c, PSu=PSu: e.matmul(
                            PSu[ui][:], wbt[kk][:, c, :], h2T[:, c, :], start=(c == 0), stop=(c == 15)))
                    sch.group(fns, reads=[b_wt[kk]] + b_h2, writes=[b_u[ui]])
                    ai = tc["ta"] % 3
                    tc["ta"] += 1
                    A[which] = ai
                    u = PSu[ui]
                    sch.op("act", lambda e, ai=ai, u=u, ch=ch: e.activation(out=ta[ai][:], in_=u[:], func=AF.Identity,
                                                                            scale=cw[:, 2, ch:ch + 1],
                                                                            bias=cb[:, ch:ch + 1]),
                           reads=[b_u[ui], b_const], writes=[b_ta[ai]])
                    sch.op("dve", lambda e, ai=ai, u=u, ch=ch: e.scalar_tensor_tensor(
                        out=ta[ai][:, 1:512], in0=u[:, 0:511], scalar=cw[:, 1, ch:ch + 1], in1=ta[ai][:, 1:512],
                        op0=ALU.mult, op1=ALU.add), reads=[b_u[ui], b_ta[ai], b_const], writes=[b_ta[ai]])
                    sch.op("dve", lambda e, ai=ai, u=u, ch=ch: e.scalar_tensor_tensor(
                        out=ta[ai][:, 2:512], in0=u[:, 0:510], scalar=cw[:, 0, ch:ch + 1], in1=ta[ai][:, 2:512],
                        op0=ALU.mult, op1=ALU.add), reads=[b_u[ui], b_ta[ai], b_const], writes=[b_ta[ai]])
                    if tt > 0:
                        pc_ = carry[1 - cpar]
                        sch.op("dve", lambda e, ai=ai, pc_=pc_, ch=ch: e.scalar_tensor_tensor(
                            out=ta[ai][:, 0:2], in0=pc_[:, ch, 0:2], scalar=cw[:, 0, ch:ch + 1], in1=ta[ai][:, 0:2],
                            op0=ALU.mult, op1=ALU.add), reads=[b_carry[1 - cpar][ch], b_ta[ai], b_const],
                            writes=[b_ta[ai]])
                        sch.op("dve", lambda e, ai=ai, pc_=pc_, ch=ch: e.scalar_tensor_tensor(
                            out=ta[ai][:, 0:1], in0=pc_[:, ch, 1:2], scalar=cw[:, 1, ch:ch + 1], in1=ta[ai][:, 0:1],
                            op0=ALU.mult, op1=ALU.add), reads=[b_carry[1 - cpar][ch], b_ta[ai], b_const],
                            writes=[b_ta[ai]])
                    if tt < ntiles - 1:
                        sch.op("dve", lambda e, u=u, ch=ch, cdst=carry[cpar]: e.tensor_copy(out=cdst[:, ch, :],
                                                                                            in_=u[:, 510:512]),
                               reads=[b_u[ui]], writes=[b_carry[cpar][ch]])
                prefetch(i + 2)
                sch.op("act", lambda e, ag=A["g"]: e.activation(out=gl[:], in_=ta[ag][:], func=AF.Gelu_apprx_tanh),
                       reads=[b_ta[A["g"]]], writes=[b_gl])
                yi = i % 4
                ys[i] = yi
                sch.op("pool", lambda e, yi=yi, av=A["v"]: e.tensor_tensor(out=yT[yi][:], in0=gl[:], in1=ta[av][:],
                                                                          op=ALU.mult),
                       reads=[b_gl, b_ta[A["v"]]], writes=[b_y[yi]])
                wdk[i] = load_wd(i)
            for dc in range(16):
                pb = tc["dp"] % 2
                tc["dp"] += 1
                fns = []
                for n_, i in enumerate(grp):
                    fns.append(lambda e, pb=pb, wk=wdk[i], yk=ys[i], dc=dc, n_=n_, L_=len(grp): e.matmul(
                        PSdp[pb][:], wdb[wk][:, dc * 128:(dc + 1) * 128], yT[yk][:], start=(n_ == 0),
                        stop=(n_ == L_ - 1)))
                sch.group(fns, reads=[b_wd[wdk[i]] for i in grp] + [b_y[ys[i]] for i in grp], writes=[b_dp[pb]])
                if gi == 0:
                    sch.op("dve", lambda e, pb=pb, dc=dc: e.tensor_copy(out=accT[:, dc, :], in_=PSdp[pb][:]),
                           reads=[b_dp[pb]], writes=[b_acc[dc]])
                else:
                    sch.op("dve", lambda e, pb=pb, dc=dc: e.tensor_tensor(out=accT[:, dc, :], in0=PSdp[pb][:],
                                                                         in1=accT[:, dc, :], op=ALU.add),
                           reads=[b_dp[pb], b_acc[dc]], writes=[b_acc[dc]])
        if stop_after == "pT1":
            dump("fT", accT[:], b_acc)
        for dc in range(16):
            sumsq(accT[:, dc, :], b_acc[dc], dc)
        norm_rstd(D)
        for dc in range(16):
            i = tc["ta"] % 3
            tc["ta"] += 1
            sch.op("dve", lambda e, i=i, dc=dc: e.scalar_tensor_tensor(out=ta[i][:], in0=accT[:, dc, :],
                                                                      scalar=g3[:, dc:dc + 1], in1=rs[:],
                                                                      op0=ALU.mult, op1=ALU.mult),
                   reads=[b_acc[dc], b_rs, b_const], writes=[b_ta[i]])
            sch.op("pool", lambda e, i=i, dc=dc: e.tensor_tensor(out=accT[:, dc, :], in0=ta[i][:], in1=x1T[:, dc, :],
                                                                op=ALU.add),
                   reads=[b_ta[i], b_x1[dc]], writes=[b_acc[dc]])
        for j in range(4):
            i = 0
            tc["xs"] += 1
            r0 = tt * 512 + j * 128
            for q4 in range(4):
                fns = []
                for dd in range(4):
                    dc = q4 * 4 + dd
                    fns.append(lambda e, dc=dc, dd=dd, j=j: e.matmul(PStr[:, dd * 128:(dd + 1) * 128],
                                                                    accT[:, dc, j * 128:(j + 1) * 128], ident_f[:],
                                                                    start=True, stop=True))
                sch.group(fns, reads=b_acc[q4 * 4:q4 * 4 + 4] + [b_const], writes=[b_tr])
                dst = xs[i][:, q4 * 512:(q4 + 1) * 512]
                if q4 % 2 == 0:
                    sch.op("act", lambda e, dst=dst: e.copy(out=dst, in_=PStr[:]), reads=[b_tr], writes=[b_xs[i]])
                else:
                    sch.op("dve", lambda e, dst=dst: e.tensor_copy(out=dst, in_=PStr[:]), reads=[b_tr], writes=[b_xs[i]])
            sch.dma("sp", lambda e, inc, i=i, r0=r0: inc(e.dma_start(out=out_d[r0:r0 + 128, :], in_=xs[i][:])), oslot,
                    reads=[b_xs[i]])
    return finish(es)


def make_in_maps(inputs, cores=None):
    f32 = np.float32
    cores = list(range(NCORES)) if cores is None else cores
    c = _consts()
    L = 0

    def pc(v):
        v = np.asarray(v, f32)
        return np.ascontiguousarray(v.reshape(-1, 128).T)

    shared = dict(
        w_in=np.ascontiguousarray(inputs["w_in"][L], f32),
        w_out=np.ascontiguousarray(inputs["w_out"][L], f32),
        w_up=np.ascontiguousarray(inputs["w_up"][L], f32),
        w_down=np.ascontiguousarray(inputs["w_down"][L], f32),
        g_pre_mix=np.ascontiguousarray(np.broadcast_to(np.asarray(inputs["pre_mix_gain"][L], f32)[None, :], (128, D))),
        g_post_mix=pc(inputs["post_mix_gain"][L]),
        g_pre_ffn=pc(inputs["pre_ffn_gain"][L]),
        g_post_ffn=pc(inputs["post_ffn_gain"][L]),
        g_heads=pc(np.concatenate([np.asarray(inputs["sb_out_gain"][L]), np.asarray(inputs["dil_out_gain"][L])])),
        conv_w=np.ascontiguousarray(np.asarray(inputs["conv_w"][L], f32).reshape(3, 86, 128).transpose(2, 0, 1)),
        conv_b=pc(inputs["conv_b"][L]),
    )
    shared.update(c)
    maps = []
    for b in cores:
        m = dict(shared)
        m["x"] = np.ascontiguousarray(inputs["x"][b], f32)
        maps.append(m)
    return maps


def kernel(**inputs):
    nc = build()
    maps = make_in_maps(inputs)
    res = run_bass_kernel_spmd(nc, maps, core_ids=list(range(NCORES)))
    return np.stack([np.asarray(r["out"], np.float32) for r in res.results], axis=0)
```

```python
import numpy as np
import os
LVL = int(os.environ.get('ATT_LVL', '9'))
SUB = int(os.environ.get('ATT_SUB', '9'))
from contextlib import ExitStack
import concourse.bass as bass
import concourse.mybir as mybir
from concourse.bass_utils import run_bass_kernel_spmd

F32 = mybir.dt.float32
BF = mybir.dt.bfloat16
AF = mybir.ActivationFunctionType
ALU = mybir.AluOpType

S = 2048
D = 2048
NH = 16
DH = 128
DFF = 5504
NFC = DFF // 128
QKV = 6144
EPS = 1e-6
SCALE = DH ** -0.5
NCORES = 8
LAST_COUNTS = {}


class Buf:
    __slots__ = ("name", "w", "r")

    def __init__(self, name):
        self.name = name
        self.w = None
        self.r = []


class Slot:
    def __init__(self, sem, key):
        self.sem = sem
        self.key = key
        self.count = 0


class Sched:
    ENGS = ("pe", "act", "dve", "pool", "sp")

    def __init__(self, nc, es):
        self.nc = nc
        self.q = {e: [] for e in self.ENGS}
        self.cnt = {e: 0 for e in self.ENGS}
        self.seen = {e: {} for e in self.ENGS}
        self.sems = {}
        for e in self.ENGS:
            self.sems[e] = es.enter_context(nc.semaphore("sem_" + e))
        self.es = es
        self.nslots = 0

    def slot(self):
        key = "dma%d" % self.nslots
        self.nslots += 1
        sem = self.es.enter_context(self.nc.semaphore("sem_" + key))
        self.sems[key] = sem
        return Slot(sem, key)

    def _deps(self, eng, reads, writes):
        deps = []
        for b in reads:
            if b.w is not None:
                deps.append(b.w)
            if b.name.startswith("zb") or b.name.startswith("ob") or b.name.startswith("wb") or b.name.startswith("sbk") \
                    or b.name.startswith("pp") or b.name.startswith("ptr") or b.name.startswith("PS"):
                deps.extend(t for t in b.r if t[0] != eng)
        for b in writes:
            if b.w is not None:
                deps.append(b.w)
            deps.extend(b.r)
        out = {}
        for (k, v) in deps:
            if k == "pe" and eng == "pe":
                continue
            if v > out.get(k, 0):
                out[k] = v
        res = []
        for k, v in out.items():
            if v > self.seen[eng].get(k, 0):
                self.seen[eng][k] = v
                res.append((k, v))
        return res

    def _emit_waits(self, eng, waits):
        for (k, v) in waits:
            sem = self.sems[k]
            self.q[eng].append(lambda e, sem=sem, v=v: e.wait_ge(sem, v))

    def _mark(self, ticket, reads, writes):
        for b in writes:
            b.w = ticket
            b.r = []
        for b in reads:
            b.r.append(ticket)

    def op(self, eng, fn, reads=(), writes=()):
        waits = self._deps(eng, reads, writes)
        self._emit_waits(eng, waits)
        self.cnt[eng] += 1
        sem = self.sems[eng]
        self.q[eng].append(lambda e, fn=fn, sem=sem: fn(e).then_inc(sem, 1))
        t = (eng, self.cnt[eng])
        self._mark(t, reads, writes)
        return t

    def group(self, fns, reads=(), writes=()):
        eng = "pe"
        waits = self._deps(eng, reads, writes)
        self._emit_waits(eng, waits)
        self.cnt[eng] += 1
        sem = self.sems[eng]
        n = len(fns)
        for i, fn in enumerate(fns):
            if i == n - 1:
                self.q[eng].append(lambda e, fn=fn, sem=sem: fn(e).then_inc(sem, 1))
            else:
                self.q[eng].append(lambda e, fn=fn: fn(e))
        t = (eng, self.cnt[eng])
        self._mark(t, reads, writes)
        return t

    def dma(self, eng, fn, slot, reads=(), writes=(), n=1):
        waits = self._deps(eng, reads, writes)
        self._emit_waits(eng, waits)
        slot.count += 16 * n
        sem = slot.sem
        self.q[eng].append(lambda e, fn=fn, sem=sem: fn(e, lambda ins: ins.then_inc(sem, 16)))
        t = (slot.key, slot.count)
        self._mark(t, reads, writes)
        return t

    def wait_all(self, eng, tickets):
        waits = []
        for (k, v) in tickets:
            if v > self.seen[eng].get(k, 0):
                self.seen[eng][k] = v
                waits.append((k, v))
        self._emit_waits(eng, waits)

    def barrier(self, extra=()):
        tickets = [(e, self.cnt[e]) for e in self.ENGS if self.cnt[e] > 0] + list(extra)
        for eng in self.ENGS:
            self.wait_all(eng, [t for t in tickets if not (t[0] == eng and eng == "pe")])

    def flush(self):
        nc = self.nc
        q = self.q
        with nc.Block() as block:
            @block.tensor
            def _(e):
                for f in q["pe"]:
                    f(e)

            @block.scalar
            def _(e):
                for f in q["act"]:
                    f(e)

            @block.vector
            def _(e):
                for f in q["dve"]:
                    f(e)

            @block.gpsimd
            def _(e):
                for f in q["pool"]:
                    f(e)

            @block.sync
            def _(e):
                for f in q["sp"]:
                    f(e)
        self.q = {e: [] for e in self.ENGS}


def _consts():
    f32 = np.float32
    kl = np.arange(128)[:, None]
    x = np.arange(19 * 128)[None, :]
    dl = x - 384 - kl
    c = ((dl >= 0) & (dl <= 128)).astype(f32)
    c += ((dl >= 0) & (dl % 4 == 0) & (dl <= 512)).astype(f32)
    c += ((dl >= 0) & (dl % 16 == 0) & (dl <= 2048)).astype(f32)
    x7 = np.arange(7 * 128)[None, :]
    sbm = ((x7 - 384 - kl) > 0).astype(f32)
    j = np.arange(128)[:, None]
    s = np.arange(128)[None, :]
    tge = (j >= s).astype(f32)
    ident = np.eye(128, dtype=f32)
    ones = np.ones((128, 128), f32)
    inv_freq = (np.float32(10000.0) ** (-np.arange(0, 128, 2, dtype=f32) / np.float32(128))).astype(f32)
    ang = (np.arange(S, dtype=f32)[:, None] * inv_freq[None, :]).astype(f32)
    cos = np.cos(ang).astype(f32).T
    sin = np.sin(ang).astype(f32).T
    cosT = np.concatenate([cos, cos], axis=0)
    sinS = np.concatenate([-sin, sin], axis=0)
    return dict(c_dlm=c, c_sbm=sbm, c_tge=tge, c_ident=ident, c_ones=ones,
                c_cos=np.ascontiguousarray(cosT), c_sin=np.ascontiguousarray(sinS))


def build(dbg=None, stop_after=None):
    nc = bass.Bass("TRN2", target_bir_lowering=False)
    es0 = ExitStack()

    def din(name, shape, dt=F32):
        return nc.dram_tensor(name, list(shape), dt, kind="ExternalInput").ap()

    x_d = din("x", [S, D])
    w_in_d = din("w_in", [D, QKV])
    w_out_d = din("w_out", [D, D])
    w_up_d = din("w_up", [D, 2 * DFF])
    w_down_d = din("w_down", [DFF, D])
    gpre_d = din("g_pre_mix", [128, D])
    gpost_d = din("g_post_mix", [128, 16])
    gpre2_d = din("g_pre_ffn", [128, 16])
    gpost2_d = din("g_post_ffn", [128, 16])
    og_d = din("g_heads", [128, 16])
    cw_d = din("conv_w", [128, 3, 86])
    cb_d = din("conv_b", [128, 86])
    c_dlm_d = din("c_dlm", [128, 19 * 128])
    c_sbm_d = din("c_sbm", [128, 7 * 128])
    c_tge_d = din("c_tge", [128, 128])
    c_ident_d = din("c_ident", [128, 128])
    c_ones_d = din("c_ones", [128, 128])
    c_cos_d = din("c_cos", [128, S])
    c_sin_d = din("c_sin", [128, S])
    out_d = nc.dram_tensor("out", [S, D], F32, kind="ExternalOutput").ap()
    dbg_d = {}
    if dbg:
        for name, (shape, dt) in dbg.items():
            dbg_d[name] = nc.dram_tensor("dbg_" + name, list(shape), dt, kind="ExternalOutput").ap()

    def SB(es, name, shape, dt):
        return es.enter_context(nc.sbuf_tensor(name, list(shape), dt))

    def PS(es, name, shape, dt):
        return es.enter_context(nc.psum_tensor(name, list(shape), dt))

    sch = Sched(nc, es0)

    mixT = SB(es0, "mixT", [128, 16, S], BF)
    ident_bf = SB(es0, "ident_bf", [128, 128], BF)
    ident_f = SB(es0, "ident_f", [128, 128], F32)
    ones_bf = SB(es0, "ones_bf", [128, 128], BF)
    ones_f = SB(es0, "ones_f", [128, 128], F32)
    b_hT = [Buf("hT%d" % i) for i in range(16)]
    b_mix = [[Buf("mix%d_%d" % (h, q)) for q in range(4)] for h in range(16)]
    b_const = Buf("const")
    cslot = sch.slot()
    es_h = ExitStack()
    hT = SB(es_h, "hT", [128, 16, S], BF)

    cslot_p = sch.slot()
    b_constp = Buf("constp")

    def ld_const(dst, src, eng="pool"):
        if eng == "pool":
            sch.dma(eng, lambda e, inc, dst=dst, src=src: inc(e.dma_start(out=dst, in_=src)), cslot_p,
                    writes=[b_constp])
        else:
            sch.dma(eng, lambda e, inc, dst=dst, src=src: inc(e.dma_start(out=dst, in_=src)), cslot,
                    writes=[b_const])

    def sync_pool_consts():
        for eng in ("pe", "act", "dve", "pool"):
            sch.wait_all(eng, [(cslot_p.key, cslot_p.count)])

    ld_const(ident_bf[:], c_ident_d[:])
    ld_const(ones_bf[:], c_ones_d[:])
    ld_const(ident_f[:], c_ident_d[:], eng="sp")
    ld_const(ones_f[:], c_ones_d[:], eng="sp")
    sync_pool_consts()

    es = ExitStack()
    gB = SB(es, "gB", [128, D], F32)
    ld_const(gB[:], gpre_d[:], eng="sp")
    xt = [SB(es, "xt%d" % i, [128, D], F32) for i in range(2)]
    b_xt = [Buf("xt%d" % i) for i in range(2)]
    xslot = [sch.slot() for _ in range(2)]
    junk = SB(es, "junk", [128, D], BF)
    b_junk = Buf("junk")
    stat = SB(es, "stat", [128, 16, 4], F32)
    b_stat = [Buf("stat%d" % i) for i in range(16)]
    hn = [SB(es, "hn%d" % i, [128, D], BF) for i in range(2)]
    b_hn = [Buf("hn%d" % i) for i in range(2)]
    ptr = [PS(es, "ptr%d" % i, [128, 1024], BF) for i in range(4)]
    b_ptr = [Buf("ptr%d" % i) for i in range(4)]

    for tb in range(16):
        i = tb % 2
        sch.dma("sp", lambda e, inc, i=i, tb=tb: inc(e.dma_start(out=xt[i][:], in_=x_d[tb * 128:(tb + 1) * 128, :])),
                xslot[i], writes=[b_xt[i]])
        sch.op("act", lambda e, i=i, tb=tb: e.activation(out=junk[:], in_=xt[i][:], func=AF.Square,
                                                          accum_out=stat[:, tb, 0:1]),
               reads=[b_xt[i]], writes=[b_junk, b_stat[tb]])
        sch.op("act", lambda e, tb=tb: e.activation(out=stat[:, tb, 1:2], in_=stat[:, tb, 0:1], func=AF.Ln,
                                                    scale=1.0 / D, bias=EPS),
               reads=[b_stat[tb]], writes=[b_stat[tb]])
        sch.op("act", lambda e, tb=tb: e.activation(out=stat[:, tb, 2:3], in_=stat[:, tb, 1:2], func=AF.Exp,
                                                    scale=-0.5),
               reads=[b_stat[tb]], writes=[b_stat[tb]])
        sch.op("dve", lambda e, i=i, tb=tb: e.scalar_tensor_tensor(out=hn[i][:], in0=xt[i][:], scalar=stat[:, tb, 2:3],
                                                                  in1=gB[:], op0=ALU.mult, op1=ALU.mult),
               reads=[b_xt[i], b_stat[tb], b_const], writes=[b_hn[i]])
        for half in range(2):
            pb = (tb * 2 + half) % 4
            fns = []
            for j in range(8):
                c = half * 8 + j
                fns.append(lambda e, pb=pb, j=j, c=c, i=i: e.transpose(ptr[pb][:, j * 128:(j + 1) * 128],
                                                                    hn[i][:, c * 128:(c + 1) * 128], ident_bf[:]))
            sch.group(fns, reads=[b_hn[i], b_const], writes=[b_ptr[pb]])
            dst = hT[:, half * 8:(half + 1) * 8, tb * 128:(tb + 1) * 128]
            src = ptr[pb][:].rearrange("p (c t) -> p c t", c=8)
            if half == 0:
                sch.op("act", lambda e, dst=dst, src=src: e.copy(out=dst, in_=src), reads=[b_ptr[pb]], writes=[b_hT[tb]])
            else:
                sch.op("dve", lambda e, dst=dst, src=src: e.tensor_copy(out=dst, in_=src), reads=[b_ptr[pb]],
                       writes=[b_hT[tb]])
    oslot = sch.slot()

    def finish(*inner):
        global LAST_COUNTS
        LAST_COUNTS = dict(sch.cnt)
        LAST_COUNTS["max_dma_slot"] = max([0] + [v for k, v in sch.seen["sp"].items() if k.startswith("dma")])
        sch.wait_all("sp", [(oslot.key, oslot.count)])
        sch.flush()
        for s_ in inner:
            s_.close()
        es0.close()
        return nc

    def dump(name, src_ap, bufs):
        sch.dma("sp", lambda e, inc: inc(e.dma_start(out=dbg_d[name][:], in_=src_ap)), oslot, reads=bufs)

    if stop_after == "p0":
        dump("hT", hT[:], b_hT)
        return finish(es, es_h)
    sch.flush()
    es.close()

    es = ExitStack()
    NW = 4
    wbuf = [SB(es, "wbuf%d" % i, [128, 16, 128], BF) for i in range(NW)]
    b_w = [Buf("w%d" % i) for i in range(NW)]
    wslot = [sch.slot() for _ in range(NW)]
    QT = SB(es, "QT", [128, S], BF)
    KT = SB(es, "KT", [128, S], BF)
    Vt = SB(es, "Vt", [128, 16, 128], BF)
    b_QT = [Buf("QT%d" % i) for i in range(4)]
    b_KT = [Buf("KT%d" % i) for i in range(4)]
    b_V = [Buf("V%d" % i) for i in range(4)]
    cosT = SB(es, "cosT", [128, S], F32)
    sinS = SB(es, "sinS", [128, S], F32)
    dlm = SB(es, "dlm", [128, 19 * 128], BF)
    sbm = SB(es, "sbm", [128, 7 * 128], BF)
    tge = SB(es, "tge", [128, 128], BF)
    og = SB(es, "og", [128, 16], F32)
    ld_const(cosT[:], c_cos_d[:], eng="sp")
    ld_const(sinS[:], c_sin_d[:], eng="sp")
    ld_const(og[:], og_d[:], eng="sp")
    ld_const(dlm[:], c_dlm_d[:])
    ld_const(sbm[:], c_sbm_d[:])
    ld_const(tge[:], c_tge_d[:])
    sync_pool_consts()

    def tmp(name, dt, n=2):
        ts = [SB(es, "%s%d" % (name, i), [128, 512], dt) for i in range(n)]
        return ts, [Buf("%s%d" % (name, i)) for i in range(n)]

    e_t, b_e = tmp("e_t", F32)
    sp_t, b_sp = tmp("sp_t", BF)
    t_t, b_t = tmp("t_t", F32)
    a_t, b_a = tmp("a_t", BF)
    csb_l, b_csb_l = tmp("csb", F32, 1)
    csb, b_csb = csb_l[0], b_csb_l[0]
    rr1, b_rr1 = e_t, b_e
    rr2, b_rr2 = t_t, b_t
    sq_l, b_sq_l = tmp("sq", F32, 1)
    sq, b_sq = sq_l[0], b_sq_l[0]
    rstd_l, b_rstd_l = tmp("rstd", F32, 1)
    rstd, b_rstd = rstd_l[0], b_rstd_l[0]
    lnv, b_lnv = rstd, b_rstd
    ot_l, b_ot_l = tmp("ot", F32, 1)
    ot, b_ot = ot_l[0], b_ot_l[0]
    rden, b_rden = sq, b_sq
    sqh_l, b_sqh_l = tmp("sqh", BF, 1)
    sqh, b_sqh = sqh_l[0], b_sqh_l[0]
    sql_l, b_sql_l = tmp("sql", BF, 1)
    sql, b_sql = sql_l[0], b_sql_l[0]

    pp = [PS(es, "pp%d" % i, [128, 512], F32) for i in range(2)]
    b_pp = [Buf("pp%d" % i) for i in range(2)]
    zb = [PS(es, "zb%d" % i, [128, 512], F32) for i in range(2)]
    b_zb = [Buf("zb%d" % i) for i in range(2)]
    wb = PS(es, "wb", [128, 512], F32)
    b_wb = Buf("wb")
    sbk = PS(es, "sbk", [128, 512], F32)
    b_sbk = Buf("sbk")
    ob = [PS(es, "ob%d" % i, [128, 512], F32) for i in range(2)]
    b_ob = [Buf("ob%d" % i) for i in range(2)]

    ctr = dict(w=0, p=0, z=0, t=0)

    def head_cols(h):
        if h < 8:
            return h * 128, 1024 + h * 128, 2048 + h * 128
        hh = h - 8
        return 3072 + hh * 128, 4096 + hh * 128, 5120 + hh * 128

    wq_list = []
    for h in range(NH):
        wq_list.extend(head_cols(h))
    loaded = {}

    def load_w(i):
        if i >= len(wq_list) or i in loaded:
            return
        k = i % NW
        col0 = wq_list[i]
        src = w_in_d[:, col0:col0 + 128].rearrange("(c p) e -> p c e", p=128)
        sch.dma("pool", lambda e, inc, k=k, src=src: inc(e.dma_start(out=wbuf[k][:], in_=src)), wslot[k],
                writes=[b_w[k]])
        loaded[i] = k

    for i in range(NW):
        load_w(i)

    def proj_fm(wi, evac):
        k = loaded[wi]
        for tt in range(4):
            pb = ctr["p"] % 2
            ctr["p"] += 1
            fns = []
            for c in range(16):
                fns.append(lambda e, pb=pb, k=k, c=c, tt=tt: e.matmul(
                    pp[pb][:], wbuf[k][:, c, :], hT[:, c, tt * 512:(tt + 1) * 512], start=(c == 0), stop=(c == 15)))
            sch.group(fns, reads=[b_w[k], b_const] + b_hT[4 * tt:4 * tt + 4], writes=[b_pp[pb]])
            evac(tt, pb)

    def evac_plain(dstT, b_dst, eng):
        def f(tt, pb):
            dst = dstT[:, tt * 512:(tt + 1) * 512]
            if eng == "act":
                sch.op("act", lambda e: e.copy(out=dst, in_=pp[pb][:]), reads=[b_pp[pb]], writes=[b_dst[tt]])
            else:
                sch.op("dve", lambda e: e.tensor_copy(out=dst, in_=pp[pb][:]), reads=[b_pp[pb]], writes=[b_dst[tt]])
        return f

    def evac_rope(dstT, b_dst):
        def f(tt, pb):
            i = ctr["t"] % 2
            ctr["t"] += 1
            ts = slice(tt * 512, (tt + 1) * 512)
            dst = dstT[:, ts]
            sch.op("dve", lambda e: e.tensor_tensor(out=rr1[i][:], in0=pp[pb][:], in1=cosT[:, ts], op=ALU.mult),
                   reads=[b_pp[pb], b_const], writes=[b_rr1[i]])
            sch.op("dve", lambda e: e.tensor_tensor(out=rr2[i][0:64, :], in0=pp[pb][64:128, :], in1=sinS[0:64, ts],
                                                    op=ALU.mult),
                   reads=[b_pp[pb], b_const], writes=[b_rr2[i]])
            sch.op("dve", lambda e: e.tensor_tensor(out=rr2[i][64:128, :], in0=pp[pb][0:64, :], in1=sinS[64:128, ts],
                                                    op=ALU.mult),
                   reads=[b_pp[pb], b_const], writes=[b_rr2[i]])
            sch.op("pool", lambda e: e.tensor_tensor(out=dst, in0=rr1[i][:], in1=rr2[i][:], op=ALU.add),
                   reads=[b_rr1[i], b_rr2[i]], writes=[b_dst[tt]])
        return f

    def proj_v(wi):
        k = loaded[wi]
        for g in range(4):
            pb = ctr["p"] % 2
            ctr["p"] += 1
            fns = []
            for tb in range(4 * g, 4 * g + 4):
                for c in range(16):
                    fns.append(lambda e, pb=pb, k=k, c=c, tb=tb: e.matmul(
                        pp[pb][:, (tb % 4) * 128:(tb % 4 + 1) * 128], hT[:, c, tb * 128:(tb + 1) * 128],
                        wbuf[k][:, c, :], start=(c == 0), stop=(c == 15)))
            sch.group(fns, reads=[b_w[k], b_const] + b_hT[4 * g:4 * g + 4], writes=[b_pp[pb]])
            dst = Vt[:, 4 * g:4 * g + 4, :]
            src = pp[pb][:].rearrange("p (a d) -> p a d", a=4)
            sch.op("act", lambda e, dst=dst, src=src: e.copy(out=dst, in_=src), reads=[b_pp[pb]], writes=[b_V[g]])

    def head_norm(h, Qt, src_ap, b_src, ssq=None, b_ssq=None):
        ssq = wb if ssq is None else ssq
        b_ssq = b_wb if b_ssq is None else b_ssq
        qs = slice(Qt * 512, (Qt + 1) * 512)
        sch.op("act", lambda e: e.activation(out=sq[:], in_=src_ap, func=AF.Square), reads=[b_src], writes=[b_sq])
        sch.op("dve", lambda e: e.tensor_copy(out=sqh[:], in_=sq[:]), reads=[b_sq], writes=[b_sqh])
        sch.op("dve", lambda e: e.tensor_tensor(out=sql[:], in0=sq[:], in1=sqh[:], op=ALU.subtract),
               reads=[b_sq, b_sqh], writes=[b_sql])
        sch.group([lambda e: e.matmul(ssq[:], ones_bf[:], sqh[:], start=True, stop=False),
                   lambda e: e.matmul(ssq[:], ones_bf[:], sql[:], start=False, stop=True)],
                  reads=[b_sqh, b_sql, b_const], writes=[b_ssq])
        sch.op("act", lambda e: e.activation(out=lnv[:], in_=ssq[:], func=AF.Ln, scale=1.0 / DH, bias=EPS),
               reads=[b_ssq], writes=[b_lnv])
        sch.op("act", lambda e: e.activation(out=rstd[:], in_=lnv[:], func=AF.Exp, scale=-0.5),
               reads=[b_lnv], writes=[b_rstd])
        sch.op("dve", lambda e: e.scalar_tensor_tensor(out=mixT[:, h, qs], in0=src_ap, scalar=og[:, h:h + 1],
                                                       in1=rstd[:], op0=ALU.mult, op1=ALU.mult),
               reads=[b_src, b_rstd, b_const], writes=[b_mix[h][Qt]])

    wbs, b_wbs = [wb, pp[0]], [b_wb, b_pp[0]]
    sbks, b_sbks = [sbk, pp[1]], [b_sbk, b_pp[1]]

    def _steps():
        steps = []
        for Qt in range(4):
            kbs = list(range(4 * Qt + 3, -1, -1))
            for idx, kb in enumerate(kbs):
                zi = ctr["z"] % 2
                ctr["z"] += 1
                steps.append((Qt, idx, kb, len(kbs), zi))
        return steps

    def _pipeline(steps, s1, s2):
        s1(*steps[0])
        for i in range(len(steps)):
            if i + 1 < len(steps):
                s1(*steps[i + 1])
            s2(*steps[i])

    def attn_sb(h):
        def s1(Qt, idx, kb, n, zi):
            qs = slice(Qt * 512, (Qt + 1) * 512)
            ks = slice(kb * 128, (kb + 1) * 128)
            d0 = 4 * Qt - kb
            wbz, b_wbz, sbz, b_sbz = wbs[zi], b_wbs[zi], sbks[zi], b_sbks[zi]
            sch.group([lambda e: e.matmul(zb[zi][:], KT[:, ks], QT[:, qs], start=True, stop=True)],
                      reads=[b_KT[kb // 4], b_QT[Qt]], writes=[b_zb[zi]])
            sch.op("act", lambda e: e.activation(out=e_t[zi][:], in_=zb[zi][:], func=AF.Exp, scale=SCALE),
                   reads=[b_zb[zi]], writes=[b_e[zi]])
            sch.op("act", lambda e: e.activation(out=sp_t[zi][:], in_=e_t[zi][:], func=AF.Ln, bias=1.0),
                   reads=[b_e[zi]], writes=[b_sp[zi]])
            if d0 <= 0:
                ms = slice(128 * (d0 + 3), 128 * (d0 + 3) + 512)
                sch.op("pool", lambda e: e.tensor_tensor(out=sp_t[zi][:], in0=sp_t[zi][:], in1=sbm[:, ms], op=ALU.mult),
                       reads=[b_sp[zi], b_const], writes=[b_sp[zi]])
            sch.group([lambda e: e.matmul(wbz[:], tge[:], sp_t[zi][:], start=True, stop=True)],
                      reads=[b_sp[zi], b_const], writes=[b_wbz])
            if kb > 0:
                sch.group([lambda e: e.matmul(sbz[:], ones_bf[:], sp_t[zi][:], start=True, stop=True)],
                          reads=[b_sp[zi], b_const], writes=[b_sbz])

        def s2(Qt, idx, kb, n, zi):
            oi = Qt % 2
            d0 = 4 * Qt - kb
            wbz, b_wbz, sbz, b_sbz = wbs[zi], b_wbs[zi], sbks[zi], b_sbks[zi]
            if idx == 0:
                sch.op("dve", lambda e: e.tensor_scalar(out=t_t[zi][:], in0=zb[zi][:], scalar1=SCALE, scalar2=None,
                                                        op0=ALU.mult),
                       reads=[b_zb[zi]], writes=[b_t[zi]])
            else:
                sch.op("dve", lambda e: e.scalar_tensor_tensor(out=t_t[zi][:], in0=zb[zi][:], scalar=SCALE, in1=csb[:],
                                                               op0=ALU.mult, op1=ALU.subtract),
                       reads=[b_zb[zi], b_csb], writes=[b_t[zi]])
            sch.op("dve", lambda e: e.scalar_tensor_tensor(out=t_t[zi][:], in0=wbz[:], scalar=-1.0, in1=t_t[zi][:],
                                                           op0=ALU.mult, op1=ALU.add),
                   reads=[b_t[zi], b_wbz], writes=[b_t[zi]])
            if kb > 0:
                if idx == 0:
                    sch.op("dve", lambda e: e.tensor_copy(out=csb[:], in_=sbz[:]), reads=[b_sbz], writes=[b_csb])
                else:
                    sch.op("dve", lambda e: e.tensor_tensor(out=csb[:], in0=sbz[:], in1=csb[:], op=ALU.add),
                           reads=[b_csb, b_sbz], writes=[b_csb])
            sch.op("act", lambda e: e.activation(out=a_t[zi][:], in_=t_t[zi][:], func=AF.Exp),
                   reads=[b_t[zi]], writes=[b_a[zi]])
            if d0 <= 0:
                ms = slice(128 * (d0 + 3), 128 * (d0 + 3) + 512)
                sch.op("pool", lambda e: e.tensor_tensor(out=a_t[zi][:], in0=a_t[zi][:], in1=sbm[:, ms], op=ALU.mult),
                       reads=[b_a[zi], b_const], writes=[b_a[zi]])
            sch.group([lambda e: e.matmul(ob[oi][:], Vt[:, kb, :], a_t[zi][:], start=(idx == 0), stop=(idx == n - 1))],
                      reads=[b_a[zi], b_V[kb // 4]], writes=[b_ob[oi]])
            if idx == n - 1:
                head_norm(h, Qt, ob[oi][:], b_ob[oi], ob[1 - oi], b_ob[1 - oi])

        _pipeline(_steps(), s1, s2)

    dens, b_dens = [sbk, pp[0]], [b_sbk, b_pp[0]]

    def attn_dl(h):
        def s1(Qt, idx, kb, n, zi):
            qs = slice(Qt * 512, (Qt + 1) * 512)
            ks = slice(kb * 128, (kb + 1) * 128)
            sch.group([lambda e: e.matmul(zb[zi][:], KT[:, ks], QT[:, qs], start=True, stop=True)],
                      reads=[b_KT[kb // 4], b_QT[Qt]], writes=[b_zb[zi]])
            sch.op("act", lambda e: e.activation(out=e_t[zi][:], in_=zb[zi][:], func=AF.Exp, scale=SCALE),
                   reads=[b_zb[zi]], writes=[b_e[zi]])

        def s2(Qt, idx, kb, n, zi):
            oi = Qt % 2
            dn, b_dn = dens[oi], b_dens[oi]
            d0 = 4 * Qt - kb
            ms = slice(128 * (d0 + 3), 128 * (d0 + 3) + 512)
            meng = "dve" if (idx % 2 == 0) else "pool"
            sch.op(meng, lambda e: e.tensor_tensor(out=a_t[zi][:], in0=e_t[zi][:], in1=dlm[:, ms], op=ALU.mult),
                   reads=[b_e[zi], b_const], writes=[b_a[zi]])
            sch.group([lambda e: e.matmul(ob[oi][:], Vt[:, kb, :], a_t[zi][:], start=(idx == 0), stop=(idx == n - 1))],
                      reads=[b_a[zi], b_V[kb // 4]], writes=[b_ob[oi]])
            sch.group([lambda e: e.matmul(dn[:], ones_bf[:], a_t[zi][:], start=(idx == 0), stop=(idx == n - 1))],
                      reads=[b_a[zi], b_const], writes=[b_dn])
            if idx == n - 1:
                sch.op("dve", lambda e: e.reciprocal(out=rden[:], in_=dn[:]), reads=[b_dn], writes=[b_rden])
                sch.op("dve", lambda e: e.tensor_tensor(out=ot[:], in0=ob[oi][:], in1=rden[:], op=ALU.mult),
                       reads=[b_ob[oi], b_rden], writes=[b_ot])
                head_norm(h, Qt, ot[:], b_ot)

        _pipeline(_steps(), s1, s2)

    heads = {"pA1": [0, 8], "pAs": [0], "pAd": [8], "pAp": [0]}.get(stop_after, list(range(NH)))
    for h in heads:
        wi = 3 * h
        for j in range(3):
            load_w(wi + j)
        if h < 8:
            proj_fm(wi, evac_plain(QT, b_QT, "act"))
            load_w(wi + 4)
            proj_fm(wi + 1, evac_plain(KT, b_KT, "dve"))
            load_w(wi + 5)
            proj_v(wi + 2)
            load_w(wi + 6)
            if stop_after != "pAp":
                attn_sb(h)
        else:
            proj_fm(wi, evac_rope(QT, b_QT))
            load_w(wi + 4)
            proj_fm(wi + 1, evac_rope(KT, b_KT))
            load_w(wi + 5)
            proj_v(wi + 2)
            load_w(wi + 6)
            attn_dl(h)

    if stop_after in ("pA", "pA1", "pAs", "pAd", "pAp"):
        dump("mixT", mixT[:], [b for row in b_mix for b in row])
        return finish(es, es_h)
    sch.flush()
    es.close()
    es_h.close()

    es = ExitStack()
    x1T = SB(es, "x1T", [128, 16, 512], F32)
    accT = SB(es, "accT", [128, 16, 512], F32)
    h2T = SB(es, "h2T", [128, 16, 512], BF)
    b_x1 = [Buf("x1_%d" % i) for i in range(16)]
    b_acc = [Buf("acc_%d" % i) for i in range(16)]
    b_h2 = [Buf("h2_%d" % i) for i in range(16)]
    NWT = 4
    wbt = [SB(es, "wbt%d" % i, [128, 16, 128], BF) for i in range(NWT)]
    b_wt = [Buf("wt%d" % i) for i in range(NWT)]
    wtslot = [sch.slot() for _ in range(NWT)]
    NWD = 4
    wdb = [SB(es, "wdb%d" % i, [128, D], BF) for i in range(NWD)]
    b_wd = [Buf("wd%d" % i) for i in range(NWD)]
    wdslot = [sch.slot() for _ in range(NWD)]
    yT = [SB(es, "yT%d" % i, [128, 512], BF) for i in range(4)]
    b_y = [Buf("y%d" % i) for i in range(4)]
    g1 = SB(es, "g1", [128, 16], F32)
    g2 = SB(es, "g2", [128, 16], F32)
    g3 = SB(es, "g3", [128, 16], F32)
    cw = SB(es, "cw", [128, 3, 86], F32)
    cb = SB(es, "cb", [128, 86], F32)
    carry = [SB(es, "carry%d" % i, [128, 86, 2], F32) for i in range(2)]
    b_carry = [[Buf("carry%d_%d" % (i, c)) for c in range(86)] for i in range(2)]
    ld_const(g1[:], gpost_d[:], eng="sp")
    ld_const(g2[:], gpre2_d[:], eng="sp")
    ld_const(g3[:], gpost2_d[:], eng="sp")
    ld_const(cw[:], cw_d[:], eng="sp")
    ld_const(cb[:], cb_d[:], eng="sp")
    xs = [SB(es, "xs%d" % i, [128, D], F32) for i in range(1)]
    b_xs = [Buf("xs%d" % i) for i in range(1)]
    xsslot = [sch.slot() for _ in range(1)]
    ta, b_ta = [], []
    for i in range(3):
        ta.append(SB(es, "ta%d" % i, [128, 512], F32))
        b_ta.append(Buf("ta%d" % i))
    gl = SB(es, "gl", [128, 512], F32)
    b_gl = Buf("gl")
    sqb = [SB(es, "sqb%d" % i, [128, 512], BF) for i in range(2)]
    b_sqb = [Buf("sqb%d" % i) for i in range(2)]
    rs = SB(es, "rs", [128, 512], F32)
    b_rs = Buf("rs")
    PSug = [PS(es, "PSug%d" % i, [128, 512], F32) for i in range(2)]
    b_ug = [Buf("PSug%d" % i) for i in range(2)]
    PSuv = [PS(es, "PSuv%d" % i, [128, 512], F32) for i in range(2)]
    b_uv = [Buf("PSuv%d" % i) for i in range(2)]
    PSdp = [PS(es, "PSdp%d" % i, [128, 512], F32) for i in range(2)]
    b_dp = [Buf("PSdp%d" % i) for i in range(2)]
    PSss = PS(es, "PSss", [128, 512], F32)
    b_ss = Buf("PSss")
    PStr = PS(es, "PStr", [128, 512], F32)
    b_tr = Buf("PStr")
    tc = dict(w=0, wd=0, dp=0, ta=0, sq=0, u=0, xs=0)

    def load_wt(src):
        k = tc["w"] % NWT
        tc["w"] += 1
        sch.dma("pool", lambda e, inc, k=k, src=src: inc(e.dma_start(out=wbt[k][:], in_=src)), wtslot[k],
                writes=[b_wt[k]])
        return k

    def load_wd(fc):
        k = tc["wd"] % NWD
        tc["wd"] += 1
        src = w_down_d[fc * 128:(fc + 1) * 128, :]
        sch.dma("pool", lambda e, inc, k=k, src=src: inc(e.dma_start(out=wdb[k][:], in_=src)), wdslot[k],
                writes=[b_wd[k]])
        return k

    def wo_src(dc):
        return w_out_d[:, dc * 128:(dc + 1) * 128].rearrange("(c p) e -> p c e", p=128)

    def wu_src(col0):
        return w_up_d[:, col0:col0 + 128].rearrange("(c p) e -> p c e", p=128)

    def norm_rstd(n):
        sch.op("act", lambda e: e.activation(out=rs[:], in_=PSss[:], func=AF.Ln, scale=1.0 / n, bias=EPS),
               reads=[b_ss], writes=[b_rs])
        sch.op("act", lambda e: e.activation(out=rs[:], in_=rs[:], func=AF.Exp, scale=-0.5), reads=[b_rs],
               writes=[b_rs])

    def sumsq(src_ap, b_src, dc):
        i = tc["sq"] % 2
        tc["sq"] += 1
        sch.op("act", lambda e, i=i: e.activation(out=sqb[i][:], in_=src_ap, func=AF.Square), reads=[b_src],
               writes=[b_sqb[i]])
        sch.group([lambda e, i=i, dc=dc: e.matmul(PSss[:], ones_bf[:], sqb[i][:], start=(dc == 0), stop=(dc == 15))],
                  reads=[b_sqb[i], b_const], writes=[b_ss])

    ntiles = 4 if stop_after not in ("pT1", "pTx") else 1
    for tt in range(ntiles):
        ts = slice(tt * 512, (tt + 1) * 512)
        for j in range(4):
            i = 0
            tc["xs"] += 1
            r0 = tt * 512 + j * 128
            sch.dma("sp", lambda e, inc, i=i, r0=r0: inc(e.dma_start(out=xs[i][:], in_=x_d[r0:r0 + 128, :])), xsslot[i],
                    writes=[b_xs[i]])
            for q4 in range(4):
                fns = []
                for dd in range(4):
                    dc = q4 * 4 + dd
                    fns.append(lambda e, i=i, dc=dc, dd=dd: e.matmul(PStr[:, dd * 128:(dd + 1) * 128],
                                                                    xs[i][:, dc * 128:(dc + 1) * 128], ident_f[:],
                                                                    start=True, stop=True))
                sch.group(fns, reads=[b_xs[i], b_const], writes=[b_tr])
                dst = x1T[:, q4 * 4:q4 * 4 + 4, j * 128:(j + 1) * 128]
                src = PStr[:].rearrange("p (a t) -> p a t", a=4)
                eng = "act" if q4 % 2 == 0 else "dve"
                if eng == "act":
                    sch.op("act", lambda e, dst=dst, src=src: e.copy(out=dst, in_=src), reads=[b_tr],
                           writes=b_x1[q4 * 4:q4 * 4 + 4])
                else:
                    sch.op("dve", lambda e, dst=dst, src=src: e.tensor_copy(out=dst, in_=src), reads=[b_tr],
                           writes=b_x1[q4 * 4:q4 * 4 + 4])
        wks = {}
        for dc in range(2):
            wks[dc] = load_wt(wo_src(dc))
        for dc in range(16):
            if dc + 2 < 16:
                wks[dc + 2] = load_wt(wo_src(dc + 2))
            k = wks[dc]
            pb = tc["dp"] % 2
            tc["dp"] += 1
            fns = []
            for c in range(16):
                fns.append(lambda e, pb=pb, k=k, c=c, ts=ts: e.matmul(PSdp[pb][:], wbt[k][:, c, :], mixT[:, c, ts],
                                                                     start=(c == 0), stop=(c == 15)))
            sch.group(fns, reads=[b_wt[k]] + [b_mix[h][tt] for h in range(16)], writes=[b_dp[pb]])
            sch.op("act", lambda e, pb=pb, dc=dc: e.copy(out=accT[:, dc, :], in_=PSdp[pb][:]), reads=[b_dp[pb]],
                   writes=[b_acc[dc]])
            sumsq(accT[:, dc, :], b_acc[dc], dc)
        norm_rstd(D)
        for dc in range(16):
            i = tc["ta"] % 3
            tc["ta"] += 1
            sch.op("dve", lambda e, i=i, dc=dc: e.scalar_tensor_tensor(out=ta[i][:], in0=accT[:, dc, :],
                                                                      scalar=g1[:, dc:dc + 1], in1=rs[:],
                                                                      op0=ALU.mult, op1=ALU.mult),
                   reads=[b_acc[dc], b_rs, b_const], writes=[b_ta[i]])
            sch.op("pool", lambda e, i=i, dc=dc: e.tensor_tensor(out=x1T[:, dc, :], in0=ta[i][:], in1=x1T[:, dc, :],
                                                                op=ALU.add),
                   reads=[b_ta[i], b_x1[dc]], writes=[b_x1[dc]])
        for dc in range(16):
            sumsq(x1T[:, dc, :], b_x1[dc], dc)
        norm_rstd(D)
        for dc in range(16):
            sch.op("dve", lambda e, dc=dc: e.scalar_tensor_tensor(out=h2T[:, dc, :], in0=x1T[:, dc, :],
                                                                 scalar=g2[:, dc:dc + 1], in1=rs[:],
                                                                 op0=ALU.mult, op1=ALU.mult),
                   reads=[b_x1[dc], b_rs, b_const], writes=[b_h2[dc]])
        if stop_after == "pTx":
            dump("x1T", x1T[:], b_x1)
            dump("h2T", h2T[:], b_h2)
            return finish(es)
        cpar = tt % 2
        groups = [list(range(g0, min(g0 + 4, NFC))) for g0 in range(0, NFC, 4)]
        pre = {}

        def prefetch(i):
            if i < NFC and i not in pre:
                pre[i] = (load_wt(wu_src(i * 128)), load_wt(wu_src(DFF + i * 128)))

        prefetch(0)
        prefetch(1)
        for gi, grp in enumerate(groups):
            wdk = {}
            ys = {}
            for i in grp:
                kg, kv = pre[i]
                ui = tc["u"] % 2
                tc["u"] += 1
                A = {}
                for (which, kk, PSu, b_u, ch) in (("g", kg, PSug, b_ug, i), ("v", kv, PSuv, b_uv, NFC + i)):
                    fns = []
                    for c in range(16):
                        fns.append(lambda e, ui=ui, kk=kk, c=c, PSu=PSu: e.matmul(
                            PSu[ui][:], wbt[kk][:, c, :], h2T[:, c, :], start=(c == 0), stop=(c == 15)))
                    sch.group(fns, reads=[b_wt[kk]] + b_h2, writes=[b_u[ui]])
                    ai = tc["ta"] % 3
                    tc["ta"] += 1
                    A[which] = ai
                    u = PSu[ui]
                    sch.op("act", lambda e, ai=ai, u=u, ch=ch: e.activation(out=ta[ai][:], in_=u[:], func=AF.Identity,
                                                                            scale=cw[:, 2, ch:ch + 1],
                                                                            bias=cb[:, ch:ch + 1]),
                           reads=[b_u[ui], b_const], writes=[b_ta[ai]])
                    sch.op("dve", lambda e, ai=ai, u=u, ch=ch: e.scalar_tensor_tensor(
                        out=ta[ai][:, 1:512], in0=u[:, 0:511], scalar=cw[:, 1, ch:ch + 1], in1=ta[ai][:, 1:512],
                        op0=ALU.mult, op1=ALU.add), reads=[b_u[ui], b_ta[ai], b_const], writes=[b_ta[ai]])
                    sch.op("dve", lambda e, ai=ai, u=u, ch=ch: e.scalar_tensor_tensor(
                        out=ta[ai][:, 2:512], in0=u[:, 0:510], scalar=cw[:, 0, ch:ch + 1], in1=ta[ai][:, 2:512],
                        op0=ALU.mult, op1=ALU.add), reads=[b_u[ui], b_ta[ai], b_const], writes=[b_ta[ai]])
                    if tt > 0:
                        pc_ = carry[1 - cpar]
                        sch.op("dve", lambda e, ai=ai, pc_=pc_, ch=ch: e.scalar_tensor_tensor(
                            out=ta[ai][:, 0:2], in0=pc_[:, ch, 0:2], scalar=cw[:, 0, ch:ch + 1], in1=ta[ai][:, 0:2],
                            op0=ALU.mult, op1=ALU.add), reads=[b_carry[1 - cpar][ch], b_ta[ai], b_const],
                            writes=[b_ta[ai]])
                        sch.op("dve", lambda e, ai=ai, pc_=pc_, ch=ch: e.scalar_tensor_tensor(
                            out=ta[ai][:, 0:1], in0=pc_[:, ch, 1:2], scalar=cw[:, 1, ch:ch + 1], in1=ta[ai][:, 0:1],
                            op0=ALU.mult, op1=ALU.add), reads=[b_carry[1 - cpar][ch], b_ta[ai], b_const],
                            writes=[b_ta[ai]])
                    if tt < ntiles - 1:
                        sch.op("dve", lambda e, u=u, ch=ch, cdst=carry[cpar]: e.tensor_copy(out=cdst[:, ch, :],
                                                                                            in_=u[:, 510:512]),
                               reads=[b_u[ui]], writes=[b_carry[cpar][ch]])
                prefetch(i + 2)
                sch.op("act", lambda e, ag=A["g"]: e.activation(out=gl[:], in_=ta[ag][:], func=AF.Gelu_apprx_tanh),
                       reads=[b_ta[A["g"]]], writes=[b_gl])
                yi = i % 4
                ys[i] = yi
                sch.op("pool", lambda e, yi=yi, av=A["v"]: e.tensor_tensor(out=yT[yi][:], in0=gl[:], in1=ta[av][:],
                                                                          op=ALU.mult),
                       reads=[b_gl, b_ta[A["v"]]], writes=[b_y[yi]])
                wdk[i] = load_wd(i)
            for dc in range(16):
                pb = tc["dp"] % 2
                tc["dp"] += 1
                fns = []
                for n_, i in enumerate(grp):
                    fns.append(lambda e, pb=pb, wk=wdk[i], yk=ys[i], dc=dc, n_=n_, L_=len(grp): e.matmul(
                        PSdp[pb][:], wdb[wk][:, dc * 128:(dc + 1) * 128], yT[yk][:], start=(n_ == 0),
                        stop=(n_ == L_ - 1)))
                sch.group(fns, reads=[b_wd[wdk[i]] for i in grp] + [b_y[ys[i]] for i in grp], writes=[b_dp[pb]])
                if gi == 0:
                    sch.op("dve", lambda e, pb=pb, dc=dc: e.tensor_copy(out=accT[:, dc, :], in_=PSdp[pb][:]),
                           reads=[b_dp[pb]], writes=[b_acc[dc]])
                else:
                    sch.op("dve", lambda e, pb=pb, dc=dc: e.tensor_tensor(out=accT[:, dc, :], in0=PSdp[pb][:],
                                                                         in1=accT[:, dc, :], op=ALU.add),
                           reads=[b_dp[pb], b_acc[dc]], writes=[b_acc[dc]])
        if stop_after == "pT1":
            dump("fT", accT[:], b_acc)
        for dc in range(16):
            sumsq(accT[:, dc, :], b_acc[dc], dc)
        norm_rstd(D)
        for dc in range(16):
            i = tc["ta"] % 3
            tc["ta"] += 1
            sch.op("dve", lambda e, i=i, dc=dc: e.scalar_tensor_tensor(out=ta[i][:], in0=accT[:, dc, :],
                                                                      scalar=g3[:, dc:dc + 1], in1=rs[:],
                                                                      op0=ALU.mult, op1=ALU.mult),
                   reads=[b_acc[dc], b_rs, b_const], writes=[b_ta[i]])
            sch.op("pool", lambda e, i=i, dc=dc: e.tensor_tensor(out=accT[:, dc, :], in0=ta[i][:], in1=x1T[:, dc, :],
                                                                op=ALU.add),
                   reads=[b_ta[i], b_x1[dc]], writes=[b_acc[dc]])
        for j in range(4):
            i = 0
            tc["xs"] += 1
            r0 = tt * 512 + j * 128
            for q4 in range(4):
                fns = []
                for dd in range(4):
                    dc = q4 * 4 + dd
                    fns.append(lambda e, dc=dc, dd=dd, j=j: e.matmul(PStr[:, dd * 128:(dd + 1) * 128],
                                                                    accT[:, dc, j * 128:(j + 1) * 128], ident_f[:],
                                                                    start=True, stop=True))
                sch.group(fns, reads=b_acc[q4 * 4:q4 * 4 + 4] + [b_const], writes=[b_tr])
                dst = xs[i][:, q4 * 512:(q4 + 1) * 512]
                if q4 % 2 == 0:
                    sch.op("act", lambda e, dst=dst: e.copy(out=dst, in_=PStr[:]), reads=[b_tr], writes=[b_xs[i]])
                else:
                    sch.op("dve", lambda e, dst=dst: e.tensor_copy(out=dst, in_=PStr[:]), reads=[b_tr], writes=[b_xs[i]])
            sch.dma("sp", lambda e, inc, i=i, r0=r0: inc(e.dma_start(out=out_d[r0:r0 + 128, :], in_=xs[i][:])), oslot,
                    reads=[b_xs[i]])
    return finish(es)


def make_in_maps(inputs, cores=None):
    f32 = np.float32
    cores = list(range(NCORES)) if cores is None else cores
    c = _consts()
    L = 0

    def pc(v):
        v = np.asarray(v, f32)
        return np.ascontiguousarray(v.reshape(-1, 128).T)

    shared = dict(
        w_in=np.ascontiguousarray(inputs["w_in"][L], f32),
        w_out=np.ascontiguousarray(inputs["w_out"][L], f32),
        w_up=np.ascontiguousarray(inputs["w_up"][L], f32),
        w_down=np.ascontiguousarray(inputs["w_down"][L], f32),
        g_pre_mix=np.ascontiguousarray(np.broadcast_to(np.asarray(inputs["pre_mix_gain"][L], f32)[None, :], (128, D))),
        g_post_mix=pc(inputs["post_mix_gain"][L]),
        g_pre_ffn=pc(inputs["pre_ffn_gain"][L]),
        g_post_ffn=pc(inputs["post_ffn_gain"][L]),
        g_heads=pc(np.concatenate([np.asarray(inputs["sb_out_gain"][L]), np.asarray(inputs["dil_out_gain"][L])])),
        conv_w=np.ascontiguousarray(np.asarray(inputs["conv_w"][L], f32).reshape(3, 86, 128).transpose(2, 0, 1)),
        conv_b=pc(inputs["conv_b"][L]),
    )
    shared.update(c)
    maps = []
    for b in cores:
        m = dict(shared)
        m["x"] = np.ascontiguousarray(inputs["x"][b], f32)
        maps.append(m)
    return maps


def kernel(**inputs):
    nc = build()
    maps = make_in_maps(inputs)
    res = run_bass_kernel_spmd(nc, maps, core_ids=list(range(NCORES)))
    return np.stack([np.asarray(r["out"], np.float32) for r in res.results], axis=0)
```

```python
import numpy as np
import os
LVL = int(os.environ.get('ATT_LVL', '9'))
SUB = int(os.environ.get('ATT_SUB', '9'))
from contextlib import ExitStack
import concourse.bass as bass
import concourse.mybir as mybir
from concourse.bass_utils import run_bass_kernel_spmd

F32 = mybir.dt.float32
BF = mybir.dt.bfloat16
AF = mybir.ActivationFunctionType
ALU = mybir.AluOpType

S = 2048
D = 2048
NH = 16
DH = 128
DFF = 5504
NFC = DFF // 128
QKV = 6144
EPS = 1e-6
SCALE = DH ** -0.5
NCORES = 8
LAST_COUNTS = {}


class Buf:
    __slots__ = ("name", "w", "r")

    def __init__(self, name):
        self.name = name
        self.w = None
        self.r = []


class Slot:
    def __init__(self, sem, key):
        self.sem = sem
        self.key = key
        self.count = 0


class Sched:
    ENGS = ("pe", "act", "dve", "pool", "sp")

    def __init__(self, nc, es):
        self.nc = nc
        self.q = {e: [] for e in self.ENGS}
        self.cnt = {e: 0 for e in self.ENGS}
        self.seen = {e: {} for e in self.ENGS}
        self.sems = {}
        for e in self.ENGS:
            self.sems[e] = es.enter_context(nc.semaphore("sem_" + e))
        self.es = es
        self.nslots = 0

    def slot(self):
        key = "dma%d" % self.nslots
        self.nslots += 1
        sem = self.es.enter_context(self.nc.semaphore("sem_" + key))
        self.sems[key] = sem
        return Slot(sem, key)

    def _deps(self, eng, reads, writes):
        deps = []
        for b in reads:
            if b.w is not None:
                deps.append(b.w)
            if b.name.startswith("zb") or b.name.startswith("ob") or b.name.startswith("wb") or b.name.startswith("sbk") \
                    or b.name.startswith("pp") or b.name.startswith("ptr") or b.name.startswith("PS"):
                deps.extend(t for t in b.r if t[0] != eng)
        for b in writes:
            if b.w is not None:
                deps.append(b.w)
            deps.extend(b.r)
        out = {}
        for (k, v) in deps:
            if k == "pe" and eng == "pe":
                continue
            if v > out.get(k, 0):
                out[k] = v
        res = []
        for k, v in out.items():
            if v > self.seen[eng].get(k, 0):
                self.seen[eng][k] = v
                res.append((k, v))
        return res

    def _emit_waits(self, eng, waits):
        for (k, v) in waits:
            sem = self.sems[k]
            self.q[eng].append(lambda e, sem=sem, v=v: e.wait_ge(sem, v))

    def _mark(self, ticket, reads, writes):
        for b in writes:
            b.w = ticket
            b.r = []
        for b in reads:
            b.r.append(ticket)

    def op(self, eng, fn, reads=(), writes=()):
        waits = self._deps(eng, reads, writes)
        self._emit_waits(eng, waits)
        self.cnt[eng] += 1
        sem = self.sems[eng]
        self.q[eng].append(lambda e, fn=fn, sem=sem: fn(e).then_inc(sem, 1))
        t = (eng, self.cnt[eng])
        self._mark(t, reads, writes)
        return t

    def group(self, fns, reads=(), writes=()):
        eng = "pe"
        waits = self._deps(eng, reads, writes)
        self._emit_waits(eng, waits)
        self.cnt[eng] += 1
        sem = self.sems[eng]
        n = len(fns)
        for i, fn in enumerate(fns):
            if i == n - 1:
                self.q[eng].append(lambda e, fn=fn, sem=sem: fn(e).then_inc(sem, 1))
            else:
                self.q[eng].append(lambda e, fn=fn: fn(e))
        t = (eng, self.cnt[eng])
        self._mark(t, reads, writes)
        return t

    def dma(self, eng, fn, slot, reads=(), writes=(), n=1):
        waits = self._deps(eng, reads, writes)
        self._emit_waits(eng, waits)
        slot.count += 16 * n
        sem = slot.sem
        self.q[eng].append(lambda e, fn=fn, sem=sem: fn(e, lambda ins: ins.then_inc(sem, 16)))
        t = (slot.key, slot.count)
        self._mark(t, reads, writes)
        return t

    def wait_all(self, eng, tickets):
        waits = []
        for (k, v) in tickets:
            if v > self.seen[eng].get(k, 0):
                self.seen[eng][k] = v
                waits.append((k, v))
        self._emit_waits(eng, waits)

    def barrier(self, extra=()):
        tickets = [(e, self.cnt[e]) for e in self.ENGS if self.cnt[e] > 0] + list(extra)
        for eng in self.ENGS:
            self.wait_all(eng, [t for t in tickets if not (t[0] == eng and eng == "pe")])

    def flush(self):
        nc = self.nc
        q = self.q
        with nc.Block() as block:
            @block.tensor
            def _(e):
                for f in q["pe"]:
                    f(e)

            @block.scalar
            def _(e):
                for f in q["act"]:
                    f(e)

            @block.vector
            def _(e):
                for f in q["dve"]:
                    f(e)

            @block.gpsimd
            def _(e):
                for f in q["pool"]:
                    f(e)

            @block.sync
            def _(e):
                for f in q["sp"]:
                    f(e)
        self.q = {e: [] for e in self.ENGS}


def _consts():
    f32 = np.float32
    kl = np.arange(128)[:, None]
    x = np.arange(19 * 128)[None, :]
    dl = x - 384 - kl
    c = ((dl >= 0) & (dl <= 128)).astype(f32)
    c += ((dl >= 0) & (dl % 4 == 0) & (dl <= 512)).astype(f32)
    c += ((dl >= 0) & (dl % 16 == 0) & (dl <= 2048)).astype(f32)
    x7 = np.arange(7 * 128)[None, :]
    sbm = ((x7 - 384 - kl) > 0).astype(f32)
    j = np.arange(128)[:, None]
    s = np.arange(128)[None, :]
    tge = (j >= s).astype(f32)
    ident = np.eye(128, dtype=f32)
    ones = np.ones((128, 128), f32)
    inv_freq = (np.float32(10000.0) ** (-np.arange(0, 128, 2, dtype=f32) / np.float32(128))).astype(f32)
    ang = (np.arange(S, dtype=f32)[:, None] * inv_freq[None, :]).astype(f32)
    cos = np.cos(ang).astype(f32).T
    sin = np.sin(ang).astype(f32).T
    cosT = np.concatenate([cos, cos], axis=0)
    sinS = np.concatenate([-sin, sin], axis=0)
    return dict(c_dlm=c, c_sbm=sbm, c_tge=tge, c_ident=ident, c_ones=ones,
                c_cos=np.ascontiguousarray(cosT), c_sin=np.ascontiguousarray(sinS))


def build(dbg=None, stop_after=None):
    nc = bass.Bass("TRN2", target_bir_lowering=False)
    es0 = ExitStack()

    def din(name, shape, dt=F32):
        return nc.dram_tensor(name, list(shape), dt, kind="ExternalInput").ap()

    x_d = din("x", [S, D])
    w_in_d = din("w_in", [D, QKV])
    w_out_d = din("w_out", [D, D])
    w_up_d = din("w_up", [D, 2 * DFF])
    w_down_d = din("w_down", [DFF, D])
    gpre_d = din("g_pre_mix", [128, D])
    gpost_d = din("g_post_mix", [128, 16])
    gpre2_d = din("g_pre_ffn", [128, 16])
    gpost2_d = din("g_post_ffn", [128, 16])
    og_d = din("g_heads", [128, 16])
    cw_d = din("conv_w", [128, 3, 86])
    cb_d = din("conv_b", [128, 86])
    c_dlm_d = din("c_dlm", [128, 19 * 128])
    c_sbm_d = din("c_sbm", [128, 7 * 128])
    c_tge_d = din("c_tge", [128, 128])
    c_ident_d = din("c_ident", [128, 128])
    c_ones_d = din("c_ones", [128, 128])
    c_cos_d = din("c_cos", [128, S])
    c_sin_d = din("c_sin", [128, S])
    out_d = nc.dram_tensor("out", [S, D], F32, kind="ExternalOutput").ap()
    dbg_d = {}
    if dbg:
        for name, (shape, dt) in dbg.items():
            dbg_d[name] = nc.dram_tensor("dbg_" + name, list(shape), dt, kind="ExternalOutput").ap()

    def SB(es, name, shape, dt):
        return es.enter_context(nc.sbuf_tensor(name, list(shape), dt))

    def PS(es, name, shape, dt):
        return es.enter_context(nc.psum_tensor(name, list(shape), dt))

    sch = Sched(nc, es0)

    mixT = SB(es0, "mixT", [128, 16, S], BF)
    ident_bf = SB(es0, "ident_bf", [128, 128], BF)
    ident_f = SB(es0, "ident_f", [128, 128], F32)
    ones_bf = SB(es0, "ones_bf", [128, 128], BF)
    ones_f = SB(es0, "ones_f", [128, 128], F32)
    b_hT = [Buf("hT%d" % i) for i in range(16)]
    b_mix = [[Buf("mix%d_%d" % (h, q)) for q in range(4)] for h in range(16)]
    b_const = Buf("const")
    cslot = sch.slot()
    es_h = ExitStack()
    hT = SB(es_h, "hT", [128, 16, S], BF)

    cslot_p = sch.slot()
    b_constp = Buf("constp")

    def ld_const(dst, src, eng="pool"):
        if eng == "pool":
            sch.dma(eng, lambda e, inc, dst=dst, src=src: inc(e.dma_start(out=dst, in_=src)), cslot_p,
                    writes=[b_constp])
        else:
            sch.dma(eng, lambda e, inc, dst=dst, src=src: inc(e.dma_start(out=dst, in_=src)), cslot,
                    writes=[b_const])

    def sync_pool_consts():
        for eng in ("pe", "act", "dve", "pool"):
            sch.wait_all(eng, [(cslot_p.key, cslot_p.count)])

    ld_const(ident_bf[:], c_ident_d[:])
    ld_const(ones_bf[:], c_ones_d[:])
    ld_const(ident_f[:], c_ident_d[:], eng="sp")
    ld_const(ones_f[:], c_ones_d[:], eng="sp")
    sync_pool_consts()

    es = ExitStack()
    gB = SB(es, "gB", [128, D], F32)
    ld_const(gB[:], gpre_d[:], eng="sp")
    xt = [SB(es, "xt%d" % i, [128, D], F32) for i in range(2)]
    b_xt = [Buf("xt%d" % i) for i in range(2)]
    xslot = [sch.slot() for _ in range(2)]
    junk = SB(es, "junk", [128, D], BF)
    b_junk = Buf("junk")
    stat = SB(es, "stat", [128, 16, 4], F32)
    b_stat = [Buf("stat%d" % i) for i in range(16)]
    hn = [SB(es, "hn%d" % i, [128, D], BF) for i in range(2)]
    b_hn = [Buf("hn%d" % i) for i in range(2)]
    ptr = [PS(es, "ptr%d" % i, [128, 1024], BF) for i in range(4)]
    b_ptr = [Buf("ptr%d" % i) for i in range(4)]

    for tb in range(16):
        i = tb % 2
        sch.dma("sp", lambda e, inc, i=i, tb=tb: inc(e.dma_start(out=xt[i][:], in_=x_d[tb * 128:(tb + 1) * 128, :])),
                xslot[i], writes=[b_xt[i]])
        sch.op("act", lambda e, i=i, tb=tb: e.activation(out=junk[:], in_=xt[i][:], func=AF.Square,
                                                          accum_out=stat[:, tb, 0:1]),
               reads=[b_xt[i]], writes=[b_junk, b_stat[tb]])
        sch.op("act", lambda e, tb=tb: e.activation(out=stat[:, tb, 1:2], in_=stat[:, tb, 0:1], func=AF.Ln,
                                                    scale=1.0 / D, bias=EPS),
               reads=[b_stat[tb]], writes=[b_stat[tb]])
        sch.op("act", lambda e, tb=tb: e.activation(out=stat[:, tb, 2:3], in_=stat[:, tb, 1:2], func=AF.Exp,
                                                    scale=-0.5),
               reads=[b_stat[tb]], writes=[b_stat[tb]])
        sch.op("dve", lambda e, i=i, tb=tb: e.scalar_tensor_tensor(out=hn[i][:], in0=xt[i][:], scalar=stat[:, tb, 2:3],
                                                                  in1=gB[:], op0=ALU.mult, op1=ALU.mult),
               reads=[b_xt[i], b_stat[tb], b_const], writes=[b_hn[i]])
        for half in range(2):
            pb = (tb * 2 + half) % 4
            fns = []
            for j in range(8):
                c = half * 8 + j
                fns.append(lambda e, pb=pb, j=j, c=c, i=i: e.transpose(ptr[pb][:, j * 128:(j + 1) * 128],
                                                                    hn[i][:, c * 128:(c + 1) * 128], ident_bf[:]))
            sch.group(fns, reads=[b_hn[i], b_const], writes=[b_ptr[pb]])
            dst = hT[:, half * 8:(half + 1) * 8, tb * 128:(tb + 1) * 128]
            src = ptr[pb][:].rearrange("p (c t) -> p c t", c=8)
            if half == 0:
                sch.op("act", lambda e, dst=dst, src=src: e.copy(out=dst, in_=src), reads=[b_ptr[pb]], writes=[b_hT[tb]])
            else:
                sch.op("dve", lambda e, dst=dst, src=src: e.tensor_copy(out=dst, in_=src), reads=[b_ptr[pb]],
                       writes=[b_hT[tb]])
    oslot = sch.slot()

    def finish(*inner):
        global LAST_COUNTS
        LAST_COUNTS = dict(sch.cnt)
        LAST_COUNTS["max_dma_slot"] = max([0] + [v for k, v in sch.seen["sp"].items() if k.startswith("dma")])
        sch.wait_all("sp", [(oslot.key, oslot.count)])
        sch.flush()
        for s_ in inner:
            s_.close()
        es0.close()
        return nc

    def dump(name, src_ap, bufs):
        sch.dma("sp", lambda e, inc: inc(e.dma_start(out=dbg_d[name][:], in_=src_ap)), oslot, reads=bufs)

    if stop_after == "p0":
        dump("hT", hT[:], b_hT)
        return finish(es, es_h)
    sch.flush()
    es.close()

    es = ExitStack()
    NW = 4
    wbuf = [SB(es, "wbuf%d" % i, [128, 16, 128], BF) for i in range(NW)]
    b_w = [Buf("w%d" % i) for i in range(NW)]
    wslot = [sch.slot() for _ in range(NW)]
    QT = SB(es, "QT", [128, S], BF)
    KT = SB(es, "KT", [128, S], BF)
    Vt = SB(es, "Vt", [128, 16, 128], BF)
    b_QT = [Buf("QT%d" % i) for i in range(4)]
    b_KT = [Buf("KT%d" % i) for i in range(4)]
    b_V = [Buf("V%d" % i) for i in range(4)]
    cosT = SB(es, "cosT", [128, S], F32)
    sinS = SB(es, "sinS", [128, S], F32)
    dlm = SB(es, "dlm", [128, 19 * 128], BF)
    sbm = SB(es, "sbm", [128, 7 * 128], BF)
    tge = SB(es, "tge", [128, 128], BF)
    og = SB(es, "og", [128, 16], F32)
    ld_const(cosT[:], c_cos_d[:], eng="sp")
    ld_const(sinS[:], c_sin_d[:], eng="sp")
    ld_const(og[:], og_d[:], eng="sp")
    ld_const(dlm[:], c_dlm_d[:])
    ld_const(sbm[:], c_sbm_d[:])
    ld_const(tge[:], c_tge_d[:])
    sync_pool_consts()

    def tmp(name, dt, n=2):
        ts = [SB(es, "%s%d" % (name, i), [128, 512], dt) for i in range(n)]
        return ts, [Buf("%s%d" % (name, i)) for i in range(n)]

    e_t, b_e = tmp("e_t", F32)
    sp_t, b_sp = tmp("sp_t", BF)
    t_t, b_t = tmp("t_t", F32)
    a_t, b_a = tmp("a_t", BF)
    csb_l, b_csb_l = tmp("csb", F32, 1)
    csb, b_csb = csb_l[0], b_csb_l[0]
    rr1, b_rr1 = e_t, b_e
    rr2, b_rr2 = t_t, b_t
    sq_l, b_sq_l = tmp("sq", F32, 1)
    sq, b_sq = sq_l[0], b_sq_l[0]
    rstd_l, b_rstd_l = tmp("rstd", F32, 1)
    rstd, b_rstd = rstd_l[0], b_rstd_l[0]
    lnv, b_lnv = rstd, b_rstd
    ot_l, b_ot_l = tmp("ot", F32, 1)
    ot, b_ot = ot_l[0], b_ot_l[0]
    rden, b_rden = sq, b_sq
    sqh_l, b_sqh_l = tmp("sqh", BF, 1)
    sqh, b_sqh = sqh_l[0], b_sqh_l[0]
    sql_l, b_sql_l = tmp("sql", BF, 1)
    sql, b_sql = sql_l[0], b_sql_l[0]

    pp = [PS(es, "pp%d" % i, [128, 512], F32) for i in range(2)]
    b_pp = [Buf("pp%d" % i) for i in range(2)]
    zb = [PS(es, "zb%d" % i, [128, 512], F32) for i in range(2)]
    b_zb = [Buf("zb%d" % i) for i in range(2)]
    wb = PS(es, "wb", [128, 512], F32)
    b_wb = Buf("wb")
    sbk = PS(es, "sbk", [128, 512], F32)
    b_sbk = Buf("sbk")
    ob = [PS(es, "ob%d" % i, [128, 512], F32) for i in range(2)]
    b_ob = [Buf("ob%d" % i) for i in range(2)]

    ctr = dict(w=0, p=0, z=0, t=0)

    def head_cols(h):
        if h < 8:
            return h * 128, 1024 + h * 128, 2048 + h * 128
        hh = h - 8
        return 3072 + hh * 128, 4096 + hh * 128, 5120 + hh * 128

    wq_list = []
    for h in range(NH):
        wq_list.extend(head_cols(h))
    loaded = {}

    def load_w(i):
        if i >= len(wq_list) or i in loaded:
            return
        k = i % NW
        col0 = wq_list[i]
        src = w_in_d[:, col0:col0 + 128].rearrange("(c p) e -> p c e", p=128)
        sch.dma("pool", lambda e, inc, k=k, src=src: inc(e.dma_start(out=wbuf[k][:], in_=src)), wslot[k],
                writes=[b_w[k]])
        loaded[i] = k

    for i in range(NW):
        load_w(i)

    def proj_fm(wi, evac):
        k = loaded[wi]
        for tt in range(4):
            pb = ctr["p"] % 2
            ctr["p"] += 1
            fns = []
            for c in range(16):
                fns.append(lambda e, pb=pb, k=k, c=c, tt=tt: e.matmul(
                    pp[pb][:], wbuf[k][:, c, :], hT[:, c, tt * 512:(tt + 1) * 512], start=(c == 0), stop=(c == 15)))
            sch.group(fns, reads=[b_w[k], b_const] + b_hT[4 * tt:4 * tt + 4], writes=[b_pp[pb]])
            evac(tt, pb)

    def evac_plain(dstT, b_dst, eng):
        def f(tt, pb):
            dst = dstT[:, tt * 512:(tt + 1) * 512]
            if eng == "act":
                sch.op("act", lambda e: e.copy(out=dst, in_=pp[pb][:]), reads=[b_pp[pb]], writes=[b_dst[tt]])
            else:
                sch.op("dve", lambda e: e.tensor_copy(out=dst, in_=pp[pb][:]), reads=[b_pp[pb]], writes=[b_dst[tt]])
        return f

    def evac_rope(dstT, b_dst):
        def f(tt, pb):
            i = ctr["t"] % 2
            ctr["t"] += 1
            ts = slice(tt * 512, (tt + 1) * 512)
            dst = dstT[:, ts]
            sch.op("dve", lambda e: e.tensor_tensor(out=rr1[i][:], in0=pp[pb][:], in1=cosT[:, ts], op=ALU.mult),
                   reads=[b_pp[pb], b_const], writes=[b_rr1[i]])
            sch.op("dve", lambda e: e.tensor_tensor(out=rr2[i][0:64, :], in0=pp[pb][64:128, :], in1=sinS[0:64, ts],
                                                    op=ALU.mult),
                   reads=[b_pp[pb], b_const], writes=[b_rr2[i]])
            sch.op("dve", lambda e: e.tensor_tensor(out=rr2[i][64:128, :], in0=pp[pb][0:64, :], in1=sinS[64:128, ts],
                                                    op=ALU.mult),
                   reads=[b_pp[pb], b_const], writes=[b_rr2[i]])
            sch.op("pool", lambda e: e.tensor_tensor(out=dst, in0=rr1[i][:], in1=rr2[i][:], op=ALU.add),
                   reads=[b_rr1[i], b_rr2[i]], writes=[b_dst[tt]])
        return f

    def proj_v(wi):
        k = loaded[wi]
        for g in range(4):
            pb = ctr["p"] % 2
            ctr["p"] += 1
            fns = []
            for tb in range(4 * g, 4 * g + 4):
                for c in range(16):
                    fns.append(lambda e, pb=pb, k=k, c=c, tb=tb: e.matmul(
                        pp[pb][:, (tb % 4) * 128:(tb % 4 + 1) * 128], hT[:, c, tb * 128:(tb + 1) * 128],
                        wbuf[k][:, c, :], start=(c == 0), stop=(c == 15)))
            sch.group(fns, reads=[b_w[k], b_const] + b_hT[4 * g:4 * g + 4], writes=[b_pp[pb]])
            dst = Vt[:, 4 * g:4 * g + 4, :]
            src = pp[pb][:].rearrange("p (a d) -> p a d", a=4)
            sch.op("act", lambda e, dst=dst, src=src: e.copy(out=dst, in_=src), reads=[b_pp[pb]], writes=[b_V[g]])

    def head_norm(h, Qt, src_ap, b_src, ssq=None, b_ssq=None):
        ssq = wb if ssq is None else ssq
        b_ssq = b_wb if b_ssq is None else b_ssq
        qs = slice(Qt * 512, (Qt + 1) * 512)
        sch.op("act", lambda e: e.activation(out=sq[:], in_=src_ap, func=AF.Square), reads=[b_src], writes=[b_sq])
        sch.op("dve", lambda e: e.tensor_copy(out=sqh[:], in_=sq[:]), reads=[b_sq], writes=[b_sqh])
        sch.op("dve", lambda e: e.tensor_tensor(out=sql[:], in0=sq[:], in1=sqh[:], op=ALU.subtract),
               reads=[b_sq, b_sqh], writes=[b_sql])
        sch.group([lambda e: e.matmul(ssq[:], ones_bf[:], sqh[:], start=True, stop=False),
                   lambda e: e.matmul(ssq[:], ones_bf[:], sql[:], start=False, stop=True)],
                  reads=[b_sqh, b_sql, b_const], writes=[b_ssq])
        sch.op("act", lambda e: e.activation(out=lnv[:], in_=ssq[:], func=AF.Ln, scale=1.0 / DH, bias=EPS),
               reads=[b_ssq], writes=[b_lnv])
        sch.op("act", lambda e: e.activation(out=rstd[:], in_=lnv[:], func=AF.Exp, scale=-0.5),
               reads=[b_lnv], writes=[b_rstd])
        sch.op("dve", lambda e: e.scalar_tensor_tensor(out=mixT[:, h, qs], in0=src_ap, scalar=og[:, h:h + 1],
                                                       in1=rstd[:], op0=ALU.mult, op1=ALU.mult),
               reads=[b_src, b_rstd, b_const], writes=[b_mix[h][Qt]])

    wbs, b_wbs = [wb, pp[0]], [b_wb, b_pp[0]]
    sbks, b_sbks = [sbk, pp[1]], [b_sbk, b_pp[1]]

    def _steps():
        steps = []
        for Qt in range(4):
            kbs = list(range(4 * Qt + 3, -1, -1))
            for idx, kb in enumerate(kbs):
                zi = ctr["z"] % 2
                ctr["z"] += 1
                steps.append((Qt, idx, kb, len(kbs), zi))
        return steps

    def _pipeline(steps, s1, s2):
        s1(*steps[0])
        for i in range(len(steps)):
            if i + 1 < len(steps):
                s1(*steps[i + 1])
            s2(*steps[i])

    def attn_sb(h):
        def s1(Qt, idx, kb, n, zi):
            qs = slice(Qt * 512, (Qt + 1) * 512)
            ks = slice(kb * 128, (kb + 1) * 128)
            d0 = 4 * Qt - kb
            wbz, b_wbz, sbz, b_sbz = wbs[zi], b_wbs[zi], sbks[zi], b_sbks[zi]
            sch.group([lambda e: e.matmul(zb[zi][:], KT[:, ks], QT[:, qs], start=True, stop=True)],
                      reads=[b_KT[kb // 4], b_QT[Qt]], writes=[b_zb[zi]])
            sch.op("act", lambda e: e.activation(out=e_t[zi][:], in_=zb[zi][:], func=AF.Exp, scale=SCALE),
                   reads=[b_zb[zi]], writes=[b_e[zi]])
            sch.op("act", lambda e: e.activation(out=sp_t[zi][:], in_=e_t[zi][:], func=AF.Ln, bias=1.0),
                   reads=[b_e[zi]], writes=[b_sp[zi]])
            if d0 <= 0:
                ms = slice(128 * (d0 + 3), 128 * (d0 + 3) + 512)
                sch.op("pool", lambda e: e.tensor_tensor(out=sp_t[zi][:], in0=sp_t[zi][:], in1=sbm[:, ms], op=ALU.mult),
                       reads=[b_sp[zi], b_const], writes=[b_sp[zi]])
            sch.group([lambda e: e.matmul(wbz[:], tge[:], sp_t[zi][:], start=True, stop=True)],
                      reads=[b_sp[zi], b_const], writes=[b_wbz])
            if kb > 0:
                sch.group([lambda e: e.matmul(sbz[:], ones_bf[:], sp_t[zi][:], start=True, stop=True)],
                          reads=[b_sp[zi], b_const], writes=[b_sbz])

        def s2(Qt, idx, kb, n, zi):
            oi = Qt % 2
            d0 = 4 * Qt - kb
            wbz, b_wbz, sbz, b_sbz = wbs[zi], b_wbs[zi], sbks[zi], b_sbks[zi]
            if idx == 0:
                sch.op("dve", lambda e: e.tensor_scalar(out=t_t[zi][:], in0=zb[zi][:], scalar1=SCALE, scalar2=None,
                                                        op0=ALU.mult),
                       reads=[b_zb[zi]], writes=[b_t[zi]])
            else:
                sch.op("dve", lambda e: e.scalar_tensor_tensor(out=t_t[zi][:], in0=zb[zi][:], scalar=SCALE, in1=csb[:],
                                                               op0=ALU.mult, op1=ALU.subtract),
                       reads=[b_zb[zi], b_csb], writes=[b_t[zi]])
            sch.op("dve", lambda e: e.scalar_tensor_tensor(out=t_t[zi][:], in0=wbz[:], scalar=-1.0, in1=t_t[zi][:],
                                                           op0=ALU.mult, op1=ALU.add),
                   reads=[b_t[zi], b_wbz], writes=[b_t[zi]])
            if kb > 0:
                if idx == 0:
                    sch.op("dve", lambda e: e.tensor_copy(out=csb[:], in_=sbz[:]), reads=[b_sbz], writes=[b_csb])
                else:
                    sch.op("dve", lambda e: e.tensor_tensor(out=csb[:], in0=sbz[:], in1=csb[:], op=ALU.add),
                           reads=[b_csb, b_sbz], writes=[b_csb])
            sch.op("act", lambda e: e.activation(out=a_t[zi][:], in_=t_t[zi][:], func=AF.Exp),
                   reads=[b_t[zi]], writes=[b_a[zi]])
            if d0 <= 0:
                ms = slice(128 * (d0 + 3), 128 * (d0 + 3) + 512)
                sch.op("pool", lambda e: e.tensor_tensor(out=a_t[zi][:], in0=a_t[zi][:], in1=sbm[:, ms], op=ALU.mult),
                       reads=[b_a[zi], b_const], writes=[b_a[zi]])
            sch.group([lambda e: e.matmul(ob[oi][:], Vt[:, kb, :], a_t[zi][:], start=(idx == 0), stop=(idx == n - 1))],
                      reads=[b_a[zi], b_V[kb // 4]], writes=[b_ob[oi]])
            if idx == n - 1:
                head_norm(h, Qt, ob[oi][:], b_ob[oi], ob[1 - oi], b_ob[1 - oi])

        _pipeline(_steps(), s1, s2)

    dens, b_dens = [sbk, pp[0]], [b_sbk, b_pp[0]]

    def attn_dl(h):
        def s1(Qt, idx, kb, n, zi):
            qs = slice(Qt * 512, (Qt + 1) * 512)
            ks = slice(kb * 128, (kb + 1) * 128)
            sch.group([lambda e: e.matmul(zb[zi][:], KT[:, ks], QT[:, qs], start=True, stop=True)],
                      reads=[b_KT[kb // 4], b_QT[Qt]], writes=[b_zb[zi]])
            sch.op("act", lambda e: e.activation(out=e_t[zi][:], in_=zb[zi][:], func=AF.Exp, scale=SCALE),
                   reads=[b_zb[zi]], writes=[b_e[zi]])

        def s2(Qt, idx, kb, n, zi):
            oi = Qt % 2
            dn, b_dn = dens[oi], b_dens[oi]
            d0 = 4 * Qt - kb
            ms = slice(128 * (d0 + 3), 128 * (d0 + 3) + 512)
            meng = "dve" if (idx % 2 == 0) else "pool"
            sch.op(meng, lambda e: e.tensor_tensor(out=a_t[zi][:], in0=e_t[zi][:], in1=dlm[:, ms], op=ALU.mult),
                   reads=[b_e[zi], b_const], writes=[b_a[zi]])
            sch.group([lambda e: e.matmul(ob[oi][:], Vt[:, kb, :], a_t[zi][:], start=(idx == 0), stop=(idx == n - 1))],
                      reads=[b_a[zi], b_V[kb // 4]], writes=[b_ob[oi]])
            sch.group([lambda e: e.matmul(dn[:], ones_bf[:], a_t[zi][:], start=(idx == 0), stop=(idx == n - 1))],
                      reads=[b_a[zi], b_const], writes=[b_dn])
            if idx == n - 1:
                sch.op("dve", lambda e: e.reciprocal(out=rden[:], in_=dn[:]), reads=[b_dn], writes=[b_rden])
                sch.op("dve", lambda e: e.tensor_tensor(out=ot[:], in0=ob[oi][:], in1=rden[:], op=ALU.mult),
                       reads=[b_ob[oi], b_rden], writes=[b_ot])
                head_norm(h, Qt, ot[:], b_ot)

        _pipeline(_steps(), s1, s2)

    heads = {"pA1": [0, 8], "pAs": [0], "pAd": [8], "pAp": [0]}.get(stop_after, list(range(NH)))
    for h in heads:
        wi = 3 * h
        for j in range(3):
            load_w(wi + j)
        if h < 8:
            proj_fm(wi, evac_plain(QT, b_QT, "act"))
            load_w(wi + 4)
            proj_fm(wi + 1, evac_plain(KT, b_KT, "dve"))
            load_w(wi + 5)
            proj_v(wi + 2)
            load_w(wi + 6)
            if stop_after != "pAp":
                attn_sb(h)
        else:
            proj_fm(wi, evac_rope(QT, b_QT))
            load_w(wi + 4)
            proj_fm(wi + 1, evac_rope(KT, b_KT))
            load_w(wi + 5)
            proj_v(wi + 2)
            load_w(wi + 6)
            attn_dl(h)

    if stop_after in ("pA", "pA1", "pAs", "pAd", "pAp"):
        dump("mixT", mixT[:], [b for row in b_mix for b in row])
        return finish(es, es_h)
    sch.flush()
    es.close()
    es_h.close()

    es = ExitStack()
    x1T = SB(es, "x1T", [128, 16, 512], F32)
    accT = SB(es, "accT", [128, 16, 512], F32)
    h2T = SB(es, "h2T", [128, 16, 512], BF)
    b_x1 = [Buf("x1_%d" % i) for i in range(16)]
    b_acc = [Buf("acc_%d" % i) for i in range(16)]
    b_h2 = [Buf("h2_%d" % i) for i in range(16)]
    NWT = 4
    wbt = [SB(es, "wbt%d" % i, [128, 16, 128], BF) for i in range(NWT)]
    b_wt = [Buf("wt%d" % i) for i in range(NWT)]
    wtslot = [sch.slot() for _ in range(NWT)]
    NWD = 4
    wdb = [SB(es, "wdb%d" % i, [128, D], BF) for i in range(NWD)]
    b_wd = [Buf("wd%d" % i) for i in range(NWD)]
    wdslot = [sch.slot() for _ in range(NWD)]
    yT = [SB(es, "yT%d" % i, [128, 512], BF) for i in range(4)]
    b_y = [Buf("y%d" % i) for i in range(4)]
    g1 = SB(es, "g1", [128, 16], F32)
    g2 = SB(es, "g2", [128, 16], F32)
    g3 = SB(es, "g3", [128, 16], F32)
    cw = SB(es, "cw", [128, 3, 86], F32)
    cb = SB(es, "cb", [128, 86], F32)
    carry = [SB(es, "carry%d" % i, [128, 86, 2], F32) for i in range(2)]
    b_carry = [[Buf("carry%d_%d" % (i, c)) for c in range(86)] for i in range(2)]
    ld_const(g1[:], gpost_d[:], eng="sp")
    ld_const(g2[:], gpre2_d[:], eng="sp")
    ld_const(g3[:], gpost2_d[:], eng="sp")
    ld_const(cw[:], cw_d[:], eng="sp")
    ld_const(cb[:], cb_d[:], eng="sp")
    xs = [SB(es, "xs%d" % i, [128, D], F32) for i in range(1)]
    b_xs = [Buf("xs%d" % i) for i in range(1)]
    xsslot = [sch.slot() for _ in range(1)]
    ta, b_ta = [], []
    for i in range(3):
        ta.append(SB(es, "ta%d" % i, [128, 512], F32))
        b_ta.append(Buf("ta%d" % i))
    gl = SB(es, "gl", [128, 512], F32)
    b_gl = Buf("gl")
    sqb = [SB(es, "sqb%d" % i, [128, 512], BF) for i in range(2)]
    b_sqb = [Buf("sqb%d" % i) for i in range(2)]
    rs = SB(es, "rs", [128, 512], F32)
    b_rs = Buf("rs")
    PSug = [PS(es, "PSug%d" % i, [128, 512], F32) for i in range(2)]
    b_ug = [Buf("PSug%d" % i) for i in range(2)]
    PSuv = [PS(es, "PSuv%d" % i, [128, 512], F32) for i in range(2)]
    b_uv = [Buf("PSuv%d" % i) for i in range(2)]
    PSdp = [PS(es, "PSdp%d" % i, [128, 512], F32) for i in range(2)]
    b_dp = [Buf("PSdp%d" % i) for i in range(2)]
    PSss = PS(es, "PSss", [128, 512], F32)
    b_ss = Buf("PSss")
    PStr = PS(es, "PStr", [128, 512], F32)
    b_tr = Buf("PStr")
    tc = dict(w=0, wd=0, dp=0, ta=0, sq=0, u=0, xs=0)

    def load_wt(src):
        k = tc["w"] % NWT
        tc["w"] += 1
        sch.dma("pool", lambda e, inc, k=k, src=src: inc(e.dma_start(out=wbt[k][:], in_=src)), wtslot[k],
                writes=[b_wt[k]])
        return k

    def load_wd(fc):
        k = tc["wd"] % NWD
        tc["wd"] += 1
        src = w_down_d[fc * 128:(fc + 1) * 128, :]
        sch.dma("pool", lambda e, inc, k=k, src=src: inc(e.dma_start(out=wdb[k][:], in_=src)), wdslot[k],
                writes=[b_wd[k]])
        return k

    def wo_src(dc):
        return w_out_d[:, dc * 128:(dc + 1) * 128].rearrange("(c p) e -> p c e", p=128)

    def wu_src(col0):
        return w_up_d[:, col0:col0 + 128].rearrange("(c p) e -> p c e", p=128)

    def norm_rstd(n):
        sch.op("act", lambda e: e.activation(out=rs[:], in_=PSss[:], func=AF.Ln, scale=1.0 / n, bias=EPS),
               reads=[b_ss], writes=[b_rs])
        sch.op("act", lambda e: e.activation(out=rs[:], in_=rs[:], func=AF.Exp, scale=-0.5), reads=[b_rs],
               writes=[b_rs])

    def sumsq(src_ap, b_src, dc):
        i = tc["sq"] % 2
        tc["sq"] += 1
        sch.op("act", lambda e, i=i: e.activation(out=sqb[i][:], in_=src_ap, func=AF.Square), reads=[b_src],
               writes=[b_sqb[i]])
        sch.group([lambda e, i=i, dc=dc: e.matmul(PSss[:], ones_bf[:], sqb[i][:], start=(dc == 0), stop=(dc == 15))],
                  reads=[b_sqb[i], b_const], writes=[b_ss])

    ntiles = 4 if stop_after not in ("pT1", "pTx") else 1
    for tt in range(ntiles):
        ts = slice(tt * 512, (tt + 1) * 512)
        for j in range(4):
            i = 0
            tc["xs"] += 1
            r0 = tt * 512 + j * 128
            sch.dma("sp", lambda e, inc, i=i, r0=r0: inc(e.dma_start(out=xs[i][:], in_=x_d[r0:r0 + 128, :])), xsslot[i],
                    writes=[b_xs[i]])
            for q4 in range(4):
                fns = []
                for dd in range(4):
                    dc = q4 * 4 + dd
                    fns.append(lambda e, i=i, dc=dc, dd=dd: e.matmul(PStr[:, dd * 128:(dd + 1) * 128],
                                                                    xs[i][:, dc * 128:(dc + 1) * 128], ident_f[:],
                                                                    start=True, stop=True))
                sch.group(fns, reads=[b_xs[i], b_const], writes=[b_tr])
                dst = x1T[:, q4 * 4:q4 * 4 + 4, j * 128:(j + 1) * 128]
                src = PStr[:].rearrange("p (a t) -> p a t", a=4)
                eng = "act" if q4 % 2 == 0 else "dve"
                if eng == "act":
                    sch.op("act", lambda e, dst=dst, src=src: e.copy(out=dst, in_=src), reads=[b_tr],
                           writes=b_x1[q4 * 4:q4 * 4 + 4])
                else:
                    sch.op("dve", lambda e, dst=dst, src=src: e.tensor_copy(out=dst, in_=src), reads=[b_tr],
                           writes=b_x1[q4 * 4:q4 * 4 + 4])
        wks = {}
        for dc in range(2):
            wks[dc] = load_wt(wo_src(dc))
        for dc in range(16):
            if dc + 2 < 16:
                wks[dc + 2] = load_wt(wo_src(dc + 2))
            k = wks[dc]
            pb = tc["dp"] % 2
            tc["dp"] += 1
            fns = []
            for c in range(16):
                fns.append(lambda e, pb=pb, k=k, c=c, ts=ts: e.matmul(PSdp[pb][:], wbt[k][:, c, :], mixT[:, c, ts],
                                                                     start=(c == 0), stop=(c == 15)))
            sch.group(fns, reads=[b_wt[k]] + [b_mix[h][tt] for h in range(16)], writes=[b_dp[pb]])
            sch.op("act", lambda e, pb=pb, dc=dc: e.copy(out=accT[:, dc, :], in_=PSdp[pb][:]), reads=[b_dp[pb]],
                   writes=[b_acc[dc]])
            sumsq(accT[:, dc, :], b_acc[dc], dc)
        norm_rstd(D)
        for dc in range(16):
            i = tc["ta"] % 3
            tc["ta"] += 1
            sch.op("dve", lambda e, i=i, dc=dc: e.scalar_tensor_tensor(out=ta[i][:], in0=accT[:, dc, :],
                                                                      scalar=g1[:, dc:dc + 1], in1=rs[:],
                                                                      op0=ALU.mult, op1=ALU.mult),
                   reads=[b_acc[dc], b_rs, b_const], writes=[b_ta[i]])
            sch.op("pool", lambda e, i=i, dc=dc: e.tensor_tensor(out=x1T[:, dc, :], in0=ta[i][:], in1=x1T[:, dc, :],
                                                                op=ALU.add),
                   reads=[b_ta[i], b_x1[dc]], writes=[b_x1[dc]])
        for dc in range(16):
            sumsq(x1T[:, dc, :], b_x1[dc], dc)
        norm_rstd(D)
        for dc in range(16):
            sch.op("dve", lambda e, dc=dc: e.scalar_tensor_tensor(out=h2T[:, dc, :], in0=x1T[:, dc, :],
                                                                 scalar=g2[:, dc:dc + 1], in1=rs[:],
                                                                 op0=ALU.mult, op1=ALU.mult),
                   reads=[b_x1[dc], b_rs, b_const], writes=[b_h2[dc]])
        if stop_after == "pTx":
            dump("x1T", x1T[:], b_x1)
            dump("h2T", h2T[:], b_h2)
            return finish(es)
        cpar = tt % 2
        groups = [list(range(g0, min(g0 + 4, NFC))) for g0 in range(0, NFC, 4)]
        pre = {}

        def prefetch(i):
            if i < NFC and i not in pre:
                pre[i] = (load_wt(wu_src(i * 128)), load_wt(wu_src(DFF + i * 128)))

        prefetch(0)
        prefetch(1)
        def chunk_pe(i):
            kg, kv = pre[i]
            ui = tc["u"] % 2
            tc["u"] += 1
            for (kk, PSu, b_u) in ((kg, PSug, b_ug), (kv, PSuv, b_uv)):
                fns = []
                for c in range(16):
                    fns.append(lambda e, ui=ui, kk=kk, c=c, PSu=PSu: e.matmul(
                        PSu[ui][:], wbt[kk][:, c, :], h2T[:, c, :], start=(c == 0), stop=(c == 15)))
                sch.group(fns, reads=[b_wt[kk]] + b_h2, writes=[b_u[ui]])
            return ui

        def chunk_rest(i, ui, wdk, ys):
            A = {}
            for (which, PSu, b_u, ch) in (("g", PSug, b_ug, i), ("v", PSuv, b_uv, NFC + i)):
                ai = tc["ta"] % 3
                tc["ta"] += 1
                A[which] = ai
                u = PSu[ui]
                sch.op("act", lambda e, ai=ai, u=u, ch=ch: e.activation(out=ta[ai][:], in_=u[:], func=AF.Identity,
                                                                        scale=cw[:, 2, ch:ch + 1],
                                                                        bias=cb[:, ch:ch + 1]),
                       reads=[b_u[ui], b_const], writes=[b_ta[ai]])
                sch.op("dve", lambda e, ai=ai, u=u, ch=ch: e.scalar_tensor_tensor(
                    out=ta[ai][:, 1:512], in0=u[:, 0:511], scalar=cw[:, 1, ch:ch + 1], in1=ta[ai][:, 1:512],
                    op0=ALU.mult, op1=ALU.add), reads=[b_u[ui], b_ta[ai], b_const], writes=[b_ta[ai]])
                sch.op("dve", lambda e, ai=ai, u=u, ch=ch: e.scalar_tensor_tensor(
                    out=ta[ai][:, 2:512], in0=u[:, 0:510], scalar=cw[:, 0, ch:ch + 1], in1=ta[ai][:, 2:512],
                    op0=ALU.mult, op1=ALU.add), reads=[b_u[ui], b_ta[ai], b_const], writes=[b_ta[ai]])
                if tt > 0:
                    pc_ = carry[1 - cpar]
                    sch.op("dve", lambda e, ai=ai, pc_=pc_, ch=ch: e.scalar_tensor_tensor(
                        out=ta[ai][:, 0:2], in0=pc_[:, ch, 0:2], scalar=cw[:, 0, ch:ch + 1], in1=ta[ai][:, 0:2],
                        op0=ALU.mult, op1=ALU.add), reads=[b_carry[1 - cpar][ch], b_ta[ai], b_const],
                        writes=[b_ta[ai]])
                    sch.op("dve", lambda e, ai=ai, pc_=pc_, ch=ch: e.scalar_tensor_tensor(
                        out=ta[ai][:, 0:1], in0=pc_[:, ch, 1:2], scalar=cw[:, 1, ch:ch + 1], in1=ta[ai][:, 0:1],
                        op0=ALU.mult, op1=ALU.add), reads=[b_carry[1 - cpar][ch], b_ta[ai], b_const],
                        writes=[b_ta[ai]])
                if tt < ntiles - 1:
                    sch.op("dve", lambda e, u=u, ch=ch, cdst=carry[cpar]: e.tensor_copy(out=cdst[:, ch, :],
                                                                                        in_=u[:, 510:512]),
                           reads=[b_u[ui]], writes=[b_carry[cpar][ch]])
            prefetch(i + 2)
            sch.op("act", lambda e, ag=A["g"]: e.activation(out=gl[:], in_=ta[ag][:], func=AF.Gelu_apprx_tanh),
                   reads=[b_ta[A["g"]]], writes=[b_gl])
            yi = i % 4
            ys[i] = yi
            sch.op("pool", lambda e, yi=yi, av=A["v"]: e.tensor_tensor(out=yT[yi][:], in0=gl[:], in1=ta[av][:],
                                                                      op=ALU.mult),
                   reads=[b_gl, b_ta[A["v"]]], writes=[b_y[yi]])
            wdk[i] = load_wd(i)

        def down_proj(gi, grp, wdk, ys):
            for dc in range(16):
                pb = tc["dp"] % 2
                tc["dp"] += 1
                fns = []
                for n_, i in enumerate(grp):
                    fns.append(lambda e, pb=pb, wk=wdk[i], yk=ys[i], dc=dc, n_=n_, L_=len(grp): e.matmul(
                        PSdp[pb][:], wdb[wk][:, dc * 128:(dc + 1) * 128], yT[yk][:], start=(n_ == 0),
                        stop=(n_ == L_ - 1)))
                sch.group(fns, reads=[b_wd[wdk[i]] for i in grp] + [b_y[ys[i]] for i in grp], writes=[b_dp[pb]])
                if gi == 0:
                    sch.op("dve", lambda e, pb=pb, dc=dc: e.tensor_copy(out=accT[:, dc, :], in_=PSdp[pb][:]),
                           reads=[b_dp[pb]], writes=[b_acc[dc]])
                else:
                    sch.op("dve", lambda e, pb=pb, dc=dc: e.tensor_tensor(out=accT[:, dc, :], in0=PSdp[pb][:],
                                                                         in1=accT[:, dc, :], op=ALU.add),
                           reads=[b_dp[pb], b_acc[dc]], writes=[b_acc[dc]])

        pending = None
        for gi, grp in enumerate(groups):
            wdk_g, ys_g = {}, {}
            for n_, i in enumerate(grp):
                ui = chunk_pe(i)
                if n_ == 0 and pending is not None:
                    down_proj(*pending)
                    pending = None
                chunk_rest(i, ui, wdk_g, ys_g)
            pending = (gi, grp, wdk_g, ys_g)
        down_proj(*pending)
        if stop_after == "pT1":
            dump("fT", accT[:], b_acc)
        for dc in range(16):
            sumsq(accT[:, dc, :], b_acc[dc], dc)
        norm_rstd(D)
        for dc in range(16):
            i = tc["ta"] % 3
            tc["ta"] += 1
            sch.op("dve", lambda e, i=i, dc=dc: e.scalar_tensor_tensor(out=ta[i][:], in0=accT[:, dc, :],
                                                                      scalar=g3[:, dc:dc + 1], in1=rs[:],
                                                                      op0=ALU.mult, op1=ALU.mult),
                   reads=[b_acc[dc], b_rs, b_const], writes=[b_ta[i]])
            sch.op("pool", lambda e, i=i, dc=dc: e.tensor_tensor(out=accT[:, dc, :], in0=ta[i][:], in1=x1T[:, dc, :],
                                                                op=ALU.add),
                   reads=[b_ta[i], b_x1[dc]], writes=[b_acc[dc]])
        for j in range(4):
            i = 0
            tc["xs"] += 1
            r0 = tt * 512 + j * 128
            for q4 in range(4):
                fns = []
                for dd in range(4):
                    dc = q4 * 4 + dd
                    fns.append(lambda e, dc=dc, dd=dd, j=j: e.matmul(PStr[:, dd * 128:(dd + 1) * 128],
                                                                    accT[:, dc, j * 128:(j + 1) * 128], ident_f[:],
                                                                    start=True, stop=True))
                sch.group(fns, reads=b_acc[q4 * 4:q4 * 4 + 4] + [b_const], writes=[b_tr])
                dst = xs[i][:, q4 * 512:(q4 + 1) * 512]
                if q4 % 2 == 0:
                    sch.op("act", lambda e, dst=dst: e.copy(out=dst, in_=PStr[:]), reads=[b_tr], writes=[b_xs[i]])
                else:
                    sch.op("dve", lambda e, dst=dst: e.tensor_copy(out=dst, in_=PStr[:]), reads=[b_tr], writes=[b_xs[i]])
            sch.dma("sp", lambda e, inc, i=i, r0=r0: inc(e.dma_start(out=out_d[r0:r0 + 128, :], in_=xs[i][:])), oslot,
                    reads=[b_xs[i]])
    return finish(es)


def make_in_maps(inputs, cores=None):
    f32 = np.float32
    cores = list(range(NCORES)) if cores is None else cores
    c = _consts()
    L = 0

    def pc(v):
        v = np.asarray(v, f32)
        return np.ascontiguousarray(v.reshape(-1, 128).T)

    shared = dict(
        w_in=np.ascontiguousarray(inputs["w_in"][L], f32),
        w_out=np.ascontiguousarray(inputs["w_out"][L], f32),
        w_up=np.ascontiguousarray(inputs["w_up"][L], f32),
        w_down=np.ascontiguousarray(inputs["w_down"][L], f32),
        g_pre_mix=np.ascontiguousarray(np.broadcast_to(np.asarray(inputs["pre_mix_gain"][L], f32)[None, :], (128, D))),
        g_post_mix=pc(inputs["post_mix_gain"][L]),
        g_pre_ffn=pc(inputs["pre_ffn_gain"][L]),
        g_post_ffn=pc(inputs["post_ffn_gain"][L]),
        g_heads=pc(np.concatenate([np.asarray(inputs["sb_out_gain"][L]), np.asarray(inputs["dil_out_gain"][L])])),
        conv_w=np.ascontiguousarray(np.asarray(inputs["conv_w"][L], f32).reshape(3, 86, 128).transpose(2, 0, 1)),
        conv_b=pc(inputs["conv_b"][L]),
    )
    shared.update(c)
    maps = []
    for b in cores:
        m = dict(shared)
        m["x"] = np.ascontiguousarray(inputs["x"][b], f32)
        maps.append(m)
    return maps


def kernel(**inputs):
    nc = build()
    maps = make_in_maps(inputs)
    res = run_bass_kernel_spmd(nc, maps, core_ids=list(range(NCORES)))
    return np.stack([np.asarray(r["out"], np.float32) for r in res.results], axis=0)
```

```python
import numpy as np
import os
LVL = int(os.environ.get('ATT_LVL', '9'))
SUB = int(os.environ.get('ATT_SUB', '9'))
from contextlib import ExitStack
import concourse.bass as bass
import concourse.mybir as mybir
from concourse.bass_utils import run_bass_kernel_spmd

F32 = mybir.dt.float32
BF = mybir.dt.bfloat16
AF = mybir.ActivationFunctionType
ALU = mybir.AluOpType

S = 2048
D = 2048
NH = 16
DH = 128
DFF = 5504
NFC = DFF // 128
QKV = 6144
EPS = 1e-6
SCALE = DH ** -0.5
NCORES = 8
LAST_COUNTS = {}


class Buf:
    __slots__ = ("name", "w", "r")

    def __init__(self, name):
        self.name = name
        self.w = None
        self.r = []


class Slot:
    def __init__(self, sem, key):
        self.sem = sem
        self.key = key
        self.count = 0


class Sched:
    ENGS = ("pe", "act", "dve", "pool", "sp")

    def __init__(self, nc, es):
        self.nc = nc
        self.q = {e: [] for e in self.ENGS}
        self.cnt = {e: 0 for e in self.ENGS}
        self.seen = {e: {} for e in self.ENGS}
        self.sems = {}
        for e in self.ENGS:
            self.sems[e] = es.enter_context(nc.semaphore("sem_" + e))
        self.es = es
        self.nslots = 0

    def slot(self):
        key = "dma%d" % self.nslots
        self.nslots += 1
        sem = self.es.enter_context(self.nc.semaphore("sem_" + key))
        self.sems[key] = sem
        return Slot(sem, key)

    def _deps(self, eng, reads, writes):
        deps = []
        for b in reads:
            if b.w is not None:
                deps.append(b.w)
            if b.name.startswith("zb") or b.name.startswith("ob") or b.name.startswith("wb") or b.name.startswith("sbk") \
                    or b.name.startswith("pp") or b.name.startswith("ptr") or b.name.startswith("PS"):
                deps.extend(t for t in b.r if t[0] != eng)
        for b in writes:
            if b.w is not None:
                deps.append(b.w)
            deps.extend(b.r)
        out = {}
        for (k, v) in deps:
            if k == "pe" and eng == "pe":
                continue
            if v > out.get(k, 0):
                out[k] = v
        res = []
        for k, v in out.items():
            if v > self.seen[eng].get(k, 0):
                self.seen[eng][k] = v
                res.append((k, v))
        return res

    def _emit_waits(self, eng, waits):
        for (k, v) in waits:
            sem = self.sems[k]
            self.q[eng].append(lambda e, sem=sem, v=v: e.wait_ge(sem, v))

    def _mark(self, ticket, reads, writes):
        for b in writes:
            b.w = ticket
            b.r = []
        for b in reads:
            b.r.append(ticket)

    def op(self, eng, fn, reads=(), writes=()):
        waits = self._deps(eng, reads, writes)
        self._emit_waits(eng, waits)
        self.cnt[eng] += 1
        sem = self.sems[eng]
        self.q[eng].append(lambda e, fn=fn, sem=sem: fn(e).then_inc(sem, 1))
        t = (eng, self.cnt[eng])
        self._mark(t, reads, writes)
        return t

    def group(self, fns, reads=(), writes=()):
        eng = "pe"
        waits = self._deps(eng, reads, writes)
        self._emit_waits(eng, waits)
        self.cnt[eng] += 1
        sem = self.sems[eng]
        n = len(fns)
        for i, fn in enumerate(fns):
            if i == n - 1:
                self.q[eng].append(lambda e, fn=fn, sem=sem: fn(e).then_inc(sem, 1))
            else:
                self.q[eng].append(lambda e, fn=fn: fn(e))
        t = (eng, self.cnt[eng])
        self._mark(t, reads, writes)
        return t

    def dma(self, eng, fn, slot, reads=(), writes=(), n=1):
        waits = self._deps(eng, reads, writes)
        self._emit_waits(eng, waits)
        slot.count += 16 * n
        sem = slot.sem
        self.q[eng].append(lambda e, fn=fn, sem=sem: fn(e, lambda ins: ins.then_inc(sem, 16)))
        t = (slot.key, slot.count)
        self._mark(t, reads, writes)
        return t

    def wait_all(self, eng, tickets):
        waits = []
        for (k, v) in tickets:
            if v > self.seen[eng].get(k, 0):
                self.seen[eng][k] = v
                waits.append((k, v))
        self._emit_waits(eng, waits)

    def barrier(self, extra=()):
        tickets = [(e, self.cnt[e]) for e in self.ENGS if self.cnt[e] > 0] + list(extra)
        for eng in self.ENGS:
            self.wait_all(eng, [t for t in tickets if not (t[0] == eng and eng == "pe")])

    def flush(self):
        nc = self.nc
        q = self.q
        with nc.Block() as block:
            @block.tensor
            def _(e):
                for f in q["pe"]:
                    f(e)

            @block.scalar
            def _(e):
                for f in q["act"]:
                    f(e)

            @block.vector
            def _(e):
                for f in q["dve"]:
                    f(e)

            @block.gpsimd
            def _(e):
                for f in q["pool"]:
                    f(e)

            @block.sync
            def _(e):
                for f in q["sp"]:
                    f(e)
        self.q = {e: [] for e in self.ENGS}


def _consts():
    f32 = np.float32
    kl = np.arange(128)[:, None]
    x = np.arange(19 * 128)[None, :]
    dl = x - 384 - kl
    c = ((dl >= 0) & (dl <= 128)).astype(f32)
    c += ((dl >= 0) & (dl % 4 == 0) & (dl <= 512)).astype(f32)
    c += ((dl >= 0) & (dl % 16 == 0) & (dl <= 2048)).astype(f32)
    x7 = np.arange(7 * 128)[None, :]
    sbm = ((x7 - 384 - kl) > 0).astype(f32)
    j = np.arange(128)[:, None]
    s = np.arange(128)[None, :]
    tge = (j >= s).astype(f32)
    ident = np.eye(128, dtype=f32)
    ones = np.ones((128, 128), f32)
    inv_freq = (np.float32(10000.0) ** (-np.arange(0, 128, 2, dtype=f32) / np.float32(128))).astype(f32)
    ang = (np.arange(S, dtype=f32)[:, None] * inv_freq[None, :]).astype(f32)
    cos = np.cos(ang).astype(f32).T
    sin = np.sin(ang).astype(f32).T
    cosT = np.concatenate([cos, cos], axis=0)
    sinS = np.concatenate([-sin, sin], axis=0)
    return dict(c_dlm=c, c_sbm=sbm, c_tge=tge, c_ident=ident, c_ones=ones,
                c_cos=np.ascontiguousarray(cosT), c_sin=np.ascontiguousarray(sinS))


def build(dbg=None, stop_after=None):
    nc = bass.Bass("TRN2", target_bir_lowering=False)
    es0 = ExitStack()

    def din(name, shape, dt=F32):
        return nc.dram_tensor(name, list(shape), dt, kind="ExternalInput").ap()

    x_d = din("x", [S, D])
    w_in_d = din("w_in", [D, QKV])
    w_out_d = din("w_out", [D, D])
    w_up_d = din("w_up", [D, 2 * DFF])
    w_down_d = din("w_down", [DFF, D])
    gpre_d = din("g_pre_mix", [128, D])
    gpost_d = din("g_post_mix", [128, 16])
    gpre2_d = din("g_pre_ffn", [128, 16])
    gpost2_d = din("g_post_ffn", [128, 16])
    og_d = din("g_heads", [128, 16])
    cw_d = din("conv_w", [128, 3, 86])
    cb_d = din("conv_b", [128, 86])
    c_dlm_d = din("c_dlm", [128, 19 * 128])
    c_sbm_d = din("c_sbm", [128, 7 * 128])
    c_tge_d = din("c_tge", [128, 128])
    c_ident_d = din("c_ident", [128, 128])
    c_ones_d = din("c_ones", [128, 128])
    c_cos_d = din("c_cos", [128, S])
    c_sin_d = din("c_sin", [128, S])
    out_d = nc.dram_tensor("out", [S, D], F32, kind="ExternalOutput").ap()
    dbg_d = {}
    if dbg:
        for name, (shape, dt) in dbg.items():
            dbg_d[name] = nc.dram_tensor("dbg_" + name, list(shape), dt, kind="ExternalOutput").ap()

    def SB(es, name, shape, dt):
        return es.enter_context(nc.sbuf_tensor(name, list(shape), dt))

    def PS(es, name, shape, dt):
        return es.enter_context(nc.psum_tensor(name, list(shape), dt))

    sch = Sched(nc, es0)

    mixT = SB(es0, "mixT", [128, 16, S], BF)
    ident_bf = SB(es0, "ident_bf", [128, 128], BF)
    ident_f = SB(es0, "ident_f", [128, 128], F32)
    ones_bf = SB(es0, "ones_bf", [128, 128], BF)
    ones_f = SB(es0, "ones_f", [128, 128], F32)
    b_hT = [Buf("hT%d" % i) for i in range(16)]
    b_mix = [[Buf("mix%d_%d" % (h, q)) for q in range(4)] for h in range(16)]
    b_const = Buf("const")
    cslot = sch.slot()
    es_h = ExitStack()
    hT = SB(es_h, "hT", [128, 16, S], BF)

    cslot_p = sch.slot()
    b_constp = Buf("constp")

    def ld_const(dst, src, eng="pool"):
        if eng == "pool":
            sch.dma(eng, lambda e, inc, dst=dst, src=src: inc(e.dma_start(out=dst, in_=src)), cslot_p,
                    writes=[b_constp])
        else:
            sch.dma(eng, lambda e, inc, dst=dst, src=src: inc(e.dma_start(out=dst, in_=src)), cslot,
                    writes=[b_const])

    def sync_pool_consts():
        for eng in ("pe", "act", "dve", "pool"):
            sch.wait_all(eng, [(cslot_p.key, cslot_p.count)])

    ld_const(ident_bf[:], c_ident_d[:])
    ld_const(ones_bf[:], c_ones_d[:])
    ld_const(ident_f[:], c_ident_d[:], eng="sp")
    ld_const(ones_f[:], c_ones_d[:], eng="sp")
    sync_pool_consts()

    es = ExitStack()
    gB = SB(es, "gB", [128, D], F32)
    ld_const(gB[:], gpre_d[:], eng="sp")
    xt = [SB(es, "xt%d" % i, [128, D], F32) for i in range(2)]
    b_xt = [Buf("xt%d" % i) for i in range(2)]
    xslot = [sch.slot() for _ in range(2)]
    junk = SB(es, "junk", [128, D], BF)
    b_junk = Buf("junk")
    stat = SB(es, "stat", [128, 16, 4], F32)
    b_stat = [Buf("stat%d" % i) for i in range(16)]
    hn = [SB(es, "hn%d" % i, [128, D], BF) for i in range(2)]
    b_hn = [Buf("hn%d" % i) for i in range(2)]
    ptr = [PS(es, "ptr%d" % i, [128, 1024], BF) for i in range(4)]
    b_ptr = [Buf("ptr%d" % i) for i in range(4)]

    for tb in range(16):
        i = tb % 2
        sch.dma("sp", lambda e, inc, i=i, tb=tb: inc(e.dma_start(out=xt[i][:], in_=x_d[tb * 128:(tb + 1) * 128, :])),
                xslot[i], writes=[b_xt[i]])
        sch.op("act", lambda e, i=i, tb=tb: e.activation(out=junk[:], in_=xt[i][:], func=AF.Square,
                                                          accum_out=stat[:, tb, 0:1]),
               reads=[b_xt[i]], writes=[b_junk, b_stat[tb]])
        sch.op("act", lambda e, tb=tb: e.activation(out=stat[:, tb, 1:2], in_=stat[:, tb, 0:1], func=AF.Ln,
                                                    scale=1.0 / D, bias=EPS),
               reads=[b_stat[tb]], writes=[b_stat[tb]])
        sch.op("act", lambda e, tb=tb: e.activation(out=stat[:, tb, 2:3], in_=stat[:, tb, 1:2], func=AF.Exp,
                                                    scale=-0.5),
               reads=[b_stat[tb]], writes=[b_stat[tb]])
        sch.op("dve", lambda e, i=i, tb=tb: e.scalar_tensor_tensor(out=hn[i][:], in0=xt[i][:], scalar=stat[:, tb, 2:3],
                                                                  in1=gB[:], op0=ALU.mult, op1=ALU.mult),
               reads=[b_xt[i], b_stat[tb], b_const], writes=[b_hn[i]])
        for half in range(2):
            pb = (tb * 2 + half) % 4
            fns = []
            for j in range(8):
                c = half * 8 + j
                fns.append(lambda e, pb=pb, j=j, c=c, i=i: e.transpose(ptr[pb][:, j * 128:(j + 1) * 128],
                                                                    hn[i][:, c * 128:(c + 1) * 128], ident_bf[:]))
            sch.group(fns, reads=[b_hn[i], b_const], writes=[b_ptr[pb]])
            dst = hT[:, half * 8:(half + 1) * 8, tb * 128:(tb + 1) * 128]
            src = ptr[pb][:].rearrange("p (c t) -> p c t", c=8)
            if half == 0:
                sch.op("act", lambda e, dst=dst, src=src: e.copy(out=dst, in_=src), reads=[b_ptr[pb]], writes=[b_hT[tb]])
            else:
                sch.op("dve", lambda e, dst=dst, src=src: e.tensor_copy(out=dst, in_=src), reads=[b_ptr[pb]],
                       writes=[b_hT[tb]])
    oslot = sch.slot()

    def finish(*inner):
        global LAST_COUNTS
        LAST_COUNTS = dict(sch.cnt)
        LAST_COUNTS["max_dma_slot"] = max([0] + [v for k, v in sch.seen["sp"].items() if k.startswith("dma")])
        sch.wait_all("sp", [(oslot.key, oslot.count)])
        sch.flush()
        for s_ in inner:
            s_.close()
        es0.close()
        return nc

    def dump(name, src_ap, bufs):
        sch.dma("sp", lambda e, inc: inc(e.dma_start(out=dbg_d[name][:], in_=src_ap)), oslot, reads=bufs)

    if stop_after == "p0":
        dump("hT", hT[:], b_hT)
        return finish(es, es_h)
    sch.flush()
    es.close()

    es = ExitStack()
    NW = 4
    wbuf = [SB(es, "wbuf%d" % i, [128, 16, 128], BF) for i in range(NW)]
    b_w = [Buf("w%d" % i) for i in range(NW)]
    wslot = [sch.slot() for _ in range(NW)]
    QT = SB(es, "QT", [128, S], BF)
    KT = SB(es, "KT", [128, S], BF)
    Vt = SB(es, "Vt", [128, 16, 128], BF)
    b_QT = [Buf("QT%d" % i) for i in range(4)]
    b_KT = [Buf("KT%d" % i) for i in range(4)]
    b_V = [Buf("V%d" % i) for i in range(4)]
    cosT = SB(es, "cosT", [128, S], F32)
    sinS = SB(es, "sinS", [128, S], F32)
    dlm = SB(es, "dlm", [128, 19 * 128], BF)
    sbm = SB(es, "sbm", [128, 7 * 128], BF)
    tge = SB(es, "tge", [128, 128], BF)
    og = SB(es, "og", [128, 16], F32)
    ld_const(cosT[:], c_cos_d[:], eng="sp")
    ld_const(sinS[:], c_sin_d[:], eng="sp")
    ld_const(og[:], og_d[:], eng="sp")
    ld_const(dlm[:], c_dlm_d[:])
    ld_const(sbm[:], c_sbm_d[:])
    ld_const(tge[:], c_tge_d[:])
    sync_pool_consts()

    def tmp(name, dt, n=2):
        ts = [SB(es, "%s%d" % (name, i), [128, 512], dt) for i in range(n)]
        return ts, [Buf("%s%d" % (name, i)) for i in range(n)]

    e_t, b_e = tmp("e_t", F32)
    sp_t, b_sp = tmp("sp_t", BF)
    t_t, b_t = tmp("t_t", F32)
    a_t, b_a = tmp("a_t", BF)
    csb_l, b_csb_l = tmp("csb", F32, 1)
    csb, b_csb = csb_l[0], b_csb_l[0]
    rr1, b_rr1 = e_t, b_e
    rr2, b_rr2 = t_t, b_t
    sq_l, b_sq_l = tmp("sq", F32, 1)
    sq, b_sq = sq_l[0], b_sq_l[0]
    rstd_l, b_rstd_l = tmp("rstd", F32, 1)
    rstd, b_rstd = rstd_l[0], b_rstd_l[0]
    lnv, b_lnv = rstd, b_rstd
    ot_l, b_ot_l = tmp("ot", F32, 1)
    ot, b_ot = ot_l[0], b_ot_l[0]
    rden, b_rden = sq, b_sq
    sqh_l, b_sqh_l = tmp("sqh", BF, 1)
    sqh, b_sqh = sqh_l[0], b_sqh_l[0]
    sql_l, b_sql_l = tmp("sql", BF, 1)
    sql, b_sql = sql_l[0], b_sql_l[0]

    pp = [PS(es, "pp%d" % i, [128, 512], F32) for i in range(2)]
    b_pp = [Buf("pp%d" % i) for i in range(2)]
    zb = [PS(es, "zb%d" % i, [128, 512], F32) for i in range(2)]
    b_zb = [Buf("zb%d" % i) for i in range(2)]
    wb = PS(es, "wb", [128, 512], F32)
    b_wb = Buf("wb")
    sbk = PS(es, "sbk", [128, 512], F32)
    b_sbk = Buf("sbk")
    ob = [PS(es, "ob%d" % i, [128, 512], F32) for i in range(2)]
    b_ob = [Buf("ob%d" % i) for i in range(2)]

    ctr = dict(w=0, p=0, z=0, t=0)

    def head_cols(h):
        if h < 8:
            return h * 128, 1024 + h * 128, 2048 + h * 128
        hh = h - 8
        return 3072 + hh * 128, 4096 + hh * 128, 5120 + hh * 128

    wq_list = []
    for h in range(NH):
        wq_list.extend(head_cols(h))
    loaded = {}

    def load_w(i):
        if i >= len(wq_list) or i in loaded:
            return
        k = i % NW
        col0 = wq_list[i]
        src = w_in_d[:, col0:col0 + 128].rearrange("(c p) e -> p c e", p=128)
        sch.dma("pool", lambda e, inc, k=k, src=src: inc(e.dma_start(out=wbuf[k][:], in_=src)), wslot[k],
                writes=[b_w[k]])
        loaded[i] = k

    for i in range(NW):
        load_w(i)

    def proj_fm(wi, evac):
        k = loaded[wi]
        for tt in range(4):
            pb = ctr["p"] % 2
            ctr["p"] += 1
            fns = []
            for c in range(16):
                fns.append(lambda e, pb=pb, k=k, c=c, tt=tt: e.matmul(
                    pp[pb][:], wbuf[k][:, c, :], hT[:, c, tt * 512:(tt + 1) * 512], start=(c == 0), stop=(c == 15)))
            sch.group(fns, reads=[b_w[k], b_const] + b_hT[4 * tt:4 * tt + 4], writes=[b_pp[pb]])
            evac(tt, pb)

    def evac_plain(dstT, b_dst, eng):
        def f(tt, pb):
            dst = dstT[:, tt * 512:(tt + 1) * 512]
            if eng == "act":
                sch.op("act", lambda e: e.copy(out=dst, in_=pp[pb][:]), reads=[b_pp[pb]], writes=[b_dst[tt]])
            else:
                sch.op("dve", lambda e: e.tensor_copy(out=dst, in_=pp[pb][:]), reads=[b_pp[pb]], writes=[b_dst[tt]])
        return f

    def evac_rope(dstT, b_dst):
        def f(tt, pb):
            i = ctr["t"] % 2
            ctr["t"] += 1
            ts = slice(tt * 512, (tt + 1) * 512)
            dst = dstT[:, ts]
            sch.op("dve", lambda e: e.tensor_tensor(out=rr1[i][:], in0=pp[pb][:], in1=cosT[:, ts], op=ALU.mult),
                   reads=[b_pp[pb], b_const], writes=[b_rr1[i]])
            sch.op("dve", lambda e: e.tensor_tensor(out=rr2[i][0:64, :], in0=pp[pb][64:128, :], in1=sinS[0:64, ts],
                                                    op=ALU.mult),
                   reads=[b_pp[pb], b_const], writes=[b_rr2[i]])
            sch.op("dve", lambda e: e.tensor_tensor(out=rr2[i][64:128, :], in0=pp[pb][0:64, :], in1=sinS[64:128, ts],
                                                    op=ALU.mult),
                   reads=[b_pp[pb], b_const], writes=[b_rr2[i]])
            sch.op("pool", lambda e: e.tensor_tensor(out=dst, in0=rr1[i][:], in1=rr2[i][:], op=ALU.add),
                   reads=[b_rr1[i], b_rr2[i]], writes=[b_dst[tt]])
        return f

    def proj_v(wi):
        k = loaded[wi]
        for g in range(4):
            pb = ctr["p"] % 2
            ctr["p"] += 1
            fns = []
            for tb in range(4 * g, 4 * g + 4):
                for c in range(16):
                    fns.append(lambda e, pb=pb, k=k, c=c, tb=tb: e.matmul(
                        pp[pb][:, (tb % 4) * 128:(tb % 4 + 1) * 128], hT[:, c, tb * 128:(tb + 1) * 128],
                        wbuf[k][:, c, :], start=(c == 0), stop=(c == 15)))
            sch.group(fns, reads=[b_w[k], b_const] + b_hT[4 * g:4 * g + 4], writes=[b_pp[pb]])
            dst = Vt[:, 4 * g:4 * g + 4, :]
            src = pp[pb][:].rearrange("p (a d) -> p a d", a=4)
            sch.op("act", lambda e, dst=dst, src=src: e.copy(out=dst, in_=src), reads=[b_pp[pb]], writes=[b_V[g]])

    def head_norm(h, Qt, src_ap, b_src, ssq=None, b_ssq=None):
        ssq = wb if ssq is None else ssq
        b_ssq = b_wb if b_ssq is None else b_ssq
        qs = slice(Qt * 512, (Qt + 1) * 512)
        sch.op("act", lambda e: e.activation(out=sq[:], in_=src_ap, func=AF.Square), reads=[b_src], writes=[b_sq])
        sch.op("dve", lambda e: e.tensor_copy(out=sqh[:], in_=sq[:]), reads=[b_sq], writes=[b_sqh])
        sch.op("dve", lambda e: e.tensor_tensor(out=sql[:], in0=sq[:], in1=sqh[:], op=ALU.subtract),
               reads=[b_sq, b_sqh], writes=[b_sql])
        sch.group([lambda e: e.matmul(ssq[:], ones_bf[:], sqh[:], start=True, stop=False),
                   lambda e: e.matmul(ssq[:], ones_bf[:], sql[:], start=False, stop=True)],
                  reads=[b_sqh, b_sql, b_const], writes=[b_ssq])
        sch.op("act", lambda e: e.activation(out=lnv[:], in_=ssq[:], func=AF.Ln, scale=1.0 / DH, bias=EPS),
               reads=[b_ssq], writes=[b_lnv])
        sch.op("act", lambda e: e.activation(out=rstd[:], in_=lnv[:], func=AF.Exp, scale=-0.5),
               reads=[b_lnv], writes=[b_rstd])
        sch.op("dve", lambda e: e.scalar_tensor_tensor(out=mixT[:, h, qs], in0=src_ap, scalar=og[:, h:h + 1],
                                                       in1=rstd[:], op0=ALU.mult, op1=ALU.mult),
               reads=[b_src, b_rstd, b_const], writes=[b_mix[h][Qt]])

    wbs, b_wbs = [wb, pp[0]], [b_wb, b_pp[0]]
    sbks, b_sbks = [sbk, pp[1]], [b_sbk, b_pp[1]]

    def _steps():
        steps = []
        for Qt in range(4):
            kbs = list(range(4 * Qt + 3, -1, -1))
            for idx, kb in enumerate(kbs):
                zi = ctr["z"] % 2
                ctr["z"] += 1
                steps.append((Qt, idx, kb, len(kbs), zi))
        return steps

    def _pipeline(steps, s1, s2):
        s1(*steps[0])
        for i in range(len(steps)):
            if i + 1 < len(steps):
                s1(*steps[i + 1])
            s2(*steps[i])

    def attn_sb(h):
        def s1(Qt, idx, kb, n, zi):
            qs = slice(Qt * 512, (Qt + 1) * 512)
            ks = slice(kb * 128, (kb + 1) * 128)
            d0 = 4 * Qt - kb
            wbz, b_wbz, sbz, b_sbz = wbs[zi], b_wbs[zi], sbks[zi], b_sbks[zi]
            sch.group([lambda e: e.matmul(zb[zi][:], KT[:, ks], QT[:, qs], start=True, stop=True)],
                      reads=[b_KT[kb // 4], b_QT[Qt]], writes=[b_zb[zi]])
            sch.op("act", lambda e: e.activation(out=e_t[zi][:], in_=zb[zi][:], func=AF.Exp, scale=SCALE),
                   reads=[b_zb[zi]], writes=[b_e[zi]])
            sch.op("act", lambda e: e.activation(out=sp_t[zi][:], in_=e_t[zi][:], func=AF.Ln, bias=1.0),
                   reads=[b_e[zi]], writes=[b_sp[zi]])
            if d0 <= 0:
                ms = slice(128 * (d0 + 3), 128 * (d0 + 3) + 512)
                sch.op("pool", lambda e: e.tensor_tensor(out=sp_t[zi][:], in0=sp_t[zi][:], in1=sbm[:, ms], op=ALU.mult),
                       reads=[b_sp[zi], b_const], writes=[b_sp[zi]])
            sch.group([lambda e: e.matmul(wbz[:], tge[:], sp_t[zi][:], start=True, stop=True)],
                      reads=[b_sp[zi], b_const], writes=[b_wbz])
            if kb > 0:
                sch.group([lambda e: e.matmul(sbz[:], ones_bf[:], sp_t[zi][:], start=True, stop=True)],
                          reads=[b_sp[zi], b_const], writes=[b_sbz])

        def s2(Qt, idx, kb, n, zi):
            oi = Qt % 2
            d0 = 4 * Qt - kb
            wbz, b_wbz, sbz, b_sbz = wbs[zi], b_wbs[zi], sbks[zi], b_sbks[zi]
            if idx == 0:
                sch.op("dve", lambda e: e.tensor_scalar(out=t_t[zi][:], in0=zb[zi][:], scalar1=SCALE, scalar2=None,
                                                        op0=ALU.mult),
                       reads=[b_zb[zi]], writes=[b_t[zi]])
            else:
                sch.op("dve", lambda e: e.scalar_tensor_tensor(out=t_t[zi][:], in0=zb[zi][:], scalar=SCALE, in1=csb[:],
                                                               op0=ALU.mult, op1=ALU.subtract),
                       reads=[b_zb[zi], b_csb], writes=[b_t[zi]])
            sch.op("dve", lambda e: e.scalar_tensor_tensor(out=t_t[zi][:], in0=wbz[:], scalar=-1.0, in1=t_t[zi][:],
                                                           op0=ALU.mult, op1=ALU.add),
                   reads=[b_t[zi], b_wbz], writes=[b_t[zi]])
            if kb > 0:
                if idx == 0:
                    sch.op("dve", lambda e: e.tensor_copy(out=csb[:], in_=sbz[:]), reads=[b_sbz], writes=[b_csb])
                else:
                    sch.op("dve", lambda e: e.tensor_tensor(out=csb[:], in0=sbz[:], in1=csb[:], op=ALU.add),
                           reads=[b_csb, b_sbz], writes=[b_csb])
            sch.op("act", lambda e: e.activation(out=a_t[zi][:], in_=t_t[zi][:], func=AF.Exp),
                   reads=[b_t[zi]], writes=[b_a[zi]])
            if d0 <= 0:
                ms = slice(128 * (d0 + 3), 128 * (d0 + 3) + 512)
                sch.op("pool", lambda e: e.tensor_tensor(out=a_t[zi][:], in0=a_t[zi][:], in1=sbm[:, ms], op=ALU.mult),
                       reads=[b_a[zi], b_const], writes=[b_a[zi]])
            sch.group([lambda e: e.matmul(ob[oi][:], Vt[:, kb, :], a_t[zi][:], start=(idx == 0), stop=(idx == n - 1))],
                      reads=[b_a[zi], b_V[kb // 4]], writes=[b_ob[oi]])
            if idx == n - 1:
                head_norm(h, Qt, ob[oi][:], b_ob[oi], ob[1 - oi], b_ob[1 - oi])

        _pipeline(_steps(), s1, s2)

    dens, b_dens = [sbk, pp[0]], [b_sbk, b_pp[0]]

    def attn_dl(h):
        def s1(Qt, idx, kb, n, zi):
            qs = slice(Qt * 512, (Qt + 1) * 512)
            ks = slice(kb * 128, (kb + 1) * 128)
            sch.group([lambda e: e.matmul(zb[zi][:], KT[:, ks], QT[:, qs], start=True, stop=True)],
                      reads=[b_KT[kb // 4], b_QT[Qt]], writes=[b_zb[zi]])
            sch.op("act", lambda e: e.activation(out=e_t[zi][:], in_=zb[zi][:], func=AF.Exp, scale=SCALE),
                   reads=[b_zb[zi]], writes=[b_e[zi]])

        def s2(Qt, idx, kb, n, zi):
            oi = Qt % 2
            dn, b_dn = dens[oi], b_dens[oi]
            d0 = 4 * Qt - kb
            ms = slice(128 * (d0 + 3), 128 * (d0 + 3) + 512)
            meng = "dve" if (idx % 2 == 0) else "pool"
            sch.op(meng, lambda e: e.tensor_tensor(out=a_t[zi][:], in0=e_t[zi][:], in1=dlm[:, ms], op=ALU.mult),
                   reads=[b_e[zi], b_const], writes=[b_a[zi]])
            sch.group([lambda e: e.matmul(ob[oi][:], Vt[:, kb, :], a_t[zi][:], start=(idx == 0), stop=(idx == n - 1))],
                      reads=[b_a[zi], b_V[kb // 4]], writes=[b_ob[oi]])
            sch.group([lambda e: e.matmul(dn[:], ones_bf[:], a_t[zi][:], start=(idx == 0), stop=(idx == n - 1))],
                      reads=[b_a[zi], b_const], writes=[b_dn])
            if idx == n - 1:
                sch.op("dve", lambda e: e.reciprocal(out=rden[:], in_=dn[:]), reads=[b_dn], writes=[b_rden])
                sch.op("dve", lambda e: e.tensor_tensor(out=ot[:], in0=ob[oi][:], in1=rden[:], op=ALU.mult),
                       reads=[b_ob[oi], b_rden], writes=[b_ot])
                head_norm(h, Qt, ot[:], b_ot)

        _pipeline(_steps(), s1, s2)

    heads = {"pA1": [0, 8], "pAs": [0], "pAd": [8], "pAp": [0]}.get(stop_after, list(range(NH)))
    for h in heads:
        wi = 3 * h
        for j in range(3):
            load_w(wi + j)
        if h < 8:
            proj_fm(wi, evac_plain(QT, b_QT, "act"))
            load_w(wi + 4)
            proj_fm(wi + 1, evac_plain(KT, b_KT, "dve"))
            load_w(wi + 5)
            proj_v(wi + 2)
            load_w(wi + 6)
            if stop_after != "pAp":
                attn_sb(h)
        else:
            proj_fm(wi, evac_rope(QT, b_QT))
            load_w(wi + 4)
            proj_fm(wi + 1, evac_rope(KT, b_KT))
            load_w(wi + 5)
            proj_v(wi + 2)
            load_w(wi + 6)
            attn_dl(h)

    if stop_after in ("pA", "pA1", "pAs", "pAd", "pAp"):
        dump("mixT", mixT[:], [b for row in b_mix for b in row])
        return finish(es, es_h)
    sch.flush()
    es.close()
    es_h.close()

    es = ExitStack()
    x1T = SB(es, "x1T", [128, 16, 512], F32)
    accT = SB(es, "accT", [128, 16, 512], F32)
    h2T = SB(es, "h2T", [128, 16, 512], BF)
    b_x1 = [Buf("x1_%d" % i) for i in range(16)]
    b_acc = [Buf("acc_%d" % i) for i in range(16)]
    b_h2 = [Buf("h2_%d" % i) for i in range(16)]
    NWT = 4
    wbt = [SB(es, "wbt%d" % i, [128, 16, 128], BF) for i in range(NWT)]
    b_wt = [Buf("wt%d" % i) for i in range(NWT)]
    wtslot = [sch.slot() for _ in range(NWT)]
    NWD = 4
    wdb = [SB(es, "wdb%d" % i, [128, D], BF) for i in range(NWD)]
    b_wd = [Buf("wd%d" % i) for i in range(NWD)]
    wdslot = [sch.slot() for _ in range(NWD)]
    yT = [SB(es, "yT%d" % i, [128, 512], BF) for i in range(4)]
    b_y = [Buf("y%d" % i) for i in range(4)]
    g1 = SB(es, "g1", [128, 16], F32)
    g2 = SB(es, "g2", [128, 16], F32)
    g3 = SB(es, "g3", [128, 16], F32)
    cw = SB(es, "cw", [128, 3, 86], F32)
    cb = SB(es, "cb", [128, 86], F32)
    carry = [SB(es, "carry%d" % i, [128, 86, 2], F32) for i in range(2)]
    b_carry = [[Buf("carry%d_%d" % (i, c)) for c in range(86)] for i in range(2)]
    ld_const(g1[:], gpost_d[:], eng="sp")
    ld_const(g2[:], gpre2_d[:], eng="sp")
    ld_const(g3[:], gpost2_d[:], eng="sp")
    ld_const(cw[:], cw_d[:], eng="sp")
    ld_const(cb[:], cb_d[:], eng="sp")
    xs = [SB(es, "xs%d" % i, [128, D], F32) for i in range(1)]
    b_xs = [Buf("xs%d" % i) for i in range(1)]
    xsslot = [sch.slot() for _ in range(1)]
    ta, b_ta = [], []
    for i in range(3):
        ta.append(SB(es, "ta%d" % i, [128, 512], F32))
        b_ta.append(Buf("ta%d" % i))
    gl = SB(es, "gl", [128, 512], F32)
    b_gl = Buf("gl")
    sqb = [SB(es, "sqb%d" % i, [128, 512], BF) for i in range(2)]
    b_sqb = [Buf("sqb%d" % i) for i in range(2)]
    rs = SB(es, "rs", [128, 512], F32)
    b_rs = Buf("rs")
    PSug = [PS(es, "PSug%d" % i, [128, 512], F32) for i in range(2)]
    b_ug = [Buf("PSug%d" % i) for i in range(2)]
    PSuv = [PS(es, "PSuv%d" % i, [128, 512], F32) for i in range(2)]
    b_uv = [Buf("PSuv%d" % i) for i in range(2)]
    PSdp = [PS(es, "PSdp%d" % i, [128, 512], F32) for i in range(2)]
    b_dp = [Buf("PSdp%d" % i) for i in range(2)]
    PSss = PS(es, "PSss", [128, 512], F32)
    b_ss = Buf("PSss")
    PStr = PS(es, "PStr", [128, 512], F32)
    b_tr = Buf("PStr")
    tc = dict(w=0, wd=0, dp=0, ta=0, sq=0, u=0, xs=0)

    def load_wt(src):
        k = tc["w"] % NWT
        tc["w"] += 1
        sch.dma("pool", lambda e, inc, k=k, src=src: inc(e.dma_start(out=wbt[k][:], in_=src)), wtslot[k],
                writes=[b_wt[k]])
        return k

    def load_wd(fc):
        k = tc["wd"] % NWD
        tc["wd"] += 1
        src = w_down_d[fc * 128:(fc + 1) * 128, :]
        sch.dma("pool", lambda e, inc, k=k, src=src: inc(e.dma_start(out=wdb[k][:], in_=src)), wdslot[k],
                writes=[b_wd[k]])
        return k

    def wo_src(dc):
        return w_out_d[:, dc * 128:(dc + 1) * 128].rearrange("(c p) e -> p c e", p=128)

    def wu_src(col0):
        return w_up_d[:, col0:col0 + 128].rearrange("(c p) e -> p c e", p=128)

    def norm_rstd(n):
        sch.op("act", lambda e: e.activation(out=rs[:], in_=PSss[:], func=AF.Ln, scale=1.0 / n, bias=EPS),
               reads=[b_ss], writes=[b_rs])
        sch.op("act", lambda e: e.activation(out=rs[:], in_=rs[:], func=AF.Exp, scale=-0.5), reads=[b_rs],
               writes=[b_rs])

    def sumsq(src_ap, b_src, dc, defer=False):
        i = tc["sq"] % 2
        tc["sq"] += 1
        sch.op("act", lambda e, i=i: e.activation(out=sqb[i][:], in_=src_ap, func=AF.Square), reads=[b_src],
               writes=[b_sqb[i]])

        def pe_part():
            sch.group([lambda e, i=i, dc=dc: e.matmul(PSss[:], ones_bf[:], sqb[i][:], start=(dc == 0),
                                                       stop=(dc == 15))],
                      reads=[b_sqb[i], b_const], writes=[b_ss])
        if defer:
            return pe_part
        pe_part()

    ntiles = 4 if stop_after not in ("pT1", "pTx") else 1
    for tt in range(ntiles):
        ts = slice(tt * 512, (tt + 1) * 512)
        for j in range(4):
            i = 0
            tc["xs"] += 1
            r0 = tt * 512 + j * 128
            sch.dma("sp", lambda e, inc, i=i, r0=r0: inc(e.dma_start(out=xs[i][:], in_=x_d[r0:r0 + 128, :])), xsslot[i],
                    writes=[b_xs[i]])
            for q4 in range(4):
                fns = []
                for dd in range(4):
                    dc = q4 * 4 + dd
                    fns.append(lambda e, i=i, dc=dc, dd=dd: e.matmul(PStr[:, dd * 128:(dd + 1) * 128],
                                                                    xs[i][:, dc * 128:(dc + 1) * 128], ident_f[:],
                                                                    start=True, stop=True))
                sch.group(fns, reads=[b_xs[i], b_const], writes=[b_tr])
                dst = x1T[:, q4 * 4:q4 * 4 + 4, j * 128:(j + 1) * 128]
                src = PStr[:].rearrange("p (a t) -> p a t", a=4)
                eng = "act" if q4 % 2 == 0 else "dve"
                if eng == "act":
                    sch.op("act", lambda e, dst=dst, src=src: e.copy(out=dst, in_=src), reads=[b_tr],
                           writes=b_x1[q4 * 4:q4 * 4 + 4])
                else:
                    sch.op("dve", lambda e, dst=dst, src=src: e.tensor_copy(out=dst, in_=src), reads=[b_tr],
                           writes=b_x1[q4 * 4:q4 * 4 + 4])
        wks = {}
        for dc in range(2):
            wks[dc] = load_wt(wo_src(dc))
        pend_ssq = None
        for dc in range(16):
            if dc + 2 < 16:
                wks[dc + 2] = load_wt(wo_src(dc + 2))
            k = wks[dc]
            pb = tc["dp"] % 2
            tc["dp"] += 1
            fns = []
            for c in range(16):
                fns.append(lambda e, pb=pb, k=k, c=c, ts=ts: e.matmul(PSdp[pb][:], wbt[k][:, c, :], mixT[:, c, ts],
                                                                     start=(c == 0), stop=(c == 15)))
            sch.group(fns, reads=[b_wt[k]] + [b_mix[h][tt] for h in range(16)], writes=[b_dp[pb]])
            if pend_ssq is not None:
                pend_ssq()
            sch.op("act", lambda e, pb=pb, dc=dc: e.copy(out=accT[:, dc, :], in_=PSdp[pb][:]), reads=[b_dp[pb]],
                   writes=[b_acc[dc]])
            pend_ssq = sumsq(accT[:, dc, :], b_acc[dc], dc, defer=True)
        pend_ssq()
        norm_rstd(D)
        for dc in range(16):
            i = tc["ta"] % 3
            tc["ta"] += 1
            sch.op("dve", lambda e, i=i, dc=dc: e.scalar_tensor_tensor(out=ta[i][:], in0=accT[:, dc, :],
                                                                      scalar=g1[:, dc:dc + 1], in1=rs[:],
                                                                      op0=ALU.mult, op1=ALU.mult),
                   reads=[b_acc[dc], b_rs, b_const], writes=[b_ta[i]])
            sch.op("pool", lambda e, i=i, dc=dc: e.tensor_tensor(out=x1T[:, dc, :], in0=ta[i][:], in1=x1T[:, dc, :],
                                                                op=ALU.add),
                   reads=[b_ta[i], b_x1[dc]], writes=[b_x1[dc]])
        for dc in range(16):
            sumsq(x1T[:, dc, :], b_x1[dc], dc)
        norm_rstd(D)
        for dc in range(16):
            sch.op("dve", lambda e, dc=dc: e.scalar_tensor_tensor(out=h2T[:, dc, :], in0=x1T[:, dc, :],
                                                                 scalar=g2[:, dc:dc + 1], in1=rs[:],
                                                                 op0=ALU.mult, op1=ALU.mult),
                   reads=[b_x1[dc], b_rs, b_const], writes=[b_h2[dc]])
        if stop_after == "pTx":
            dump("x1T", x1T[:], b_x1)
            dump("h2T", h2T[:], b_h2)
            return finish(es)
        cpar = tt % 2
        groups = [list(range(g0, min(g0 + 4, NFC))) for g0 in range(0, NFC, 4)]
        pre = {}

        def prefetch(i):
            if i < NFC and i not in pre:
                pre[i] = (load_wt(wu_src(i * 128)), load_wt(wu_src(DFF + i * 128)))

        prefetch(0)
        prefetch(1)
        def chunk_pe(i):
            kg, kv = pre[i]
            ui = tc["u"] % 2
            tc["u"] += 1
            for (kk, PSu, b_u) in ((kg, PSug, b_ug), (kv, PSuv, b_uv)):
                fns = []
                for c in range(16):
                    fns.append(lambda e, ui=ui, kk=kk, c=c, PSu=PSu: e.matmul(
                        PSu[ui][:], wbt[kk][:, c, :], h2T[:, c, :], start=(c == 0), stop=(c == 15)))
                sch.group(fns, reads=[b_wt[kk]] + b_h2, writes=[b_u[ui]])
            return ui

        def chunk_rest(i, ui, wdk, ys):
            A = {}
            for (which, PSu, b_u, ch) in (("g", PSug, b_ug, i), ("v", PSuv, b_uv, NFC + i)):
                ai = tc["ta"] % 3
                tc["ta"] += 1
                A[which] = ai
                u = PSu[ui]
                sch.op("act", lambda e, ai=ai, u=u, ch=ch: e.activation(out=ta[ai][:], in_=u[:], func=AF.Identity,
                                                                        scale=cw[:, 2, ch:ch + 1],
                                                                        bias=cb[:, ch:ch + 1]),
                       reads=[b_u[ui], b_const], writes=[b_ta[ai]])
                sch.op("dve", lambda e, ai=ai, u=u, ch=ch: e.scalar_tensor_tensor(
                    out=ta[ai][:, 1:512], in0=u[:, 0:511], scalar=cw[:, 1, ch:ch + 1], in1=ta[ai][:, 1:512],
                    op0=ALU.mult, op1=ALU.add), reads=[b_u[ui], b_ta[ai], b_const], writes=[b_ta[ai]])
                sch.op("dve", lambda e, ai=ai, u=u, ch=ch: e.scalar_tensor_tensor(
                    out=ta[ai][:, 2:512], in0=u[:, 0:510], scalar=cw[:, 0, ch:ch + 1], in1=ta[ai][:, 2:512],
                    op0=ALU.mult, op1=ALU.add), reads=[b_u[ui], b_ta[ai], b_const], writes=[b_ta[ai]])
                if tt > 0:
                    pc_ = carry[1 - cpar]
                    sch.op("dve", lambda e, ai=ai, pc_=pc_, ch=ch: e.scalar_tensor_tensor(
                        out=ta[ai][:, 0:2], in0=pc_[:, ch, 0:2], scalar=cw[:, 0, ch:ch + 1], in1=ta[ai][:, 0:2],
                        op0=ALU.mult, op1=ALU.add), reads=[b_carry[1 - cpar][ch], b_ta[ai], b_const],
                        writes=[b_ta[ai]])
                    sch.op("dve", lambda e, ai=ai, pc_=pc_, ch=ch: e.scalar_tensor_tensor(
                        out=ta[ai][:, 0:1], in0=pc_[:, ch, 1:2], scalar=cw[:, 1, ch:ch + 1], in1=ta[ai][:, 0:1],
                        op0=ALU.mult, op1=ALU.add), reads=[b_carry[1 - cpar][ch], b_ta[ai], b_const],
                        writes=[b_ta[ai]])
                if tt < ntiles - 1:
                    sch.op("dve", lambda e, u=u, ch=ch, cdst=carry[cpar]: e.tensor_copy(out=cdst[:, ch, :],
                                                                                        in_=u[:, 510:512]),
                           reads=[b_u[ui]], writes=[b_carry[cpar][ch]])
            prefetch(i + 2)
            sch.op("act", lambda e, ag=A["g"]: e.activation(out=gl[:], in_=ta[ag][:], func=AF.Gelu_apprx_tanh),
                   reads=[b_ta[A["g"]]], writes=[b_gl])
            yi = i % 4
            ys[i] = yi
            sch.op("pool", lambda e, yi=yi, av=A["v"]: e.tensor_tensor(out=yT[yi][:], in0=gl[:], in1=ta[av][:],
                                                                      op=ALU.mult),
                   reads=[b_gl, b_ta[A["v"]]], writes=[b_y[yi]])
            wdk[i] = load_wd(i)

        def down_proj(gi, grp, wdk, ys):
            for dc in range(16):
                pb = tc["dp"] % 2
                tc["dp"] += 1
                fns = []
                for n_, i in enumerate(grp):
                    fns.append(lambda e, pb=pb, wk=wdk[i], yk=ys[i], dc=dc, n_=n_, L_=len(grp): e.matmul(
                        PSdp[pb][:], wdb[wk][:, dc * 128:(dc + 1) * 128], yT[yk][:], start=(n_ == 0),
                        stop=(n_ == L_ - 1)))
                sch.group(fns, reads=[b_wd[wdk[i]] for i in grp] + [b_y[ys[i]] for i in grp], writes=[b_dp[pb]])
                if gi == 0:
                    sch.op("dve", lambda e, pb=pb, dc=dc: e.tensor_copy(out=accT[:, dc, :], in_=PSdp[pb][:]),
                           reads=[b_dp[pb]], writes=[b_acc[dc]])
                else:
                    sch.op("dve", lambda e, pb=pb, dc=dc: e.tensor_tensor(out=accT[:, dc, :], in0=PSdp[pb][:],
                                                                         in1=accT[:, dc, :], op=ALU.add),
                           reads=[b_dp[pb], b_acc[dc]], writes=[b_acc[dc]])

        pending = None
        for gi, grp in enumerate(groups):
            wdk_g, ys_g = {}, {}
            for n_, i in enumerate(grp):
                ui = chunk_pe(i)
                if n_ == 0 and pending is not None:
                    down_proj(*pending)
                    pending = None
                chunk_rest(i, ui, wdk_g, ys_g)
            pending = (gi, grp, wdk_g, ys_g)
        down_proj(*pending)
        if stop_after == "pT1":
            dump("fT", accT[:], b_acc)
        for dc in range(16):
            sumsq(accT[:, dc, :], b_acc[dc], dc)
        norm_rstd(D)
        for dc in range(16):
            i = tc["ta"] % 3
            tc["ta"] += 1
            sch.op("dve", lambda e, i=i, dc=dc: e.scalar_tensor_tensor(out=ta[i][:], in0=accT[:, dc, :],
                                                                      scalar=g3[:, dc:dc + 1], in1=rs[:],
                                                                      op0=ALU.mult, op1=ALU.mult),
                   reads=[b_acc[dc], b_rs, b_const], writes=[b_ta[i]])
            sch.op("pool", lambda e, i=i, dc=dc: e.tensor_tensor(out=accT[:, dc, :], in0=ta[i][:], in1=x1T[:, dc, :],
                                                                op=ALU.add),
                   reads=[b_ta[i], b_x1[dc]], writes=[b_acc[dc]])
        for j in range(4):
            i = 0
            tc["xs"] += 1
            r0 = tt * 512 + j * 128
            for q4 in range(4):
                fns = []
                for dd in range(4):
                    dc = q4 * 4 + dd
                    fns.append(lambda e, dc=dc, dd=dd, j=j: e.matmul(PStr[:, dd * 128:(dd + 1) * 128],
                                                                    accT[:, dc, j * 128:(j + 1) * 128], ident_f[:],
                                                                    start=True, stop=True))
                sch.group(fns, reads=b_acc[q4 * 4:q4 * 4 + 4] + [b_const], writes=[b_tr])
                dst = xs[i][:, q4 * 512:(q4 + 1) * 512]
                if q4 % 2 == 0:
                    sch.op("act", lambda e, dst=dst: e.copy(out=dst, in_=PStr[:]), reads=[b_tr], writes=[b_xs[i]])
                else:
                    sch.op("dve", lambda e, dst=dst: e.tensor_copy(out=dst, in_=PStr[:]), reads=[b_tr], writes=[b_xs[i]])
            sch.dma("sp", lambda e, inc, i=i, r0=r0: inc(e.dma_start(out=out_d[r0:r0 + 128, :], in_=xs[i][:])), oslot,
                    reads=[b_xs[i]])
    return finish(es)


def make_in_maps(inputs, cores=None):
    f32 = np.float32
    cores = list(range(NCORES)) if cores is None else cores
    c = _consts()
    L = 0

    def pc(v):
        v = np.asarray(v, f32)
        return np.ascontiguousarray(v.reshape(-1, 128).T)

    shared = dict(
        w_in=np.ascontiguousarray(inputs["w_in"][L], f32),
        w_out=np.ascontiguousarray(inputs["w_out"][L], f32),
        w_up=np.ascontiguousarray(inputs["w_up"][L], f32),
        w_down=np.ascontiguousarray(inputs["w_down"][L], f32),
        g_pre_mix=np.ascontiguousarray(np.broadcast_to(np.asarray(inputs["pre_mix_gain"][L], f32)[None, :], (128, D))),
        g_post_mix=pc(inputs["post_mix_gain"][L]),
        g_pre_ffn=pc(inputs["pre_ffn_gain"][L]),
        g_post_ffn=pc(inputs["post_ffn_gain"][L]),
        g_heads=pc(np.concatenate([np.asarray(inputs["sb_out_gain"][L]), np.asarray(inputs["dil_out_gain"][L])])),
        conv_w=np.ascontiguousarray(np.asarray(inputs["conv_w"][L], f32).reshape(3, 86, 128).transpose(2, 0, 1)),
        conv_b=pc(inputs["conv_b"][L]),
    )
    shared.update(c)
    maps = []
    for b in cores:
        m = dict(shared)
        m["x"] = np.ascontiguousarray(inputs["x"][b], f32)
        maps.append(m)
    return maps


def kernel(**inputs):
    nc = build()
    maps = make_in_maps(inputs)
    res = run_bass_kernel_spmd(nc, maps, core_ids=list(range(NCORES)))
    return np.stack([np.asarray(r["out"], np.float32) for r in res.results], axis=0)
```
